# Optimizing a Trainium2 kernel written in Bass

```python
import jax, jax.numpy as jnp
from jax import lax
import numpy as np

D_MODEL = 1024
BATCH = 32
SEQ = 256
DEPTH = 2
DEC_BATCH = 4
DEC_SEQ = 4096
PAST_LEN = 512

GRID_W = 64
D_FF = 4 * D_MODEL
EPS = 1e-6
D_RNN = D_MODEL // 2
RNN_BLOCKS = 8
RNN_BLOCK = D_RNN // RNN_BLOCKS
CONV_W = 4
LRU_C = 8.0
D_POOL = D_MODEL // 2
POOL_WINDOWS = (2, 4, 8, 16)
POOL_GROUP = D_POOL // len(POOL_WINDOWS)
N_HEADS = 16
QK_NOPE = 64
QK_ROPE = 32
V_HEAD = 64
Q_RANK = 384
KV_RANK = 256
ROPE_BASE = 10000.0
Q_BLOCK = 128

kernel_name = "hybrid_rglru_pool_mla_dit_step"


def rms_norm(x, g):
    xf = x.astype(jnp.float32)
    y = xf * lax.rsqrt(jnp.mean(xf * xf, axis=-1, keepdims=True) + EPS)
    return (y * g.astype(jnp.float32)).astype(x.dtype)


def adaln(cond, w_mod, b_mod):
    m = jax.nn.silu(cond) @ w_mod + b_mod
    return [t[:, None, :] for t in jnp.split(m, 6, axis=-1)]


def modulate(x, g, shift, scale):
    return rms_norm(x, g) * (1 + scale) + shift


def sq_relu_mlp(h, w1, w2):
    return jnp.square(jax.nn.relu(h @ w1)) @ w2


def dwconv_centred(x, w, b):
    S = x.shape[1]
    left = CONV_W // 2
    xp = jnp.pad(x, ((0, 0), (left, CONV_W - 1 - left), (0, 0)))
    return b + sum(xp[:, k:k + S] * w[k] for k in range(CONV_W))


def rglru_scan(xc, w_a, b_a, w_i, b_i, lam, h0, reverse):
    B, S, _ = xc.shape
    xf = xc.astype(jnp.float32)
    xb = xf.reshape(B, S, RNN_BLOCKS, RNN_BLOCK)
    r = jax.nn.sigmoid(jnp.einsum('bsni,nij->bsnj', xb, w_a).reshape(B, S, D_RNN) + b_a)
    i = jax.nn.sigmoid(jnp.einsum('bsni,nij->bsnj', xb, w_i).reshape(B, S, D_RNN) + b_i)
    log_a = LRU_C * r * jax.nn.log_sigmoid(lam.astype(jnp.float32))
    a = jnp.exp(log_a)
    u = jnp.sqrt(-jnp.expm1(2.0 * log_a)) * (i * xf)
    if reverse:
        a, u = jnp.flip(a, axis=1), jnp.flip(u, axis=1)
    u = u.at[:, 0].add(a[:, 0] * h0.astype(jnp.float32))

    def combine(lhs, rhs):
        a1, b1 = lhs
        a2, b2 = rhs
        return a1 * a2, a2 * b1 + b2

    _, h = lax.associative_scan(combine, (a, u), axis=1)
    h_last = h[:, -1]
    if reverse:
        h = jnp.flip(h, axis=1)
    return h, h_last


def multiscale_pool(xp, w_pool, s_pool):
    B, S, _ = xp.shape
    xf = xp.astype(jnp.float32)
    cs = jnp.pad(jnp.cumsum(xf, axis=1), ((0, 0), (1, 0), (0, 0)))
    t = jnp.arange(S)
    outs = []
    for gi, w in enumerate(POOL_WINDOWS):
        left = w // 2
        right = w - 1 - left
        lo = jnp.maximum(t - left, 0)
        hi = jnp.minimum(t + right, S - 1) + 1
        sl = slice(gi * POOL_GROUP, (gi + 1) * POOL_GROUP)
        csg = cs[..., sl]
        mean = (csg[:, hi] - csg[:, lo]) / (hi - lo).astype(jnp.float32)[None, :, None]
        outs.append(mean - xf[..., sl])
    d = jnp.stack(outs, axis=2)
    y = jnp.einsum('bsgc,gcd->bsgd', d, w_pool).reshape(B, S, D_POOL)
    return (y * s_pool).astype(xp.dtype)


def lru_pool_mixer(h, p, h0_fwd, h0_bwd):
    z = h @ p['w_in']
    xr, gate, xpool = jnp.split(z, [D_RNN, 2 * D_RNN], axis=-1)
    xc = dwconv_centred(xr, p['conv_w'], p['conv_b'])
    hf, hf_last = rglru_scan(xc, p['lru_w_a'][0], p['lru_b_a'][0], p['lru_w_i'][0], p['lru_b_i'][0],
                             p['lru_lam'][0], h0_fwd, False)
    hb, hb_last = rglru_scan(xc, p['lru_w_a'][1], p['lru_b_a'][1], p['lru_w_i'][1], p['lru_b_i'][1],
                             p['lru_lam'][1], h0_bwd, True)
    y_rnn = (hf + hb).astype(h.dtype) * jax.nn.gelu(gate)
    y_pool = multiscale_pool(xpool, p['pool_w'], p['pool_scale'])
    out = jnp.concatenate([y_rnn, y_pool], axis=-1) @ p['w_out']
    return out, jnp.stack([hf_last, hb_last], axis=1)


def axial_angles(S):
    rows = S // GRID_W
    row = jnp.broadcast_to(jnp.arange(rows)[:, None], (rows, GRID_W)).reshape(-1).astype(jnp.float32)
    col = jnp.broadcast_to(jnp.arange(GRID_W)[None, :], (rows, GRID_W)).reshape(-1).astype(jnp.float32)
    half = QK_ROPE // 2
    inv = ROPE_BASE ** (-jnp.arange(0, half, 2, dtype=jnp.float32) / half)
    return row[:, None] * inv, col[:, None] * inv


def rope_rotate(x, ang):
    x1, x2 = jnp.split(x, 2, axis=-1)
    cos, sin = jnp.cos(ang), jnp.sin(ang)
    return jnp.concatenate([x1 * cos - x2 * sin, x2 * cos + x1 * sin], axis=-1)


def rope2d(x, ang_r, ang_c):
    xf = x.astype(jnp.float32)
    half = QK_ROPE // 2
    return jnp.concatenate([rope_rotate(xf[..., :half], ang_r),
                            rope_rotate(xf[..., half:], ang_c)], axis=-1).astype(x.dtype)


def mla_project(h, p):
    B, S, _ = h.shape
    z = h @ p['w_in']
    cq, ckv, kpe = jnp.split(z, [Q_RANK, Q_RANK + KV_RANK], axis=-1)
    q = (rms_norm(cq, p['g_q']) @ p['w_qb']).reshape(B, S, N_HEADS, QK_NOPE + QK_ROPE)
    return q[..., :QK_NOPE], q[..., QK_NOPE:], rms_norm(ckv, p['g_kv']), kpe


def mla_expand(ckv, w_kvb):
    B, S, _ = ckv.shape
    kv = (ckv @ w_kvb).reshape(B, S, N_HEADS, QK_NOPE + V_HEAD)
    return kv[..., :QK_NOPE], kv[..., QK_NOPE:]


def attend_blocked(q_nope, q_pe, k_nope, k_pe, v):
    B, S, H, _ = q_nope.shape
    nb = S // Q_BLOCK
    scale = (QK_NOPE + QK_ROPE) ** -0.5

    def block(args):
        qn, qp = args
        s = (jnp.einsum('bqhd,bkhd->bhqk', qn, k_nope)
             + jnp.einsum('bqhr,bkr->bhqk', qp, k_pe)).astype(jnp.float32) * scale
        pr = jax.nn.softmax(s, axis=-1).astype(v.dtype)
        return jnp.einsum('bhqk,bkhd->bqhd', pr, v)

    def to_blocks(t):
        return jnp.moveaxis(t.reshape(B, nb, Q_BLOCK, *t.shape[2:]), 1, 0)

    o = lax.map(block, (to_blocks(q_nope), to_blocks(q_pe)))
    return jnp.moveaxis(o, 0, 1).reshape(B, S, H * V_HEAD)


def mla_context(h, p):
    q_nope, q_pe, ckv, kpe = mla_project(h, p)
    k_nope, v = mla_expand(ckv, p['w_kvb'])
    o = attend_blocked(q_nope, q_pe, k_nope, kpe, v)
    return o @ p['w_out'], ckv, kpe


def mla_latent(h, p, ctx_ckv, ctx_kpe):
    S = h.shape[1]
    q_nope, q_pe, ckv, kpe = mla_project(h, p)
    ang_r, ang_c = axial_angles(S)
    q_pe = rope2d(q_pe, ang_r[:, None, :], ang_c[:, None, :])
    kpe = rope2d(kpe, ang_r, ang_c)
    k_lat, v_lat = mla_expand(ckv, p['w_kvb'])
    k_ctx, v_ctx = mla_expand(ctx_ckv.astype(h.dtype), p['w_kvb'])
    k_nope = jnp.concatenate([k_lat, k_ctx], axis=1)
    k_pe = jnp.concatenate([kpe, ctx_kpe.astype(h.dtype)], axis=1)
    v = jnp.concatenate([v_lat, v_ctx], axis=1)
    o = attend_blocked(q_nope, q_pe, k_nope, k_pe, v)
    return o @ p['w_out']


def setup_inputs(seed: int = 0) -> dict:
    key = jax.random.key(seed)
    ks = iter(jax.random.split(key, 64))
    nrm = lambda shape, s: jax.random.normal(next(ks), shape, jnp.float32) * s
    gain = lambda n: 1.0 + nrm((n,), 0.02)
    D = D_MODEL
    a0 = jax.random.uniform(next(ks), (2, D_RNN), jnp.float32, 0.9, 0.999)
    pa = a0 ** (1.0 / LRU_C)
    lam = jnp.log(pa) - jnp.log1p(-pa)
    return {
        "x_prompt": nrm((BATCH, SEQ, D), 1.0),
        "x_sample": nrm((DEC_BATCH, DEC_SEQ, D), 1.0),
        "state_l0_lru": nrm((DEC_BATCH, 2, D_RNN), 1.0),
        "cache_l1_ckv": nrm((DEC_BATCH, PAST_LEN, KV_RANK), 1.0),
        "cache_l1_kpe": nrm((DEC_BATCH, PAST_LEN, QK_ROPE), 1.0),
        "c": nrm((DEC_BATCH, D), 1.0),
        "c_ctx": nrm((D,), 1.0),
        "l0_w_mod": nrm((D, 6 * D), D ** -0.5),
        "l0_b_mod": nrm((6 * D,), 0.02),
        "l0_g_mix": gain(D),
        "l0_g_ffn": gain(D),
        "l0_w_in": nrm((D, 2 * D_RNN + D_POOL), D ** -0.5),
        "l0_conv_w": nrm((CONV_W, D_RNN), CONV_W ** -0.5),
        "l0_conv_b": nrm((D_RNN,), 0.02),
        "l0_lru_w_a": nrm((2, RNN_BLOCKS, RNN_BLOCK, RNN_BLOCK), RNN_BLOCK ** -0.5),
        "l0_lru_b_a": nrm((2, D_RNN), 0.02),
        "l0_lru_w_i": nrm((2, RNN_BLOCKS, RNN_BLOCK, RNN_BLOCK), RNN_BLOCK ** -0.5),
        "l0_lru_b_i": nrm((2, D_RNN), 0.02),
        "l0_lru_lam": lam,
        "l0_pool_w": nrm((len(POOL_WINDOWS), POOL_GROUP, POOL_GROUP), POOL_GROUP ** -0.5),
        "l0_pool_scale": gain(D_POOL),
        "l0_w_out": nrm((D_RNN + D_POOL, D), (D_RNN + D_POOL) ** -0.5),
        "l0_ffn_w1": nrm((D, D_FF), D ** -0.5),
        "l0_ffn_w2": nrm((D_FF, D), D_FF ** -0.5),
        "l1_w_mod": nrm((D, 6 * D), D ** -0.5),
        "l1_b_mod": nrm((6 * D,), 0.02),
        "l1_g_mix": gain(D),
        "l1_g_ffn": gain(D),
        "l1_w_in": nrm((D, Q_RANK + KV_RANK + QK_ROPE), D ** -0.5),
        "l1_g_q": gain(Q_RANK),
        "l1_w_qb": nrm((Q_RANK, N_HEADS * (QK_NOPE + QK_ROPE)), Q_RANK ** -0.5),
        "l1_g_kv": gain(KV_RANK),
        "l1_w_kvb": nrm((KV_RANK, N_HEADS * (QK_NOPE + V_HEAD)), KV_RANK ** -0.5),
        "l1_w_out": nrm((N_HEADS * V_HEAD, D), (N_HEADS * V_HEAD) ** -0.5),
        "l1_ffn_w1": nrm((D, D_FF), D ** -0.5),
        "l1_ffn_w2": nrm((D_FF, D), D_FF ** -0.5),
        "g_final": gain(D),
    }


def reference(x_prompt, x_sample, state_l0_lru, cache_l1_ckv, cache_l1_kpe, c, c_ctx,
              l0_w_mod, l0_b_mod, l0_g_mix, l0_g_ffn, l0_w_in, l0_conv_w, l0_conv_b,
              l0_lru_w_a, l0_lru_b_a, l0_lru_w_i, l0_lru_b_i, l0_lru_lam, l0_pool_w, l0_pool_scale,
              l0_w_out, l0_ffn_w1, l0_ffn_w2,
              l1_w_mod, l1_b_mod, l1_g_mix, l1_g_ffn, l1_w_in, l1_g_q, l1_w_qb, l1_g_kv, l1_w_kvb,
              l1_w_out, l1_ffn_w1, l1_ffn_w2, g_final):
    layers = [
        dict(w_mod=l0_w_mod, b_mod=l0_b_mod, g_mix=l0_g_mix, g_ffn=l0_g_ffn, w_in=l0_w_in,
             conv_w=l0_conv_w, conv_b=l0_conv_b, lru_w_a=l0_lru_w_a, lru_b_a=l0_lru_b_a,
             lru_w_i=l0_lru_w_i, lru_b_i=l0_lru_b_i, lru_lam=l0_lru_lam, pool_w=l0_pool_w,
             pool_scale=l0_pool_scale, w_out=l0_w_out, ffn_w1=l0_ffn_w1, ffn_w2=l0_ffn_w2),
        dict(w_mod=l1_w_mod, b_mod=l1_b_mod, g_mix=l1_g_mix, g_ffn=l1_g_ffn, w_in=l1_w_in,
             g_q=l1_g_q, w_qb=l1_w_qb, g_kv=l1_g_kv, w_kvb=l1_w_kvb, w_out=l1_w_out,
             ffn_w1=l1_ffn_w1, ffn_w2=l1_ffn_w2),
    ]
    caches = [(state_l0_lru,), (cache_l1_ckv, cache_l1_kpe)]
    xp, xs = x_prompt, x_sample
    new_state = []
    for l in range(DEPTH):
        p = layers[l]
        mp = adaln(c_ctx[None, :], p['w_mod'], p['b_mod'])
        ms = adaln(c, p['w_mod'], p['b_mod'])
        hp = modulate(xp, p['g_mix'], mp[0], mp[1])
        hs = modulate(xs, p['g_mix'], ms[0], ms[1])
        if l % 2 == 0:
            zeros = jnp.zeros((xp.shape[0], D_RNN), jnp.float32)
            op, st = lru_pool_mixer(hp, p, zeros, zeros)
            st_in = caches[l][0]
            os_, _ = lru_pool_mixer(hs, p, st_in[:, 0], st_in[:, 1])
            new_state.append(st.astype(xp.dtype))
        else:
            op, ckv, kpe = mla_context(hp, p)
            os_ = mla_latent(hs, p, caches[l][0], caches[l][1])
            new_state.append(ckv)
            new_state.append(kpe)
        xp = xp + mp[2] * op
        xs = xs + ms[2] * os_
        xp = xp + mp[5] * sq_relu_mlp(modulate(xp, p['g_ffn'], mp[3], mp[4]), p['ffn_w1'], p['ffn_w2'])
        xs = xs + ms[5] * sq_relu_mlp(modulate(xs, p['g_ffn'], ms[3], ms[4]), p['ffn_w1'], p['ffn_w2'])
    y_prompt = rms_norm(xp, g_final)
    y_sample = rms_norm(xs, g_final)
    new_lru, new_ckv, new_kpe = new_state[0], new_state[1], new_state[2]
    return (y_prompt, y_sample, new_lru, new_ckv, new_kpe)
```

```python
import contextlib
import numpy as np
import concourse.bass as bass
import concourse.mybir as mybir
from concourse.bass_utils import run_bass_kernel_spmd

F32 = mybir.dt.float32
BF16 = mybir.dt.bfloat16
AF = mybir.ActivationFunctionType
ALU = mybir.AluOpType
AX = mybir.AxisListType

D = 1024
KC = 8
EPS = 1e-6
TP, TS, HALO = 1024, 2048, 8
NV = 93
V_GMIX0, V_GFFN0, V_GMIX1, V_GFFN1, V_GFIN, V_CONVW, V_CONVB, V_BA, V_BI, V_LAM, V_SPOOL, V_GQ, V_GKV = \
    0, 8, 16, 24, 32, 40, 56, 60, 68, 76, 84, 88, 91
SCALE = 96.0 ** -0.5
PAIRS = [[0, 1], [2, 3], [4, 5], [6, 7]]


class Sem:
    def __init__(self, h):
        self.h = h; self.cnt = 0


class Buf:
    def __init__(self, name, persistent=False):
        self.name = name; self.w = None; self.r = {}; self.sem = None; self.persistent = persistent


class T:
    def __init__(self, ap, buf):
        self.ap = ap; self.bufs = list(buf) if isinstance(buf, (list, tuple)) else [buf]

    @property
    def buf(self):
        return self.bufs[0]

    @buf.setter
    def buf(self, b):
        self.bufs = [b]

    def __getitem__(self, k):
        return T(self.ap[k], self.bufs)

    def v(self, f):
        return T(f(self.ap), self.bufs)


def _bufs(*ts):
    out = []
    for t in ts:
        if isinstance(t, T): out.extend(t.bufs)
    return out


def _ap(x):
    return x.ap if isinstance(x, T) else x


class Ctx:
    def __init__(self, nc, es):
        self.nc = nc; self.es = es
        self.eng = {'pe': nc.tensor, 'act': nc.scalar, 'dve': nc.vector, 'pool': nc.gpsimd, 'sp': nc.sync}
        self.esem = {e: es.enter_context(nc.semaphore("s_" + e)) for e in ('pe', 'act', 'dve', 'pool')}
        self.cnt = {e: 0 for e in self.esem}
        self.seen = {e: {} for e in self.eng}
        self.sems = []
        self.free = []
        self.inuse = []
        self.dumps = []
        self.debug = False

    def getsem(self):
        if self.free:
            sm = self.free.pop()
        else:
            sm = Sem(self.es.enter_context(self.nc.semaphore("d%d" % len(self.sems)))); self.sems.append(sm)
        self.inuse.append(sm)
        return sm

    def recycle(self):
        self.free.extend(self.inuse); self.inuse = []

    def _sid(self, sem):
        return id(sem)

    def _wait(self, e, stamp):
        sem, val = stamp
        d = self.seen[e]; k = self._sid(sem)
        if d.get(k, 0) < val:
            self.eng[e].wait_ge(sem, val); d[k] = val

    def deps(self, e, reads, writes):
        st = []
        own = self.esem.get(e)
        for b in reads:
            if b.w: st.append(b.w)
        for b in writes:
            if b.w and b.w[0] is not own: st.append(b.w)
            st.extend(v for v in b.r.values() if v[0] is not own)
        pes = self.esem['pe']
        for s in st:
            if e == 'pe' and s[0] is pes:
                continue
            self._wait(e, s)

    def _stamp(self, stamp, reads, writes):
        k = self._sid(stamp[0])
        for b in writes:
            b.w = stamp; b.r = {}
        for b in reads:
            b.r[k] = stamp

    def op(self, e, fn, reads=(), writes=(), signal=True):
        self.deps(e, reads, writes)
        ins = fn(self.eng[e])
        sem = self.esem[e]
        if signal:
            self.cnt[e] += 1; ins.then_inc(sem, 1); stamp = (sem, self.cnt[e])
        else:
            stamp = (sem, self.cnt[e] + 1)
        self._stamp(stamp, reads, writes)
        return ins

    def dma(self, q, out, in_, sb, reads=(), writes=()):
        self.deps(q, reads, writes)
        if sb.sem is None:
            sb.sem = self.getsem()
        ins = self.eng[q].dma_start(out=_ap(out), in_=_ap(in_))
        sb.sem.cnt += 16; ins.then_inc(sb.sem.h, 16)
        self._stamp((sb.sem.h, sb.sem.cnt), reads, writes)

    def collective(self, ins_ap, outs_ap, inbuf, outbuf):
        self.deps('pool', [inbuf], [outbuf])
        sm = Sem(self.es.enter_context(self.nc.semaphore("cc%d" % len(self.sems)))); self.sems.append(sm)
        self.eng['pool'].collective_compute("AllGather", ALU.bypass, replica_groups=PAIRS, ins=[ins_ap],
                                            outs=[outs_ap]).then_inc(sm.h, 1)
        sm.cnt = 1
        self._stamp((sm.h, 1), [inbuf], [outbuf])

    def dump(self, name, t):
        if not self.debug: return
        shp = list(t.ap.shape)
        d = self.nc.dram_tensor("dbg_" + name, shp, t.ap.dtype, kind="ExternalOutput").ap()
        b = Buf("dbg_" + name)
        self.dma('sp', d, t, b, reads=[t.buf])
        self.dumps.append("dbg_" + name)

    def barrier(self):
        for e in ('pe', 'act', 'dve', 'pool', 'sp'):
            for f in self.esem:
                if self.cnt[f]: self._wait(e, (self.esem[f], self.cnt[f]))
            for sm in self.sems:
                if sm.cnt: self._wait(e, (sm.h, sm.cnt))

    def finish(self):
        for sm in self.sems:
            if sm.cnt: self._wait('sp', (sm.h, sm.cnt))
        for f in self.esem:
            if self.cnt[f]: self._wait('sp', (self.esem[f], self.cnt[f]))

    def act(self, out, in_, func, bias=0.0, scale=1.0):
        self.op('act', lambda e: e.activation(out=out.ap, in_=in_.ap, func=func, bias=_ap(bias), scale=_ap(scale)),
                reads=_bufs(in_, bias, scale), writes=out.bufs)

    def tt(self, out, a, b, op):
        self.op('dve', lambda e: e.tensor_tensor(out=out.ap, in0=a.ap, in1=b.ap, op=op), reads=_bufs(a, b), writes=out.bufs)

    def stt(self, out, a, s, b, op0, op1):
        self.op('dve', lambda e: e.scalar_tensor_tensor(out=out.ap, in0=a.ap, scalar=_ap(s), in1=b.ap, op0=op0, op1=op1),
                reads=_bufs(a, s, b), writes=out.bufs)

    def ts(self, out, a, s1, s2, op0, op1=None):
        if op1 is None:
            self.op('dve', lambda e: e.tensor_scalar(out=out.ap, in0=a.ap, scalar1=_ap(s1), scalar2=None, op0=op0),
                    reads=_bufs(a, s1), writes=out.bufs)
        else:
            self.op('dve', lambda e: e.tensor_scalar(out=out.ap, in0=a.ap, scalar1=_ap(s1), scalar2=_ap(s2), op0=op0, op1=op1),
                    reads=_bufs(a, s1, s2), writes=out.bufs)

    def copy(self, out, in_, eng='dve'):
        if eng == 'act':
            self.act(out, in_, AF.Copy)
        else:
            self.op(eng, lambda e: e.tensor_copy(out=out.ap, in_=in_.ap), reads=in_.bufs, writes=out.bufs)

    def recip(self, out, in_):
        self.op('dve', lambda e: e.reciprocal(out=out.ap, in_=in_.ap), reads=in_.bufs, writes=out.bufs)

    def memset(self, out, val, eng='dve'):
        self.op(eng, lambda e: e.memset(out.ap, val), writes=out.bufs)

    def mm(self, out, lhsT, rhs, start, stop, signal=None, tile_position=None):
        if signal is None: signal = stop
        kw = {}
        if tile_position is not None: kw['tile_position'] = tile_position
        self.op('pe', lambda e: e.matmul(out.ap, lhsT=lhsT.ap, rhs=rhs.ap, start=start, stop=stop, **kw),
                reads=_bufs(lhsT, rhs), writes=out.bufs, signal=signal)

    def transpose(self, out, in_, ident):
        self.op('pe', lambda e: e.transpose(out=out.ap, in_=in_.ap, identity=ident.ap), reads=_bufs(in_, ident), writes=out.bufs)

    def scan(self, out, a, u, init):
        self.op('dve', lambda e: e.tensor_tensor_scan(out=out.ap, data0=a.ap, data1=u.ap, initial=_ap(init), op0=ALU.mult, op1=ALU.add),
                reads=_bufs(a, u, init), writes=out.bufs)


class Region:
    def __init__(self, arena, base, size):
        self.arena = arena; self.base = base; self.size = size; self.p = 0

    def reset(self):
        self.p = 0

    def alloc(self, name, shape, dtype, persistent=False):
        n = int(np.prod(shape)); b = n * (2 if dtype == BF16 else 4)
        b = (b + 31) // 32 * 32
        assert self.p + b <= self.size, ("region overflow", name, self.p, b, self.size)
        w0 = (self.base + self.p) // 4; self.p += b
        ap = self.arena[:, w0:w0 + b // 4]
        if dtype == BF16:
            ap = ap.bitcast(BF16)
        ap = ap[:, 0:n]
        if len(shape) == 2:
            ap = ap.rearrange("p (a b) -> p a b", a=shape[0])
        elif len(shape) == 3:
            ap = ap.rearrange("p (a b c) -> p a b c", a=shape[0], b=shape[1])
        t = T(ap, Buf(name, persistent)); t.w0 = w0; t.nb = b
        return t

    def view(self, t, shape, dtype):
        n = int(np.prod(shape)); b = n * (2 if dtype == BF16 else 4)
        assert b <= t.nb
        ap = self.arena[:, t.w0:t.w0 + t.nb // 4]
        if dtype == BF16:
            ap = ap.bitcast(BF16)
        ap = ap[:, 0:n]
        if len(shape) == 2:
            ap = ap.rearrange("p (a b) -> p a b", a=shape[0])
        elif len(shape) == 3:
            ap = ap.rearrange("p (a b c) -> p a b c", a=shape[0], b=shape[1])
        return T(ap, t.buf)


def build_program(debug=False):
    nc = bass.Bass("TRN2", target_bir_lowering=False)

    def din(name, shape, dt=F32):
        return nc.dram_tensor(name, list(shape), dt, kind="ExternalInput").ap()

    def dout(name, shape, dt=F32):
        return nc.dram_tensor(name, list(shape), dt, kind="ExternalOutput").ap()

    I = {}
    for nm, shp in [("wmod_h", (D, 6 * D)), ("bmod_h", (128, 48)), ("vecs", (128, NV)),
                    ("w_in0", (D, 1536)), ("wbd", (128, 16, 128)), ("pool_w", (128, 4, 128)), ("w_out0", (D, D)),
                    ("ffn_w1_0", (D, 4 * D)), ("ffn_w2_0", (4 * D, D)), ("w_in1x", (D, 704)), ("w_qbx", (384, 2048)),
                    ("w_kvb", (256, 2048)), ("w_out1", (D, D)), ("ffn_w1_1", (D, 4 * D)), ("ffn_w2_1", (4 * D, D)),
                    ("gkv_row", (256,)), ("ident", (128, 128)),
                    ("x_p", (TP, D)), ("x_s", (TS + 2 * HALO, D)), ("cT", (128, 8, 2)), ("st0", (128, 2, 4)),
                    ("ctx_ckv", (512, 256)), ("ctx_kpe", (512, 32)), ("rope", (32, 2, TS)),
                    ("invc_p", (4, TP)), ("invc_s", (4, TS)), ("hmask", (16,)), ("sel", (2,))]:
        I[nm] = din(nm, shp)
    O = {"y_p": dout("y_p", (TP, D)), "y_s": dout("y_s", (TS, D)), "o_lru": dout("o_lru", (32, 128)),
         "o_ckv": dout("o_ckv", (TP, 256)), "o_kpe": dout("o_kpe", (TP, 32))}
    cc_st_in = [nc.dram_tensor("cc_st_in%d" % c, [2, 128], F32).ap() for c in range(4)]
    cc_st_out = [nc.dram_tensor("cc_st_out%d" % c, [4, 128], F32).ap() for c in range(4)]
    cc_mod_in = nc.dram_tensor("cc_mod_in", [128, 96], F32).ap()
    cc_mod_out = nc.dram_tensor("cc_mod_out", [256, 96], F32).ap()
    cc_kv_in = nc.dram_tensor("cc_kv_in", [288, TS], BF16).ap()
    cc_kv_out = nc.dram_tensor("cc_kv_out", [576, TS], BF16).ap()

    es = contextlib.ExitStack()
    with es:
        C = Ctx(nc, es)
        ARENA = 207 * 1024
        arena = es.enter_context(nc.sbuf_tensor("arena", [128, ARENA // 4], F32))
        psum_t = es.enter_context(nc.psum_tensor("psum", [128, 8, 512], F32))
        banks = [T(psum_t[:, i, :], Buf("bank%d" % i)) for i in range(8)]
        held = set()
        rr = [0]

        def getbank(hold=False):
            for _ in range(8):
                i = rr[0]; rr[0] = (rr[0] + 1) % 8
                if i not in held:
                    if hold: held.add(i)
                    return i, banks[i]
            raise RuntimeError("no bank")

        def release(i):
            held.discard(i)

        R_const = Region(arena, 0, 9 * 1024)
        R_ffn = Region(arena, 9 * 1024, 32 * 1024)
        R_x = Region(arena, 41 * 1024, 64 * 1024)
        R_h = Region(arena, 105 * 1024, 33 * 1024 + 256)
        R_s = Region(arena, 105 * 1024 + 33 * 1024 + 256, ARENA - (105 * 1024 + 33 * 1024 + 256))

        cbuf = Buf("const")

        def calloc(name, shape, dt=F32, own=False):
            t = R_const.alloc(name, shape, dt)
            if not own: t.buf = cbuf
            return t

        vecs = calloc("vecs", [NV]); bmod = calloc("bmod", [48]); cT = calloc("cT", [8, 2]); st0 = calloc("st0", [2, 4])
        ident = calloc("ident", [128]); sel = calloc("sel", [2]); hmask = calloc("hmask", [16])
        ones_b = calloc("ones_b", [128], BF16); ones_f = calloc("ones_f", [64])
        wbd = calloc("wbd", [16, 128], BF16); poolw = calloc("poolw", [4, 128], BF16)
        ones_b.buf = Buf("ones_b"); ones_f.buf = Buf("ones_f")
        scT = calloc("scT", [8, 2], own=True); s8 = calloc("s8", [8], own=True); s16 = calloc("s16", [8], own=True)
        lt = calloc("lt", [8], own=True)
        modsb = [calloc("modsb%d" % l, [48, 2], own=True) for l in range(2)]
        modA = [calloc("modA%d" % l, [2, 2, 8], own=True) for l in range(2)]
        stcols = calloc("stcols", [32], own=True); gsel = calloc("gsel", [4, 4], own=True)
        sinit = calloc("sinit", [4, 2], own=True); sfin = calloc("sfin", [4, 2], own=True)
        lsem = Buf("cload")
        for t, src in [(vecs, I["vecs"]), (bmod, I["bmod_h"]), (cT, I["cT"]), (st0, I["st0"]), (ident, I["ident"]),
                       (sel, I["sel"].partition_broadcast(128)), (hmask, I["hmask"].partition_broadcast(128))]:
            C.dma('sp', t, src, lsem, writes=[])
        C.dma('pool', wbd, I["wbd"], lsem, writes=[])
        C.dma('pool', poolw, I["pool_w"], lsem, writes=[])
        cbuf.w = (lsem.sem.h, lsem.sem.cnt)
        C.memset(ones_b, 1.0); C.memset(ones_f, 1.0)
        C.act(scT, cT, AF.Silu)
        C.act(lt, vecs[:, V_LAM:V_LAM + 8], AF.Exp, scale=-1.0)
        C.act(lt, lt, AF.Ln, bias=1.0)
        C.ts(s8, lt, -8.0, None, ALU.mult)
        C.ts(s16, lt, -16.0, None, ALU.mult)

        def vcol(off, i):
            return vecs[:, off + i:off + i + 1]

        R_s.reset()
        wm_slots = [R_s.alloc("wm%d" % i, [8, 1536], BF16) for i in range(2)]
        scTb = R_s.alloc("scTb", [8, 2], BF16)
        C.copy(scTb, scT)
        pcn = 0
        modl = R_s.alloc("modl", [48, 2], F32)
        wsrc = I["wmod_h"].rearrange("(kc p) f -> p kc f", p=128)
        bi, bk = getbank()
        for pc in range(4):
            sl = wm_slots[pcn % 2]; pcn += 1
            C.dma('pool', sl, wsrc[:, :, pc * 1536:(pc + 1) * 1536], sl.buf, writes=[sl.buf])
            for f in range(12):
                fc = pc * 12 + f
                for kc in range(KC):
                    C.mm(bk[:, fc * 2:fc * 2 + 2], sl[:, kc, f * 128:(f + 1) * 128], scTb[:, kc, :], kc == 0, kc == KC - 1)
        C.tt(modl, bk[:, 0:96].v(lambda a: a.rearrange("p (f j) -> p f j", j=2)),
             bmod.v(lambda a: a.unsqueeze(2).broadcast_to([128, 48, 2])), ALU.add)
        mi = Buf("ccmodin"); mo = Buf("ccmodout")
        C.dma('pool', cc_mod_in, modl.v(lambda a: a.rearrange("p f j -> p (f j)")), mi, reads=[modl.buf], writes=[mi])
        C.collective(cc_mod_in, cc_mod_out, mi, mo)
        for l in range(2):
            C.dma('sp', modsb[l].v(lambda a: a.rearrange("p f j -> p (f j)")), cc_mod_out[l * 128:(l + 1) * 128, :], modsb[l].buf,
                  reads=[mo], writes=[modsb[l].buf])
        for l in range(2):
            for j in range(2):
                for m, (sp_scale, goff) in enumerate([(1, V_GMIX0 if l == 0 else V_GMIX1), (4, V_GFFN0 if l == 0 else V_GFFN1)]):
                    C.ts(modA[l][:, j, m, :], modsb[l][:, sp_scale * 8:sp_scale * 8 + 8, j], 1.0, None, ALU.add)
                    C.tt(modA[l][:, j, m, :], modA[l][:, j, m, :], vecs[:, goff:goff + 8], ALU.mult)

        def MOD(l, j, which, c):
            if which == 'A_mix': return modA[l][:, j, 0, c:c + 1]
            if which == 'A_ffn': return modA[l][:, j, 1, c:c + 1]
            s = {'B_mix': 0, 'G_mix': 2, 'B_ffn': 3, 'G_ffn': 5}[which]
            return modsb[l][:, s * 8 + c:s * 8 + c + 1, j]

        C.barrier()

        def run_wave(kind):
            S = (kind == 'S')
            Tn = TS if S else TP
            nseq, L = (1, TS) if S else (4, 256)
            W = L + 16
            NT = Tn // 512
            cj = 1 if S else 0
            xd = I["x_s"] if S else I["x_p"]
            xoff = HALO if S else 0
            yd = O["y_s"] if S else O["y_p"]
            TH = Tn + (16 if S else 0)

            R_x.reset(); R_h.reset(); R_s.reset(); R_ffn.reset()
            hT = R_h.alloc("hT", [8, TH], BF16)
            hT_t = [T(hT.ap[:, :, t * 512:(t + 1) * 512], Buf("hT%d" % t)) for t in range(NT)]
            hT_h = T(hT.ap[:, :, Tn:TH], Buf("hTh")) if S else None

            def tsl(t):
                return slice(t * 512, (t + 1) * 512)

            def load_x_fm(region, dst_fn, j0, nj, stg):
                for j in range(j0, j0 + nj):
                    st = stg[j % len(stg)]
                    C.dma('sp', st, xd[xoff + j * 128: xoff + (j + 1) * 128, :], st.buf, writes=[st.buf])
                    dst = dst_fn(j)
                    for half in range(2):
                        bi, bk = getbank()
                        for q in range(4):
                            c = half * 4 + q
                            C.transpose(bk[:, q * 128:(q + 1) * 128], st[:, c * 128:(c + 1) * 128], ident)
                        C.copy(dst[:, half * 4:half * 4 + 4, :], bk.v(lambda a: a.rearrange("p (q t) -> p q t", q=4)),
                               eng='act' if j % 2 == 0 else 'dve')

            def norm_stats(src, nch, n, dim, tmp, i=0):
                sq = tmp['sq'][i % len(tmp['sq'])]
                C.act(sq[:, 0:nch, 0:n], src, AF.Square)
                bi, bk = getbank()
                for c in range(nch):
                    C.mm(bk[:, 0:n], ones_b, sq[:, c, 0:n], c == 0, c == nch - 1)
                sd = tmp['sd'][i % len(tmp['sd'])]
                C.act(sd[:, 0:n], bk[:, 0:n], AF.Ln, bias=EPS, scale=1.0 / dim)
                C.act(sd[:, 0:n], sd[:, 0:n], AF.Exp, scale=-0.5)

            def norm_mod(src, nch, n, dim, A, B, out, tmp, i=0, stats=True):
                if stats:
                    norm_stats(src, nch, n, dim, tmp, i)
                sd = tmp['sd'][i % len(tmp['sd'])]
                for c in range(nch):
                    if B is None:
                        C.stt(out(c), src[:, c, :], A(c), sd[:, 0:n], ALU.mult, ALU.mult)
                    else:
                        tt_ = tmp['t'][c % 2]
                        C.stt(tt_[:, 0:n], src[:, c, :], A(c), sd[:, 0:n], ALU.mult, ALU.mult)
                        C.act(out(c), tt_[:, 0:n], AF.Identity, bias=B(c))

            def norm_tmps(region, nb=1):
                return {'sq': [region.alloc("sq%d" % i, [8, 512], BF16) for i in range(nb)],
                        'sd': [region.alloc("sd%d" % i, [512], F32) for i in range(nb)],
                        't': [region.alloc("nt%d" % i, [512], F32) for i in range(2)]}

            def pipeline(n, stA, stB):
                if n == 0: return
                stA(0)
                for i in range(n):
                    if i + 1 < n: stA(i + 1)
                    stB(i)

            stg = [R_s.alloc("xstg%d" % i, [D], F32) for i in range(2)]
            ntm = norm_tmps(R_s, 2)
            if S:
                NXF = 3
                xfm = [R_x.alloc("xfm%d" % i, [8, 512], F32) for i in range(NXF)]
                xfs = []
                for i in range(NXF):
                    subs = [T(xfm[i].ap[:, :, jj * 128:(jj + 1) * 128], Buf("xfm%d_%d" % (i, jj))) for jj in range(4)]
                    xfm[i].bufs = [sb_.buf for sb_ in subs]
                    xfs.append(subs)
            else:
                xT = R_x.alloc("xT", [8, Tn], F32)
                xT_sub = [[T(xT.ap[:, :, t * 512 + jj * 128:t * 512 + (jj + 1) * 128], Buf("xT%d_%d" % (t, jj))) for jj in range(4)]
                          for t in range(NT)]
                xT_t = [T(xT.ap[:, :, t * 512:(t + 1) * 512], [sb_.buf for sb_ in xT_sub[t]]) for t in range(NT)]
                NXF = NT
                xfm = xT_t; xfs = xT_sub
            keep_x = R_x.p
            stLd = lambda t: load_x_fm(R_s, lambda j, t=t: xfs[t % NXF][j - 4 * t], 4 * t, 4, stg)
            stSt = lambda t: norm_stats(xfm[t % NXF], 8, 512, D, ntm, t)
            stAp = lambda t: norm_mod(xfm[t % NXF], 8, 512, D, lambda c: MOD(0, cj, 'A_mix', c), lambda c: MOD(0, cj, 'B_mix', c),
                                      lambda c, t=t: hT_t[t][:, c, :], ntm, t, stats=False)
            stLd(0)
            if NT > 1: stLd(1)
            if S:
                sth = R_s.alloc("xstgh", [D], F32)
                xfh = R_s.alloc("xfh", [8, 16], F32)
                C.dma('sp', sth[0:8, :], I["x_s"][0:8, :], sth.buf, writes=[sth.buf])
                C.dma('sp', sth[8:16, :], I["x_s"][TS + 8:TS + 16, :], sth.buf, writes=[sth.buf])
                for half in range(2):
                    bi, bk = getbank()
                    for q in range(4):
                        c = half * 4 + q
                        C.transpose(bk[:, q * 16:(q + 1) * 16], sth[0:16, c * 128:(c + 1) * 128], ident[0:16, 0:16])
                    C.copy(xfh[:, half * 4:half * 4 + 4, :], bk[:, 0:64].v(lambda a: a.rearrange("p (q t) -> p q t", q=4)))
                norm_mod(xfh, 8, 16, D, lambda c: MOD(0, cj, 'A_mix', c), lambda c: MOD(0, cj, 'B_mix', c),
                         lambda c: hT_h[:, c, :], ntm)
                C.tt(hT_h, hT_h, hmask.v(lambda a: a.unsqueeze(1).broadcast_to([128, 8, 16])), ALU.mult)
            stSt(0)
            for t in range(NT):
                if t + 2 < NT: stLd(t + 2)
                if t + 1 < NT: stSt(t + 1)
                stAp(t)
            C.barrier()

            R_s.reset()
            if S:
                R_x.reset()
            else:
                R_x.p = keep_x
            yT = R_s.alloc("yT", [8, Tn], BF16)
            wsl = [R_s.alloc("wsl%d" % i, [8, 128], BF16) for i in range(4)]
            win = R_s.alloc("win", [nseq, W], F32)
            xc = R_s.alloc("xc", [nseq, L], F32)
            xcb = R_s.alloc("xcb", [nseq, L], BF16)
            gg = R_s.alloc("gg", [Tn], BF16)
            b1s = [R_x.alloc("b1_%d" % d, [nseq, L], F32) for d in range(2)]
            ab = [R_x.alloc("a%d" % d, [nseq, L], F32) for d in range(2)]
            ub = [R_x.alloc("u%d" % d, [nseq, L], F32) for d in range(2)]
            hb_ = [R_x.alloc("h%d" % d, [nseq, L], F32) for d in range(2)]
            w_in0 = I["w_in0"].rearrange("(kc p) f -> p kc f", p=128)
            wslc = [0]

            def load_wcol(col0):
                s = wsl[wslc[0] % 4]; wslc[0] += 1
                C.dma('pool', s, w_in0[:, :, col0:col0 + 128], s.buf, writes=[s.buf])
                return s

            def inproj(ws, evac):
                for t in range(NT):
                    bi, bk = getbank()
                    for kc in range(KC):
                        C.mm(bk, ws[:, kc, :], hT_t[t][:, kc, :], kc == 0, kc == KC - 1)
                    evac(t, bk)

            def win_evac(t, bk):
                if S:
                    C.copy(win[:, 0, 8 + t * 512: 8 + (t + 1) * 512], bk, eng='act')
                else:
                    C.copy(win[:, 2 * t:2 * t + 2, 8:8 + L], bk.v(lambda a: a.rearrange("p (s l) -> p s l", s=2)), eng='act')

            def win_halo(ws):
                if S:
                    bi, bk = getbank()
                    for kc in range(KC):
                        C.mm(bk[:, 0:16], ws[:, kc, :], hT_h[:, kc, :], kc == 0, kc == KC - 1)
                    C.copy(win[:, 0, 0:8], bk[:, 0:8], eng='act')
                    C.copy(win[:, 0, W - 8:W], bk[:, 8:16], eng='act')

            if not S:
                C.memset(win, 0.0)
            invd = I["invc_s"] if S else I["invc_p"]
            p1 = R_ffn.alloc("p1", [nseq, W], F32); p2 = R_ffn.alloc("p2", [nseq, W], F32)
            dpl = R_ffn.alloc("dpl", [Tn], BF16)
            invt = R_ffn.alloc("invt", [nseq, L], F32)

            def pool_branch(g, ws_p):
                inproj(ws_p, win_evac); win_halo(ws_p)
                C.dma('sp', invt, invd[g].rearrange("(s l) -> s l", s=nseq).partition_broadcast(128), invt.buf, writes=[invt.buf])
                C.tt(p1[:, :, 1:W], win[:, :, 0:W - 1], win[:, :, 1:W], ALU.add)
                fin, oth = p1, p2
                if g >= 1:
                    C.tt(p2[:, :, 2:W - 1], p1[:, :, 1:W - 2], p1[:, :, 3:W], ALU.add); fin, oth = p2, p1
                if g >= 2:
                    C.tt(p1[:, :, 4:W - 3], p2[:, :, 2:W - 5], p2[:, :, 6:W - 1], ALU.add); fin, oth = p1, p2
                if g >= 3:
                    C.tt(p2[:, :, 8:W - 7], p1[:, :, 4:W - 11], p1[:, :, 12:W - 3], ALU.add); fin, oth = p2, p1
                C.tt(oth[:, :, 8:8 + L], fin[:, :, 8:8 + L], invt, ALU.mult)
                C.tt(dpl.v(lambda a: a.rearrange("p (s l) -> p s l", s=nseq)), oth[:, :, 8:8 + L], win[:, :, 8:8 + L], ALU.subtract)
                for t in range(NT):
                    bi, bk = getbank()
                    C.mm(bk, poolw[:, g, :], dpl[:, tsl(t)], True, True)
                    C.act(yT[:, 4 + g, tsl(t)], bk, AF.Copy, scale=vcol(V_SPOOL, g))

            ggs = [gg, R_s.alloc("gg2", [Tn], BF16)]

            def front_loads(c):
                return (load_wcol(c * 128), load_wcol(512 + c * 128), load_wcol(1024 + c * 128))

            def front(c, slots):
                ws_x, ws_g, ws_p_ = slots
                inproj(ws_x, win_evac); win_halo(ws_x)
                g_ = ggs[c % 2]
                inproj(ws_g, lambda t, bk: C.act(g_[:, tsl(t)], bk, AF.Gelu_apprx_tanh))
                C.ts(xc, win[:, :, 6:6 + L], vcol(V_CONVW, c * 4 + 0), vcol(V_CONVB, c), ALU.mult, ALU.add)
                for k in range(1, 4):
                    C.stt(xc, win[:, :, 6 + k:6 + k + L], vcol(V_CONVW, c * 4 + k), xc, ALU.mult, ALU.add)
                C.copy(xcb, xc, eng='act')
                return ws_p_

            ibuf = [p1, p2]

            def t3(b3, t):
                if S:
                    return b3[:, 0, t * 512:(t + 1) * 512]
                return b3[:, 2 * t:2 * t + 2, 0:L]

            def bkv(bk):
                return bk if S else bk.v(lambda a: a.rearrange("p (s l) -> p s l", s=2))

            def gates(c):
                xcbf = xcb.v(lambda a: a.rearrange("p s l -> p (s l)"))
                for t in range(NT):
                    for d in range(2):
                        bi, bk = getbank()
                        C.mm(bk, wbd[:, (d * 2 + 0) * 4 + c, :], xcbf[:, tsl(t)], True, True)
                        C.act(t3(b1s[d], t), bkv(bk), AF.Sigmoid, bias=vcol(V_BA, d * 4 + c))
                        bi, bk = getbank()
                        C.mm(bk, wbd[:, (d * 2 + 1) * 4 + c, :], xcbf[:, tsl(t)], True, True)
                        C.act(t3(ibuf[d], t), bkv(bk), AF.Sigmoid, bias=vcol(V_BI, d * 4 + c))

            wsp_next = front(0, front_loads(0))
            gates(0)
            for c in range(4):
                ws_p = wsp_next
                gg = ggs[c % 2]
                for d in range(2):
                    C.act(ab[d], b1s[d], AF.Exp, scale=s8[:, d * 4 + c:d * 4 + c + 1])
                    C.act(b1s[d], b1s[d], AF.Exp, scale=s16[:, d * 4 + c:d * 4 + c + 1])
                for d in range(2):
                    C.tt(ub[d], ibuf[d][:, :, 0:L], xc, ALU.mult)
                    C.act(b1s[d], b1s[d], AF.Relu, bias=1.0, scale=-1.0)
                    C.act(b1s[d], b1s[d], AF.Sqrt)
                for d in range(2):
                    C.tt(ub[d], ub[d], b1s[d], ALU.mult)

                def do_scans(inits):
                    for d in range(2):
                        for s in range(nseq):
                            sl = (slice(None), s, slice(None)) if d == 0 else (slice(None), s, slice(None, None, -1))
                            C.scan(hb_[d][sl], ab[d][sl], ub[d][sl], inits[d])
                if not S:
                    do_scans([0.0, 0.0])
                    for d in range(2):
                        pos = L - 1 if d == 0 else 0
                        C.copy(stcols.v(lambda a: a.rearrange("p (s d c) -> p s d c", s=4, d=2))[:, :, d, c], hb_[d][:, :, pos])
                    nl = front_loads(c + 1) if c + 1 < 4 else None
                    pool_branch(c, ws_p)
                    if c + 1 < 4:
                        wsp_next = front(c + 1, nl)
                        gates(c + 1)
                else:
                    do_scans([st0[:, 0, c:c + 1], st0[:, 1, c:c + 1]])
                    C.copy(sfin[:, c, 0:1], hb_[0][:, 0, L - 1:L]); C.copy(sfin[:, c, 1:2], hb_[1][:, 0, 0:1])
                    nl = front_loads(c + 1) if c + 1 < 4 else None
                    ccb = Buf("ccst%d" % c)
                    for d in range(2):
                        C.dma('pool', cc_st_in[c][d, :].rearrange("(p o) -> p o", o=1), sfin[:, c, d:d + 1], ccb,
                              reads=[sfin.buf], writes=[ccb])
                    ccb2 = Buf("ccst2_%d" % c)
                    C.collective(cc_st_in[c], cc_st_out[c], ccb, ccb2)
                    pool_branch(c, ws_p)
                    if c + 1 < 4:
                        wsp_next = front(c + 1, nl)
                        gates(c + 1)
                    ld = Buf("ccstl%d" % c)
                    for r_ in (0, 3):
                        C.dma('sp', gsel[:, c, r_:r_ + 1], cc_st_out[c][r_, :].rearrange("(p o) -> p o", o=1), ld,
                              reads=[ccb2], writes=[gsel.buf])
                    C.ts(sinit[:, c, 0:1], st0[:, 0, c:c + 1], sel[:, 0:1], None, ALU.mult)
                    C.stt(sinit[:, c, 0:1], gsel[:, c, 0:1], sel[:, 1:2], sinit[:, c, 0:1], ALU.mult, ALU.add)
                    C.ts(sinit[:, c, 1:2], st0[:, 1, c:c + 1], sel[:, 1:2], None, ALU.mult)
                    C.stt(sinit[:, c, 1:2], gsel[:, c, 3:4], sel[:, 0:1], sinit[:, c, 1:2], ALU.mult, ALU.add)
                    do_scans([sinit[:, c, 0:1], sinit[:, c, 1:2]])
                hsum = hb_[0]; yo = yT[:, c, :]; hfl = hb_[0].v(lambda a: a.rearrange("p s l -> p (s l)"))
                C.op('pool', lambda e, hsum=hsum: e.tensor_tensor(out=hsum.ap, in0=hsum.ap, in1=hb_[1].ap, op=ALU.add),
                     reads=_bufs(hsum, hb_[1]), writes=hsum.bufs)
                C.op('pool', lambda e, yo=yo, hfl=hfl, gg=gg: e.tensor_tensor(out=yo.ap, in0=hfl.ap, in1=gg.ap, op=ALU.mult),
                     reads=_bufs(hfl, gg), writes=yo.bufs)
            if not S:
                bi, bk = getbank()
                C.transpose(bk[0:32, 0:128], stcols, ident)
                so = R_ffn.alloc("so", [128], F32)
                C.copy(so[0:32, :], bk[0:32, 0:128])
                C.dma('sp', O["o_lru"], so[0:32, :], so.buf, reads=[so.buf])
            C.barrier()

            def ffn_slots():
                R_ffn.reset()
                w1s_ = [R_ffn.alloc("w1s%d" % i, [8, 512], BF16) for i in range(2)]
                w2s_ = [R_ffn.alloc("w2s%d" % i, [4, D], BF16) for i in range(2)]
                return w1s_, w2s_

            def ffn_load(l, g, w1s_, w2s_):
                w1src = I["ffn_w1_%d" % l].rearrange("(kc p) f -> p kc f", p=128)
                w2src = I["ffn_w2_%d" % l].rearrange("(kc p) f -> p kc f", p=128)
                w1, w2 = w1s_[g % 2], w2s_[g % 2]
                C.dma('pool', w1, w1src[:, :, g * 512:(g + 1) * 512], w1.buf, writes=[w1.buf])
                C.dma('pool', w2, w2src[:, g * 4:(g + 1) * 4, :], w2.buf, writes=[w2.buf])

            if S:
                R_x.reset()
                xT = R_x.alloc("xT", [8, Tn], F32)
                xT_sub = [[T(xT.ap[:, :, t * 512 + jj * 128:t * 512 + (jj + 1) * 128], Buf("xT%d_%d" % (t, jj))) for jj in range(4)]
                          for t in range(NT)]
                xT_t = [T(xT.ap[:, :, t * 512:(t + 1) * 512], [sb_.buf for sb_ in xT_sub[t]]) for t in range(NT)]
            R_s.reset()
            yT2 = R_s.alloc("yT", [8, Tn], BF16); yT2.buf = yT.buf
            wo = R_s.alloc("wo", [8, D], BF16)
            stg = [R_s.alloc("xstg%d" % i, [D], F32) for i in range(2)]
            wosrc = I["w_out0"].rearrange("(kc p) f -> p kc f", p=128)
            for hlf in range(2):
                C.dma('pool', wo[:, hlf * 4:hlf * 4 + 4, :], wosrc[:, hlf * 4:hlf * 4 + 4, :], wo.buf, writes=[wo.buf])

            def outproj_acc(wt, rhs_fn, nk, t, gname, l):
                for oc in range(8):
                    bi, bk = getbank()
                    for kc in range(nk):
                        C.mm(bk, wt(kc, oc), rhs_fn(kc), kc == 0, kc == nk - 1)
                    C.stt(xT_t[t][:, oc, :], bk, MOD(l, cj, gname, oc), xT_t[t][:, oc, :], ALU.mult, ALU.add)

            R_ffn.reset()
            ntm_pre = norm_tmps(R_ffn, 1)

            def stD(t):
                outproj_acc(lambda kc, oc: wo[:, kc, oc * 128:(oc + 1) * 128], lambda kc, t=t: yT2[:, kc, tsl(t)], 8, t, 'G_mix', 0)
                if t == 0:
                    norm_mod(xT_t[0], 8, 512, D, lambda c: MOD(0, cj, 'A_ffn', c), lambda c: MOD(0, cj, 'B_ffn', c),
                             lambda c: hT_t[0][:, c, :], ntm_pre, 0)

            pipeline(NT,
                     lambda t: (load_x_fm(R_s, lambda j, t=t: xT_sub[t][j - 4 * t], 4 * t, 4, stg) if S else None),
                     stD)
            C.barrier()

            def ffn(l, pre=None, skip0=False):
                R_s.reset()
                if pre is None:
                    w1s, w2s = ffn_slots()
                else:
                    w1s, w2s = pre
                ntm2 = norm_tmps(R_s, 2)

                def nrm(t):
                    norm_mod(xT_t[t], 8, 512, D, lambda c: MOD(l, cj, 'A_ffn', c), lambda c: MOD(l, cj, 'B_ffn', c),
                             lambda c, t=t: hT_t[t][:, c, :], ntm2, t)
                if not skip0:
                    nrm(0)
                actb = [R_s.alloc("actb%d" % i, [4, 512], BF16) for i in range(2)]
                rl = [R_s.alloc("rl%d" % i, [512], F32) for i in range(2)]
                k = 0
                prev = None

                def acc(pv):
                    w2_, ac_, t_ = pv
                    outproj_acc(lambda kc, oc: w2_[:, kc, oc * 128:(oc + 1) * 128], lambda kc: ac_[:, kc, :], 4, t_, 'G_ffn', l)

                for g in range(8):
                    w1, w2 = w1s[g % 2], w2s[g % 2]
                    if not (g == 0 and pre is not None):
                        ffn_load(l, g, w1s, w2s)
                    for t in range(NT):
                        if g == 0 and t + 1 < NT:
                            nrm(t + 1)
                        ac = actb[k % 2]; k += 1
                        for j in range(4):
                            bi, bk = getbank()
                            for kc in range(KC):
                                C.mm(bk, w1[:, kc, j * 128:(j + 1) * 128], hT_t[t][:, kc, :], kc == 0, kc == KC - 1)
                            r = rl[j % 2]
                            C.act(r, bk, AF.Relu)
                            C.tt(ac[:, j, :], r, r, ALU.mult)
                        if prev is not None:
                            acc(prev)
                        prev = (w2, ac, t)
                acc(prev)
                C.barrier()

            ffn(0, skip0=True)

            R_s.reset(); R_ffn.reset()
            NK = (2 * TS + 512) if S else TP
            ckvn = R_s.alloc("ckvn", [2, NK], BF16)
            Kb = R_s.alloc("Kb", [NK], BF16)
            cqn = R_s.alloc("cqn", [3, Tn], BF16)
            keep_s = R_s.p
            w1x = R_s.alloc("w1x", [8, 704], BF16)
            ntm3 = norm_tmps(R_s)
            C.dma('pool', w1x, I["w_in1x"].rearrange("(kc p) f -> p kc f", p=128), w1x.buf, writes=[w1x.buf])
            def nrmF(t):
                norm_mod(xT_t[t], 8, 512, D, lambda c: MOD(1, cj, 'A_mix', c), lambda c: MOD(1, cj, 'B_mix', c),
                         lambda c, t=t: hT_t[t][:, c, :], ntm3)
            nrmF(0)
            sq3 = R_s.view(ntm3['sq'][0], [3, 512], BF16)
            sd3 = ntm3['sd'][0]
            if S:
                ropeT = R_ffn.alloc("ropeT", [2, TS], F32)
                rp = Buf("ropeld")
                C.dma('sp', ropeT[0:32], I["rope"], rp, writes=[ropeT.buf])
                C.dma('sp', ropeT[64:96], I["rope"], rp, writes=[ropeT.buf])
                C.dma('sp', ropeT[96:128], I["rope"], rp, writes=[ropeT.buf])
                kvx = R_ffn.alloc("kvx", [2, TS], BF16)
                kpx = R_ffn.alloc("kpx", [TS], BF16)
                rt = [R_ffn.alloc("rt%d" % i, [512], F32) for i in range(2)]
            else:
                gkvt = R_ffn.alloc("gkvt", [256], F32)
                C.dma('sp', gkvt, I["gkv_row"].partition_broadcast(128), gkvt.buf, writes=[gkvt.buf])
                ost = [R_ffn.alloc("ost%d" % i, [288], F32) for i in range(2)]
                osq = R_ffn.alloc("osq", [256], F32)
                oss = R_ffn.alloc("oss", [2], F32)

            def lat_norm(col0, nch, dim, goff, out_fn, t):
                bs = []
                for c in range(nch):
                    bi, bk = getbank(hold=True); bs.append((bi, bk))
                    for kc in range(KC):
                        C.mm(bk, w1x[:, kc, col0 + c * 128:col0 + (c + 1) * 128], hT_t[t][:, kc, :], kc == 0, kc == KC - 1)
                    C.act(sq3[:, c, :], bk, AF.Square)
                bi2, bk2 = getbank()
                for c in range(nch):
                    C.mm(bk2, ones_b, sq3[:, c, :], c == 0, c == nch - 1)
                C.act(sd3, bk2, AF.Ln, bias=EPS, scale=1.0 / dim)
                C.act(sd3, sd3, AF.Exp, scale=-0.5)
                for c in range(nch):
                    C.stt(out_fn(c), bs[c][1], vcol(goff, c), sd3, ALU.mult, ALU.mult)
                    release(bs[c][0])

            for t in range(NT):
                if t + 1 < NT:
                    nrmF(t + 1)
                if S:
                    lat_norm(384, 2, 256, V_GKV, lambda c, t=t: kvx[:, c, tsl(t)], t)
                else:
                    lat_norm(384, 2, 256, V_GKV, lambda c, t=t: ckvn[:, c, tsl(t)], t)
                bi, bk = getbank()
                for kc in range(KC):
                    C.mm(bk[0:32, :], w1x[:, kc, 640:672], hT_t[t][:, kc, :], kc == 0, kc == KC - 1)
                if S:
                    bi2, bk2 = getbank()
                    for kc in range(KC):
                        C.mm(bk2[0:32, :], w1x[:, kc, 672:704], hT_t[t][:, kc, :], kc == 0, kc == KC - 1)
                    C.tt(rt[0][0:32, :], bk[0:32, :], ropeT[0:32, 0, tsl(t)], ALU.mult)
                    C.tt(rt[1][0:32, :], bk2[0:32, :], ropeT[0:32, 1, tsl(t)], ALU.mult)
                    C.tt(kpx[0:32, tsl(t)], rt[0][0:32, :], rt[1][0:32, :], ALU.add)
                else:
                    C.copy(Kb[64:96, tsl(t)], bk[0:32, :], eng='act')
            if not S:
                for j in range(Tn // 128):
                    t = j // 4
                    bi, bk = getbank()
                    for kc in range(KC):
                        C.mm(bk[:, 0:288], hT_t[t][:, kc, (j % 4) * 128:(j % 4 + 1) * 128], w1x[:, kc, 384:672], kc == 0, kc == KC - 1)
                    os_ = ost[j % 2]
                    C.act(osq, bk[:, 0:256], AF.Square)
                    C.op('dve', lambda e: e.reduce_sum(out=oss.ap[:, 0:1], in_=osq.ap, axis=AX.X), reads=[osq.buf], writes=[oss.buf])
                    C.act(oss[:, 1:2], oss[:, 0:1], AF.Sqrt, bias=EPS, scale=1.0 / 256)
                    C.recip(oss[:, 1:2], oss[:, 1:2])
                    C.stt(os_[:, 0:256], bk[:, 0:256], oss[:, 1:2], gkvt, ALU.mult, ALU.mult)
                    C.copy(os_[:, 256:288], bk[:, 256:288], eng='act')
                    C.dma('sp', O["o_ckv"][j * 128:(j + 1) * 128, :], os_[:, 0:256], os_.buf, reads=[os_.buf])
                    C.dma('sp', O["o_kpe"][j * 128:(j + 1) * 128, :], os_[:, 256:288], os_.buf, reads=[os_.buf])
            else:
                cst = R_s.view(ntm3['sq'][0], [4, 288], F32)
                C.dma('sp', cst[:, :, 0:256], I["ctx_ckv"].rearrange("(j p) f -> p j f", p=128), cst.buf, writes=[cst.buf])
                C.dma('sp', cst[:, :, 256:288], I["ctx_kpe"].rearrange("(j p) f -> p j f", p=128), cst.buf, writes=[cst.buf])
                for c in range(2):
                    bi, bk = getbank()
                    for j in range(4):
                        C.transpose(bk[:, j * 128:(j + 1) * 128], cst[:, j, c * 128:(c + 1) * 128], ident)
                    C.copy(ckvn[:, c, 2 * TS:2 * TS + 512], bk)
                bi, bk = getbank()
                for j in range(4):
                    C.transpose(bk[0:32, j * 128:(j + 1) * 128], cst[:, j, 256:288], ident)
                C.copy(Kb[64:96, 2 * TS:2 * TS + 512], bk[0:32, :], eng='act')
                cin = Buf("cckvin"); cout = Buf("cckvout")
                for c in range(2):
                    C.dma('pool', cc_kv_in[c * 128:(c + 1) * 128, :], kvx[:, c, :], cin, reads=[kvx.buf], writes=[cin])
                C.dma('pool', cc_kv_in[256:288, :], kpx[0:32, :], cin, reads=[kpx.buf], writes=[cin])
                C.collective(cc_kv_in, cc_kv_out, cin, cout)
                kl = Buf("kvload")
                for r in range(2):
                    for c in range(2):
                        C.dma('sp', ckvn[:, c, r * TS:(r + 1) * TS], cc_kv_out[r * 288 + c * 128:r * 288 + (c + 1) * 128, :], kl,
                              reads=[cout], writes=[ckvn.buf])
                    C.dma('sp', Kb[64:96, r * TS:(r + 1) * TS], cc_kv_out[r * 288 + 256:r * 288 + 288, :], kl, reads=[cout], writes=[Kb.buf])
            for t in range(NT):
                lat_norm(0, 3, 384, V_GQ, lambda c, t=t: cqn[:, c, tsl(t)], t)
            C.barrier()

            R_s.p = keep_s
            R_h.reset()
            if S:
                R_ffn.p = 16 * 1024
            else:
                R_ffn.reset()
            wq = R_s.alloc("wq", [3, 2048], BF16)
            wkv = R_s.alloc("wkv", [2, 2048], BF16)
            wo1s = [R_s.alloc("wo1s%d" % i, [D], BF16) for i in range(2)]
            C.dma('pool', wq, I["w_qbx"].rearrange("(kc p) f -> p kc f", p=128), wq.buf, writes=[wq.buf])
            C.dma('pool', wkv, I["w_kvb"].rearrange("(kc p) f -> p kc f", p=128), wkv.buf, writes=[wkv.buf])
            wo1src = I["w_out1"].rearrange("(kc p) f -> p kc f", p=128)
            NKT = NK // 128
            Vbs = [R_h.alloc("Vb%d" % i, [NKT, 65], BF16) for i in range(2)]
            Kbs = [Kb, R_ffn.alloc("Kb2", [NK], BF16)]
            qbs = [R_h.alloc("qb", [Tn], BF16), R_ffn.alloc("qb2", [Tn], BF16)]
            NPT = 5
            PT = [R_h.alloc("PT%d" % i, [512], BF16) for i in range(NPT)]
            opair = [R_h.alloc("opair%d" % i, [Tn], BF16) for i in range(2)]
            rdens = [R_s.alloc("rden", [512], BF16), R_ffn.alloc("rden2", [512], BF16)]
            rdc = [0]; delayed = []; FDELAY = 10 if S else 2
            lnt = None if S else R_h.alloc("lnt", [512], F32)
            opall = None if S else R_h.alloc("opall", [8, Tn], BF16)
            wo1f = None if S else R_s.alloc("wo1f", [8, D], BF16)
            bcs = R_h.alloc("bcs", [512], F32)
            otmp = R_s.alloc("otmp", [512], BF16)
            qr = [R_h.alloc("qr%d" % i, [512], F32) for i in range(2)] if S else None
            for vb in Vbs:
                C.memset(vb[:, :, 64:65], 1.0)
            C.copy(Kbs[1][64:96, :], Kb[64:96, :], eng='act')
            if S:
                jobs = [(0, TS, 0, NK)]
            else:
                jobs = [(s * 256, 256, s * 256, 256) for s in range(4)]
            ptc = [0]
            LOOK = 3

            def v_unit(h, kt):
                bi, bk = getbank()
                for kc in range(2):
                    C.mm(bk[:, 0:64], ckvn[:, kc, kt * 128:(kt + 1) * 128], wkv[:, kc, h * 128 + 64:h * 128 + 128], kc == 0, kc == 1)
                C.copy(Vbs[h % 2][:, kt, 0:64], bk[:, 0:64], eng='dve')

            def k_unit(h, kb0):
                bi, bk = getbank()
                for kc in range(2):
                    C.mm(bk[0:64, :], wkv[:, kc, h * 128:h * 128 + 64], ckvn[:, kc, kb0:kb0 + 512], kc == 0, kc == 1)
                C.copy(Kbs[h % 2][0:64, kb0:kb0 + 512], bk[0:64, :])

            def q_unit(h, t):
                qb = qbs[h % 2]
                bi, bk = getbank()
                if S:
                    for kc in range(3):
                        C.mm(bk, wq[:, kc, h * 128:(h + 1) * 128], cqn[:, kc, tsl(t)], kc == 0, kc == 2)
                    C.copy(qb[0:64, tsl(t)], bk[0:64, :], eng='dve')
                    C.tt(qr[0][64:96, :], bk[64:96, :], ropeT[64:96, 0, tsl(t)], ALU.mult)
                    C.tt(qr[1][64:96, :], bk[96:128, :], ropeT[96:128, 1, tsl(t)], ALU.mult)
                    C.tt(qb[64:96, tsl(t)], qr[0][64:96, :], qr[1][64:96, :], ALU.add)
                else:
                    for kc in range(3):
                        C.mm(bk[0:96, :], wq[:, kc, h * 128:h * 128 + 96], cqn[:, kc, tsl(t)], kc == 0, kc == 2)
                    C.copy(qb[0:96, tsl(t)], bk[0:96, :], eng='dve')

            def o_unit(pr, t, oc):
                wo1 = wo1s[pr % 2]
                if t == 0 and oc == 0:
                    C.dma('pool', wo1, wo1src[:, pr, :], wo1.buf, writes=[wo1.buf])
                bi, bk = getbank()
                C.mm(bk, wo1[:, oc * 128:(oc + 1) * 128], opair[pr % 2][:, tsl(t)], True, True)
                C.stt(xT_t[t][:, oc, :], bk, MOD(1, cj, 'G_mix', oc), xT_t[t][:, oc, :], ALU.mult, ALU.add)

            def prep_units(h):
                us = [(lambda kb0=kb0: k_unit(h, kb0)) for kb0 in range(0, NK, 512)]
                us += [(lambda t=t: q_unit(h, t)) for t in range(NT)]
                vs = [(lambda kt=kt: v_unit(h, kt)) for kt in range(NKT)]
                out = []
                step = max(1, len(vs) // max(1, len(us)))
                vi = 0
                for u in us:
                    out.append(u)
                    out.extend(vs[vi:vi + step]); vi += step
                out.extend(vs[vi:])
                return out

            side = []

            def attend(h):
                Kh, qb, Vh = Kbs[h % 2], qbs[h % 2], Vbs[h % 2]
                op_ = opair[(h // 2) % 2] if S else opall[:, h // 2, :]
                its = []
                for (q0, nqt, k0, nk) in jobs:
                    for qq in range(q0, q0 + nqt, 512):
                        nq = min(512, q0 + nqt - qq)
                        for ki in range(nk // 128):
                            its.append((qq, nq, k0 + ki * 128, ki, nk // 128))
                cur = {}
                pend = []

                def fin2(qq, nq, oi, ob, rd, last):
                    bi, bk = getbank()
                    C.mm(bk[0:64, 0:nq], ones_b[64:65, 0:64], rd[64:65, 0:nq], True, True)
                    C.copy(bcs[0:64, 0:nq], bk[0:64, 0:nq], eng='dve')
                    C.tt(op_[(h % 2) * 64:(h % 2) * 64 + 64, qq:qq + nq], ob[0:64, 0:nq], bcs[0:64, 0:nq], ALU.mult)
                    release(oi)
                    if last and h % 2 == 1 and S:
                        pr = h // 2
                        side[0:0] = [(lambda t=t, oc=oc, pr=pr: o_unit(pr, t, oc)) for t in range(NT) for oc in range(8)]

                def flush_one():
                    pt, (qq, nq, ks, ki, nkt) = pend.pop(0)
                    oi, ob = cur[qq]
                    C.mm(ob[0:65, 0:nq], Vh[:, ks // 128, 0:65], pt[:, 0:nq], ki == 0, ki == nkt - 1, signal=True)
                    if ki == nkt - 1:
                        rd = rdens[rdc[0] % 2]; rdc[0] += 1
                        if S:
                            def rq(k, rd=rd, ob=ob):
                                with nc.allow_low_precision("softmax 1/den is a bf16 matmul operand"):
                                    C.recip(rd[64:65, k * 128:(k + 1) * 128], ob[64:65, k * 128:(k + 1) * 128])
                            rq(0)
                            for k in range(1, 4):
                                delayed.append([k, (lambda k=k, rq=rq: rq(k))])
                        else:
                            C.act(rd[64:65, 0:nq], ob[64:65, 0:nq], AF.Ln)
                            C.act(rd[64:65, 0:nq], rd[64:65, 0:nq], AF.Exp, scale=-1.0)
                        last = (pend == [] and ii_box[0] == n_it - 1)
                        delayed.append([FDELAY, (lambda qq=qq, nq=nq, oi=oi, ob=ob, rd=rd, last=last: fin2(qq, nq, oi, ob, rd, last))])

                def tick():
                    for dl in delayed:
                        dl[0] -= 1
                    ready = [dl for dl in delayed if dl[0] <= 0]
                    for dl in ready:
                        delayed.remove(dl)
                        dl[1]()

                n_it = len(its)
                ii_box = [0]
                sacc = [0.0]

                def pop_side(frac_left):
                    if side:
                        sacc[0] += len(side) / float(max(1, frac_left))
                        while sacc[0] >= 1.0 and side:
                            sacc[0] -= 1.0
                            side.pop(0)()

                if not S:
                    steps = 6
                    stp = 0
                    pts_all = []
                    for sp in range(2):
                        pts = []
                        for ki in range(2):
                            bi, bk = getbank()
                            for jb in range(2):
                                sq_ = 2 * sp + jb
                                ks = sq_ * 256 + ki * 128
                                C.mm(bk[:, jb * 256:(jb + 1) * 256], Kh[0:96, ks:ks + 128], qb[0:96, sq_ * 256:(sq_ + 1) * 256], True, True)
                            pt = PT[ptc[0] % NPT]; ptc[0] += 1
                            C.act(pt, bk, AF.Exp, scale=SCALE)
                            pts.append(pt)
                            tick(); pop_side(steps - stp); stp += 1
                        pts_all.append(pts)
                    for sp in range(2):
                        oi, ob = getbank(hold=True)
                        for jb in range(2):
                            sq_ = 2 * sp + jb
                            for ki in range(2):
                                ks = sq_ * 256 + ki * 128
                                C.mm(ob[0:65, jb * 256:(jb + 1) * 256], Vh[:, ks // 128, 0:65], pts_all[sp][ki][:, jb * 256:(jb + 1) * 256],
                                     ki == 0, ki == 1, signal=True)
                        rd = rdens[rdc[0] % 2]; rdc[0] += 1
                        C.act(lnt[64:65, :], ob[64:65, :], AF.Ln)
                        with nc.allow_low_precision("softmax 1/den is a bf16 matmul operand"):
                            C.act(rd[64:65, :], lnt[64:65, :], AF.Exp, scale=-1.0)
                        delayed.append([FDELAY, (lambda qq=sp * 512, oi=oi, ob=ob, rd=rd, last=(sp == 1): fin2(qq, 512, oi, ob, rd, last))])
                        tick(); pop_side(steps - stp); stp += 1
                    while side:
                        side.pop(0)()
                    return
                for ii, it in enumerate(its):
                    ii_box[0] = ii
                    qq, nq, ks, ki, nkt = it
                    if ki == 0:
                        cur[qq] = getbank(hold=True)
                    bi, bk = getbank()
                    C.mm(bk[:, 0:nq], Kh[0:96, ks:ks + 128], qb[0:96, qq:qq + nq], True, True)
                    pt = PT[ptc[0] % NPT]; ptc[0] += 1
                    C.act(pt[:, 0:nq], bk[:, 0:nq], AF.Exp, scale=SCALE)
                    pend.append((pt, it))
                    if h == 0:
                        for u_ in jit0.pop(ii, []):
                            u_()
                    if len(pend) > LOOK:
                        flush_one()
                    tick()
                    if side:
                        sacc[0] += len(side) / float(n_it - ii)
                        while sacc[0] >= 1.0 and side:
                            sacc[0] -= 1.0
                            side.pop(0)()
                while pend:
                    flush_one()
                while side:
                    side.pop(0)()

            jit0 = {}
            if S:
                for kb0 in range(0, NK, 512):
                    k_unit(0, kb0)
                q_unit(0, 0)
                for kt in range(NKT):
                    jit0.setdefault(kt, []).append(lambda kt=kt: v_unit(0, kt))
                for t in range(1, NT):
                    jit0.setdefault(t, []).append(lambda t=t: q_unit(0, t))
            else:
                for u in prep_units(0):
                    u()
            for h in range(16):
                if h + 1 < 16:
                    side.extend(prep_units(h + 1))
                attend(h)
            while delayed:
                delayed.pop(0)[1]()
            while side:
                side.pop(0)()
            if not S:
                for hlf in range(2):
                    C.dma('pool', wo1f[:, hlf * 4:hlf * 4 + 4, :], wo1src[:, hlf * 4:hlf * 4 + 4, :], wo1f.buf, writes=[wo1f.buf])
                for t in range(NT):
                    outproj_acc(lambda kc, oc: wo1f[:, kc, oc * 128:(oc + 1) * 128], lambda kc, t=t: opall[:, kc, tsl(t)], 8, t, 'G_mix', 1)
            C.barrier()

            R_h.reset()
            hT2 = R_h.alloc("hT", [8, TH], BF16)
            for t in range(NT):
                hT_t[t] = T(hT2.ap[:, :, t * 512:(t + 1) * 512], Buf("hTb%d" % t))
            ffn(1)

            R_s.reset()
            ntm4 = norm_tmps(R_s, 2)
            yf = [R_s.alloc("yf%d" % i, [8, 512], F32) for i in range(2)]
            ytm = [R_s.alloc("ytm%d" % i, [D], F32) for i in range(2)]

            def fin_store(t):
                y_ = yf[t % 2]
                norm_mod(xT_t[t], 8, 512, D, lambda c: vcol(V_GFIN, c), None, lambda c, y_=y_: y_[:, c, :], ntm4, t, stats=False)
                for jj in range(4):
                    j = t * 4 + jj
                    yt = ytm[j % 2]
                    for half in range(2):
                        bi, bk = getbank()
                        for q in range(4):
                            c = half * 4 + q
                            C.transpose(bk[:, q * 128:(q + 1) * 128], y_[:, c, jj * 128:(jj + 1) * 128], ident)
                        C.copy(yt[:, half * 512:(half + 1) * 512], bk, eng='act' if j % 2 == 0 else 'dve')
                    C.dma('sp', yd[j * 128:(j + 1) * 128, :], yt, yt.buf, reads=[yt.buf])

            pipeline(NT, lambda t: norm_stats(xT_t[t], 8, 512, D, ntm4, t), fin_store)
            C.barrier()

        run_wave('P')
        C.recycle()
        run_wave('S')
        C.finish()
    return nc


def _host_consts():
    inv = (10000.0 ** (-np.arange(0, 16, 2, dtype=np.float32) / np.float32(16))).astype(np.float32)
    return inv


def kernel(x_prompt, x_sample, state_l0_lru, cache_l1_ckv, cache_l1_kpe, c, c_ctx,
           l0_w_mod, l0_b_mod, l0_g_mix, l0_g_ffn, l0_w_in, l0_conv_w, l0_conv_b,
           l0_lru_w_a, l0_lru_b_a, l0_lru_w_i, l0_lru_b_i, l0_lru_lam, l0_pool_w, l0_pool_scale,
           l0_w_out, l0_ffn_w1, l0_ffn_w2,
           l1_w_mod, l1_b_mod, l1_g_mix, l1_g_ffn, l1_w_in, l1_g_q, l1_w_qb, l1_g_kv, l1_w_kvb,
           l1_w_out, l1_ffn_w1, l1_ffn_w2, g_final, _debug=False):
    f = lambda a: np.ascontiguousarray(np.asarray(a, dtype=np.float32))
    x_prompt, x_sample = f(x_prompt), f(x_sample)

    def cols(v, n):
        return f(v).reshape(n, 128).T

    vecs = np.zeros((128, NV), np.float32)
    vecs[:, V_GMIX0:V_GMIX0 + 8] = cols(l0_g_mix, 8); vecs[:, V_GFFN0:V_GFFN0 + 8] = cols(l0_g_ffn, 8)
    vecs[:, V_GMIX1:V_GMIX1 + 8] = cols(l1_g_mix, 8); vecs[:, V_GFFN1:V_GFFN1 + 8] = cols(l1_g_ffn, 8)
    vecs[:, V_GFIN:V_GFIN + 8] = cols(g_final, 8)
    cw = f(l0_conv_w)
    for ch in range(4):
        for k in range(4):
            vecs[:, V_CONVW + ch * 4 + k] = cw[k, ch * 128:(ch + 1) * 128]
    vecs[:, V_CONVB:V_CONVB + 4] = cols(l0_conv_b, 4)
    for d in range(2):
        vecs[:, V_BA + d * 4:V_BA + d * 4 + 4] = cols(f(l0_lru_b_a)[d], 4)
        vecs[:, V_BI + d * 4:V_BI + d * 4 + 4] = cols(f(l0_lru_b_i)[d], 4)
        vecs[:, V_LAM + d * 4:V_LAM + d * 4 + 4] = cols(f(l0_lru_lam)[d], 4)
    vecs[:, V_SPOOL:V_SPOOL + 4] = cols(l0_pool_scale, 4)
    vecs[:, V_GQ:V_GQ + 3] = cols(l1_g_q, 3); vecs[:, V_GKV:V_GKV + 2] = cols(l1_g_kv, 2)
    bmods = (np.ascontiguousarray(cols(l0_b_mod, 48)), np.ascontiguousarray(cols(l1_b_mod, 48)))
    wmods = (f(l0_w_mod), f(l1_w_mod))
    wbd = np.zeros((128, 16, 128), np.float32)
    wa, wi = f(l0_lru_w_a), f(l0_lru_w_i)
    for d in range(2):
        for gi, wsrc in enumerate((wa, wi)):
            for ch in range(4):
                for j in range(2):
                    wbd[j * 64:(j + 1) * 64, (d * 2 + gi) * 4 + ch, j * 64:(j + 1) * 64] = wsrc[d, 2 * ch + j]
    pool_w = np.ascontiguousarray(f(l0_pool_w).transpose(1, 0, 2))
    perm = np.concatenate([np.arange(8, 16), np.arange(0, 8), np.arange(24, 32), np.arange(16, 24)])
    w_in1 = f(l1_w_in)
    w_in1x = np.ascontiguousarray(np.concatenate([w_in1, w_in1[:, 640 + perm]], axis=1))
    w_qb = f(l1_w_qb)
    qcols = np.concatenate([np.concatenate([np.arange(h * 96, (h + 1) * 96), h * 96 + 64 + perm]) for h in range(16)])
    w_qbx = np.ascontiguousarray(w_qb[:, qcols])
    shared = {"vecs": vecs,
              "w_in0": f(l0_w_in), "wbd": wbd, "pool_w": pool_w, "w_out0": f(l0_w_out),
              "ffn_w1_0": f(l0_ffn_w1), "ffn_w2_0": f(l0_ffn_w2), "w_in1x": w_in1x, "w_qbx": w_qbx,
              "w_kvb": f(l1_w_kvb), "w_out1": f(l1_w_out), "ffn_w1_1": f(l1_ffn_w1), "ffn_w2_1": f(l1_ffn_w2),
              "gkv_row": f(l1_g_kv), "ident": np.eye(128, dtype=np.float32)}
    inv = _host_consts()

    def invc(S_, start, n):
        tpos = np.arange(start, start + n)
        out = np.zeros((4, n), np.float32)
        for gi, w in enumerate((2, 4, 8, 16)):
            left = w // 2; right = w - 1 - left
            lo = np.maximum(tpos - left, 0); hi = np.minimum(tpos + right, S_ - 1) + 1
            out[gi] = (1.0 / (hi - lo).astype(np.float32)).astype(np.float32)
        return out

    invc_p = np.concatenate([invc(256, 0, 256)] * 4, axis=1)
    cs, cc_ = f(c), f(c_ctx)
    st = f(state_l0_lru); ckv_c = f(cache_l1_ckv); kpe_c = f(cache_l1_kpe)
    in_maps = []
    for core in range(8):
        sb, half = core // 2, core % 2
        start = half * TS
        xw = np.zeros((TS + 16, D), np.float32)
        lo, hi = start - 8, start + TS + 8
        slo, shi = max(lo, 0), min(hi, 4096)
        xw[slo - lo:shi - lo] = x_sample[sb, slo:shi]
        hm = np.zeros(16, np.float32)
        hm[0:8] = 1.0 if half == 1 else 0.0
        hm[8:16] = 1.0 if half == 0 else 0.0
        pos = np.arange(start, start + TS)
        row = (pos // 64).astype(np.float32); col = (pos % 64).astype(np.float32)
        rope = np.zeros((32, 2, TS), np.float32)
        for r in range(32):
            grp, j = r // 8, r % 8
            ang = ((row if grp < 2 else col) * inv[j]).astype(np.float32)
            rope[r, 0] = np.cos(ang); rope[r, 1] = np.sin(ang) * (-1.0 if grp % 2 == 0 else 1.0)
        cT = np.stack([cols(cc_, 8), cols(cs[sb], 8)], axis=2)
        st0 = np.stack([cols(st[sb, 0], 4), cols(st[sb, 1], 4)], axis=1)
        m = dict(shared)
        m.update({"x_p": np.ascontiguousarray(x_prompt[core * 4:(core + 1) * 4].reshape(TP, D)), "x_s": xw,
                  "cT": np.ascontiguousarray(cT), "st0": np.ascontiguousarray(st0),
                  "wmod_h": wmods[core % 2], "bmod_h": bmods[core % 2],
                  "ctx_ckv": np.ascontiguousarray(ckv_c[sb]), "ctx_kpe": np.ascontiguousarray(kpe_c[sb]),
                  "rope": rope, "invc_p": np.ascontiguousarray(invc_p), "invc_s": invc(4096, start, TS), "hmask": hm,
                  "sel": np.array([1.0, 0.0] if half == 0 else [0.0, 1.0], np.float32)})
        in_maps.append(m)
    nc = build_program()
    res = run_bass_kernel_spmd(nc, in_maps, core_ids=list(range(8)))
    R = res.results
    y_prompt = np.stack([R[k]["y_p"] for k in range(8)]).reshape(32, 256, D)
    y_sample = np.stack([R[k]["y_s"] for k in range(8)]).reshape(4, 4096, D)
    new_lru = np.stack([R[k]["o_lru"].reshape(4, 2, 512) for k in range(8)]).reshape(32, 2, 512)
    new_ckv = np.stack([R[k]["o_ckv"] for k in range(8)]).reshape(32, 256, 256)
    new_kpe = np.stack([R[k]["o_kpe"] for k in range(8)]).reshape(32, 256, 32)
    if _debug:
        return (y_prompt, y_sample, new_lru, new_ckv, new_kpe), R
    return (y_prompt.astype(np.float32), y_sample.astype(np.float32), new_lru.astype(np.float32),
            new_ckv.astype(np.float32), new_kpe.astype(np.float32))
```

```python
import contextlib
import numpy as np
import concourse.bass as bass
import concourse.mybir as mybir
from concourse.bass_utils import run_bass_kernel_spmd

F32 = mybir.dt.float32
BF16 = mybir.dt.bfloat16
AF = mybir.ActivationFunctionType
ALU = mybir.AluOpType
AX = mybir.AxisListType

D = 1024
KC = 8
EPS = 1e-6
TP, TS, HALO = 1024, 2048, 8
NV = 93
V_GMIX0, V_GFFN0, V_GMIX1, V_GFFN1, V_GFIN, V_CONVW, V_CONVB, V_BA, V_BI, V_LAM, V_SPOOL, V_GQ, V_GKV = \
    0, 8, 16, 24, 32, 40, 56, 60, 68, 76, 84, 88, 91
SCALE = 96.0 ** -0.5
PAIRS = [[0, 1], [2, 3], [4, 5], [6, 7]]


class Sem:
    def __init__(self, h):
        self.h = h; self.cnt = 0


class Buf:
    def __init__(self, name, persistent=False):
        self.name = name; self.w = None; self.r = {}; self.sem = None; self.persistent = persistent


class T:
    def __init__(self, ap, buf):
        self.ap = ap; self.bufs = list(buf) if isinstance(buf, (list, tuple)) else [buf]

    @property
    def buf(self):
        return self.bufs[0]

    @buf.setter
    def buf(self, b):
        self.bufs = [b]

    def __getitem__(self, k):
        return T(self.ap[k], self.bufs)

    def v(self, f):
        return T(f(self.ap), self.bufs)


def _bufs(*ts):
    out = []
    for t in ts:
        if isinstance(t, T): out.extend(t.bufs)
    return out


def _ap(x):
    return x.ap if isinstance(x, T) else x


class Ctx:
    def __init__(self, nc, es):
        self.nc = nc; self.es = es
        self.eng = {'pe': nc.tensor, 'act': nc.scalar, 'dve': nc.vector, 'pool': nc.gpsimd, 'sp': nc.sync}
        self.esem = {e: es.enter_context(nc.semaphore("s_" + e)) for e in ('pe', 'act', 'dve', 'pool')}
        self.cnt = {e: 0 for e in self.esem}
        self.seen = {e: {} for e in self.eng}
        self.sems = []
        self.free = []
        self.inuse = []
        self.dumps = []
        self.debug = False

    def getsem(self):
        if self.free:
            sm = self.free.pop()
        else:
            sm = Sem(self.es.enter_context(self.nc.semaphore("d%d" % len(self.sems)))); self.sems.append(sm)
        self.inuse.append(sm)
        return sm

    def recycle(self):
        self.free.extend(self.inuse); self.inuse = []

    def _sid(self, sem):
        return id(sem)

    def _wait(self, e, stamp):
        sem, val = stamp
        d = self.seen[e]; k = self._sid(sem)
        if d.get(k, 0) < val:
            self.eng[e].wait_ge(sem, val); d[k] = val

    def deps(self, e, reads, writes):
        st = []
        own = self.esem.get(e)
        for b in reads:
            if b.w: st.append(b.w)
        for b in writes:
            if b.w and b.w[0] is not own: st.append(b.w)
            st.extend(v for v in b.r.values() if v[0] is not own)
        pes = self.esem['pe']
        for s in st:
            if e == 'pe' and s[0] is pes:
                continue
            self._wait(e, s)

    def _stamp(self, stamp, reads, writes):
        k = self._sid(stamp[0])
        for b in writes:
            b.w = stamp; b.r = {}
        for b in reads:
            b.r[k] = stamp

    def op(self, e, fn, reads=(), writes=(), signal=True):
        self.deps(e, reads, writes)
        ins = fn(self.eng[e])
        sem = self.esem[e]
        if signal:
            self.cnt[e] += 1; ins.then_inc(sem, 1); stamp = (sem, self.cnt[e])
        else:
            stamp = (sem, self.cnt[e] + 1)
        self._stamp(stamp, reads, writes)
        return ins

    def dma(self, q, out, in_, sb, reads=(), writes=()):
        self.deps(q, reads, writes)
        if sb.sem is None:
            sb.sem = self.getsem()
        ins = self.eng[q].dma_start(out=_ap(out), in_=_ap(in_))
        sb.sem.cnt += 16; ins.then_inc(sb.sem.h, 16)
        self._stamp((sb.sem.h, sb.sem.cnt), reads, writes)

    def collective(self, ins_ap, outs_ap, inbuf, outbuf):
        self.deps('pool', [inbuf], [outbuf])
        sm = Sem(self.es.enter_context(self.nc.semaphore("cc%d" % len(self.sems)))); self.sems.append(sm)
        self.eng['pool'].collective_compute("AllGather", ALU.bypass, replica_groups=PAIRS, ins=[ins_ap],
                                            outs=[outs_ap]).then_inc(sm.h, 1)
        sm.cnt = 1
        self._stamp((sm.h, 1), [inbuf], [outbuf])

    def dump(self, name, t):
        if not self.debug: return
        shp = list(t.ap.shape)
        d = self.nc.dram_tensor("dbg_" + name, shp, t.ap.dtype, kind="ExternalOutput").ap()
        b = Buf("dbg_" + name)
        self.dma('sp', d, t, b, reads=[t.buf])
        self.dumps.append("dbg_" + name)

    def barrier(self):
        for e in ('pe', 'act', 'dve', 'pool', 'sp'):
            for f in self.esem:
                if self.cnt[f]: self._wait(e, (self.esem[f], self.cnt[f]))
            for sm in self.sems:
                if sm.cnt: self._wait(e, (sm.h, sm.cnt))

    def finish(self):
        for sm in self.sems:
            if sm.cnt: self._wait('sp', (sm.h, sm.cnt))
        for f in self.esem:
            if self.cnt[f]: self._wait('sp', (self.esem[f], self.cnt[f]))

    def act(self, out, in_, func, bias=0.0, scale=1.0):
        self.op('act', lambda e: e.activation(out=out.ap, in_=in_.ap, func=func, bias=_ap(bias), scale=_ap(scale)),
                reads=_bufs(in_, bias, scale), writes=out.bufs)

    def tt(self, out, a, b, op):
        self.op('dve', lambda e: e.tensor_tensor(out=out.ap, in0=a.ap, in1=b.ap, op=op), reads=_bufs(a, b), writes=out.bufs)

    def stt(self, out, a, s, b, op0, op1):
        self.op('dve', lambda e: e.scalar_tensor_tensor(out=out.ap, in0=a.ap, scalar=_ap(s), in1=b.ap, op0=op0, op1=op1),
                reads=_bufs(a, s, b), writes=out.bufs)

    def ts(self, out, a, s1, s2, op0, op1=None):
        if op1 is None:
            self.op('dve', lambda e: e.tensor_scalar(out=out.ap, in0=a.ap, scalar1=_ap(s1), scalar2=None, op0=op0),
                    reads=_bufs(a, s1), writes=out.bufs)
        else:
            self.op('dve', lambda e: e.tensor_scalar(out=out.ap, in0=a.ap, scalar1=_ap(s1), scalar2=_ap(s2), op0=op0, op1=op1),
                    reads=_bufs(a, s1, s2), writes=out.bufs)

    def copy(self, out, in_, eng='dve'):
        if eng == 'act':
            self.act(out, in_, AF.Copy)
        else:
            self.op(eng, lambda e: e.tensor_copy(out=out.ap, in_=in_.ap), reads=in_.bufs, writes=out.bufs)

    def recip(self, out, in_):
        self.op('dve', lambda e: e.reciprocal(out=out.ap, in_=in_.ap), reads=in_.bufs, writes=out.bufs)

    def memset(self, out, val, eng='dve'):
        self.op(eng, lambda e: e.memset(out.ap, val), writes=out.bufs)

    def mm(self, out, lhsT, rhs, start, stop, signal=None, tile_position=None):
        if signal is None: signal = stop
        kw = {}
        if tile_position is not None: kw['tile_position'] = tile_position
        self.op('pe', lambda e: e.matmul(out.ap, lhsT=lhsT.ap, rhs=rhs.ap, start=start, stop=stop, **kw),
                reads=_bufs(lhsT, rhs), writes=out.bufs, signal=signal)

    def transpose(self, out, in_, ident):
        self.op('pe', lambda e: e.transpose(out=out.ap, in_=in_.ap, identity=ident.ap), reads=_bufs(in_, ident), writes=out.bufs)

    def scan(self, out, a, u, init):
        self.op('dve', lambda e: e.tensor_tensor_scan(out=out.ap, data0=a.ap, data1=u.ap, initial=_ap(init), op0=ALU.mult, op1=ALU.add),
                reads=_bufs(a, u, init), writes=out.bufs)


class Region:
    def __init__(self, arena, base, size):
        self.arena = arena; self.base = base; self.size = size; self.p = 0

    def reset(self):
        self.p = 0

    def alloc(self, name, shape, dtype, persistent=False):
        n = int(np.prod(shape)); b = n * (2 if dtype == BF16 else 4)
        b = (b + 31) // 32 * 32
        assert self.p + b <= self.size, ("region overflow", name, self.p, b, self.size)
        w0 = (self.base + self.p) // 4; self.p += b
        ap = self.arena[:, w0:w0 + b // 4]
        if dtype == BF16:
            ap = ap.bitcast(BF16)
        ap = ap[:, 0:n]
        if len(shape) == 2:
            ap = ap.rearrange("p (a b) -> p a b", a=shape[0])
        elif len(shape) == 3:
            ap = ap.rearrange("p (a b c) -> p a b c", a=shape[0], b=shape[1])
        t = T(ap, Buf(name, persistent)); t.w0 = w0; t.nb = b
        return t

    def view(self, t, shape, dtype):
        n = int(np.prod(shape)); b = n * (2 if dtype == BF16 else 4)
        assert b <= t.nb
        ap = self.arena[:, t.w0:t.w0 + t.nb // 4]
        if dtype == BF16:
            ap = ap.bitcast(BF16)
        ap = ap[:, 0:n]
        if len(shape) == 2:
            ap = ap.rearrange("p (a b) -> p a b", a=shape[0])
        elif len(shape) == 3:
            ap = ap.rearrange("p (a b c) -> p a b c", a=shape[0], b=shape[1])
        return T(ap, t.buf)


def build_program(debug=False):
    nc = bass.Bass("TRN2", target_bir_lowering=False)

    def din(name, shape, dt=F32):
        return nc.dram_tensor(name, list(shape), dt, kind="ExternalInput").ap()

    def dout(name, shape, dt=F32):
        return nc.dram_tensor(name, list(shape), dt, kind="ExternalOutput").ap()

    I = {}
    for nm, shp in [("wmod_h", (D, 6 * D)), ("bmod_h", (128, 48)), ("vecs", (128, NV)),
                    ("w_in0", (D, 1536)), ("wbd", (128, 16, 128)), ("pool_w", (128, 4, 128)), ("w_out0", (D, D)),
                    ("ffn_w1_0", (D, 4 * D)), ("ffn_w2_0", (4 * D, D)), ("w_in1x", (D, 704)), ("w_qbx", (384, 2048)),
                    ("w_kvb", (256, 2048)), ("w_out1", (D, D)), ("ffn_w1_1", (D, 4 * D)), ("ffn_w2_1", (4 * D, D)),
                    ("gkv_row", (256,)), ("ident", (128, 128)),
                    ("x_p", (TP, D)), ("x_s", (TS + 2 * HALO, D)), ("cT", (128, 8, 2)), ("st0", (128, 2, 4)),
                    ("ctx_ckv", (512, 256)), ("ctx_kpe", (512, 32)), ("rope", (32, 2, TS)),
                    ("invc_p", (4, TP)), ("invc_s", (4, TS)), ("hmask", (16,)), ("sel", (2,))]:
        I[nm] = din(nm, shp)
    O = {"y_p": dout("y_p", (TP, D)), "y_s": dout("y_s", (TS, D)), "o_lru": dout("o_lru", (32, 128)),
         "o_ckv": dout("o_ckv", (TP, 256)), "o_kpe": dout("o_kpe", (TP, 32))}
    cc_st_in = [nc.dram_tensor("cc_st_in%d" % c, [2, 128], F32).ap() for c in range(4)]
    cc_st_out = [nc.dram_tensor("cc_st_out%d" % c, [4, 128], F32).ap() for c in range(4)]
    cc_mod_in = nc.dram_tensor("cc_mod_in", [128, 96], F32).ap()
    cc_mod_out = nc.dram_tensor("cc_mod_out", [256, 96], F32).ap()
    cc_kv_in = nc.dram_tensor("cc_kv_in", [288, TS], BF16).ap()
    cc_kv_out = nc.dram_tensor("cc_kv_out", [576, TS], BF16).ap()

    es = contextlib.ExitStack()
    with es:
        C = Ctx(nc, es)
        ARENA = 207 * 1024
        arena = es.enter_context(nc.sbuf_tensor("arena", [128, ARENA // 4], F32))
        psum_t = es.enter_context(nc.psum_tensor("psum", [128, 8, 512], F32))
        banks = [T(psum_t[:, i, :], Buf("bank%d" % i)) for i in range(8)]
        held = set()
        rr = [0]

        def getbank(hold=False):
            for _ in range(8):
                i = rr[0]; rr[0] = (rr[0] + 1) % 8
                if i not in held:
                    if hold: held.add(i)
                    return i, banks[i]
            raise RuntimeError("no bank")

        def release(i):
            held.discard(i)

        R_const = Region(arena, 0, 9 * 1024)
        R_ffn = Region(arena, 9 * 1024, 32 * 1024)
        R_x = Region(arena, 41 * 1024, 64 * 1024)
        R_h = Region(arena, 105 * 1024, 33 * 1024 + 256)
        R_s = Region(arena, 105 * 1024 + 33 * 1024 + 256, ARENA - (105 * 1024 + 33 * 1024 + 256))

        cbuf = Buf("const")

        def calloc(name, shape, dt=F32, own=False):
            t = R_const.alloc(name, shape, dt)
            if not own: t.buf = cbuf
            return t

        vecs = calloc("vecs", [NV]); bmod = calloc("bmod", [48]); cT = calloc("cT", [8, 2]); st0 = calloc("st0", [2, 4])
        ident = calloc("ident", [128]); sel = calloc("sel", [2]); hmask = calloc("hmask", [16])
        ones_b = calloc("ones_b", [128], BF16); ones_f = calloc("ones_f", [64])
        wbd = calloc("wbd", [16, 128], BF16); poolw = calloc("poolw", [4, 128], BF16)
        ones_b.buf = Buf("ones_b"); ones_f.buf = Buf("ones_f")
        scT = calloc("scT", [8, 2], own=True); s8 = calloc("s8", [8], own=True); s16 = calloc("s16", [8], own=True)
        lt = calloc("lt", [8], own=True)
        modsb = [calloc("modsb%d" % l, [48, 2], own=True) for l in range(2)]
        modA = [calloc("modA%d" % l, [2, 2, 8], own=True) for l in range(2)]
        stcols = calloc("stcols", [32], own=True); gsel = calloc("gsel", [4, 4], own=True)
        sinit = calloc("sinit", [4, 2], own=True); sfin = calloc("sfin", [4, 2], own=True)
        lsem = Buf("cload")
        for t, src in [(vecs, I["vecs"]), (bmod, I["bmod_h"]), (cT, I["cT"]), (st0, I["st0"]), (ident, I["ident"]),
                       (sel, I["sel"].partition_broadcast(128)), (hmask, I["hmask"].partition_broadcast(128))]:
            C.dma('sp', t, src, lsem, writes=[])
        C.dma('pool', wbd, I["wbd"], lsem, writes=[])
        C.dma('pool', poolw, I["pool_w"], lsem, writes=[])
        cbuf.w = (lsem.sem.h, lsem.sem.cnt)
        C.memset(ones_b, 1.0); C.memset(ones_f, 1.0)
        C.act(scT, cT, AF.Silu)
        C.act(lt, vecs[:, V_LAM:V_LAM + 8], AF.Exp, scale=-1.0)
        C.act(lt, lt, AF.Ln, bias=1.0)
        C.ts(s8, lt, -8.0, None, ALU.mult)
        C.ts(s16, lt, -16.0, None, ALU.mult)

        def vcol(off, i):
            return vecs[:, off + i:off + i + 1]

        R_s.reset()
        wm_slots = [R_s.alloc("wm%d" % i, [8, 1536], BF16) for i in range(2)]
        scTb = R_s.alloc("scTb", [8, 2], BF16)
        C.copy(scTb, scT)
        pcn = 0
        modl = R_s.alloc("modl", [48, 2], F32)
        wsrc = I["wmod_h"].rearrange("(kc p) f -> p kc f", p=128)
        bi, bk = getbank()
        for pc in range(4):
            sl = wm_slots[pcn % 2]; pcn += 1
            C.dma('pool', sl, wsrc[:, :, pc * 1536:(pc + 1) * 1536], sl.buf, writes=[sl.buf])
            for f in range(12):
                fc = pc * 12 + f
                for kc in range(KC):
                    C.mm(bk[:, fc * 2:fc * 2 + 2], sl[:, kc, f * 128:(f + 1) * 128], scTb[:, kc, :], kc == 0, kc == KC - 1)
        C.tt(modl, bk[:, 0:96].v(lambda a: a.rearrange("p (f j) -> p f j", j=2)),
             bmod.v(lambda a: a.unsqueeze(2).broadcast_to([128, 48, 2])), ALU.add)
        mi = Buf("ccmodin"); mo = Buf("ccmodout")
        C.dma('pool', cc_mod_in, modl.v(lambda a: a.rearrange("p f j -> p (f j)")), mi, reads=[modl.buf], writes=[mi])
        C.collective(cc_mod_in, cc_mod_out, mi, mo)
        for l in range(2):
            C.dma('sp', modsb[l].v(lambda a: a.rearrange("p f j -> p (f j)")), cc_mod_out[l * 128:(l + 1) * 128, :], modsb[l].buf,
                  reads=[mo], writes=[modsb[l].buf])
        for l in range(2):
            for j in range(2):
                for m, (sp_scale, goff) in enumerate([(1, V_GMIX0 if l == 0 else V_GMIX1), (4, V_GFFN0 if l == 0 else V_GFFN1)]):
                    C.ts(modA[l][:, j, m, :], modsb[l][:, sp_scale * 8:sp_scale * 8 + 8, j], 1.0, None, ALU.add)
                    C.tt(modA[l][:, j, m, :], modA[l][:, j, m, :], vecs[:, goff:goff + 8], ALU.mult)

        def MOD(l, j, which, c):
            if which == 'A_mix': return modA[l][:, j, 0, c:c + 1]
            if which == 'A_ffn': return modA[l][:, j, 1, c:c + 1]
            s = {'B_mix': 0, 'G_mix': 2, 'B_ffn': 3, 'G_ffn': 5}[which]
            return modsb[l][:, s * 8 + c:s * 8 + c + 1, j]

        C.barrier()

        def run_wave(kind):
            S = (kind == 'S')
            Tn = TS if S else TP
            nseq, L = (1, TS) if S else (4, 256)
            W = L + 16
            NT = Tn // 512
            cj = 1 if S else 0
            xd = I["x_s"] if S else I["x_p"]
            xoff = HALO if S else 0
            yd = O["y_s"] if S else O["y_p"]
            TH = Tn + (16 if S else 0)

            R_x.reset(); R_h.reset(); R_s.reset(); R_ffn.reset()
            hT = R_h.alloc("hT", [8, TH], BF16)
            hT_t = [T(hT.ap[:, :, t * 512:(t + 1) * 512], Buf("hT%d" % t)) for t in range(NT)]
            hT_h = T(hT.ap[:, :, Tn:TH], Buf("hTh")) if S else None

            def tsl(t):
                return slice(t * 512, (t + 1) * 512)

            def load_x_fm(region, dst_fn, j0, nj, stg):
                for j in range(j0, j0 + nj):
                    st = stg[j % len(stg)]
                    C.dma('sp', st, xd[xoff + j * 128: xoff + (j + 1) * 128, :], st.buf, writes=[st.buf])
                    dst = dst_fn(j)
                    for half in range(2):
                        bi, bk = getbank()
                        for q in range(4):
                            c = half * 4 + q
                            C.transpose(bk[:, q * 128:(q + 1) * 128], st[:, c * 128:(c + 1) * 128], ident)
                        C.copy(dst[:, half * 4:half * 4 + 4, :], bk.v(lambda a: a.rearrange("p (q t) -> p q t", q=4)),
                               eng='act' if j % 2 == 0 else 'dve')

            def norm_stats(src, nch, n, dim, tmp, i=0):
                sq = tmp['sq'][i % len(tmp['sq'])]
                C.act(sq[:, 0:nch, 0:n], src, AF.Square)
                bi, bk = getbank()
                for c in range(nch):
                    C.mm(bk[:, 0:n], ones_b, sq[:, c, 0:n], c == 0, c == nch - 1)
                sd = tmp['sd'][i % len(tmp['sd'])]
                C.act(sd[:, 0:n], bk[:, 0:n], AF.Ln, bias=EPS, scale=1.0 / dim)
                C.act(sd[:, 0:n], sd[:, 0:n], AF.Exp, scale=-0.5)

            def norm_mod(src, nch, n, dim, A, B, out, tmp, i=0, stats=True):
                if stats:
                    norm_stats(src, nch, n, dim, tmp, i)
                sd = tmp['sd'][i % len(tmp['sd'])]
                for c in range(nch):
                    if B is None:
                        C.stt(out(c), src[:, c, :], A(c), sd[:, 0:n], ALU.mult, ALU.mult)
                    else:
                        tt_ = tmp['t'][c % 2]
                        C.stt(tt_[:, 0:n], src[:, c, :], A(c), sd[:, 0:n], ALU.mult, ALU.mult)
                        C.act(out(c), tt_[:, 0:n], AF.Identity, bias=B(c))

            def norm_tmps(region, nb=1):
                return {'sq': [region.alloc("sq%d" % i, [8, 512], BF16) for i in range(nb)],
                        'sd': [region.alloc("sd%d" % i, [512], F32) for i in range(nb)],
                        't': [region.alloc("nt%d" % i, [512], F32) for i in range(2)]}

            def pipeline(n, stA, stB):
                if n == 0: return
                stA(0)
                for i in range(n):
                    if i + 1 < n: stA(i + 1)
                    stB(i)

            stg = [R_s.alloc("xstg%d" % i, [D], F32) for i in range(2)]
            ntm = norm_tmps(R_s, 2)
            if S:
                NXF = 3
                xfm = [R_x.alloc("xfm%d" % i, [8, 512], F32) for i in range(NXF)]
                xfs = []
                for i in range(NXF):
                    subs = [T(xfm[i].ap[:, :, jj * 128:(jj + 1) * 128], Buf("xfm%d_%d" % (i, jj))) for jj in range(4)]
                    xfm[i].bufs = [sb_.buf for sb_ in subs]
                    xfs.append(subs)
            else:
                xT = R_x.alloc("xT", [8, Tn], F32)
                xT_sub = [[T(xT.ap[:, :, t * 512 + jj * 128:t * 512 + (jj + 1) * 128], Buf("xT%d_%d" % (t, jj))) for jj in range(4)]
                          for t in range(NT)]
                xT_t = [T(xT.ap[:, :, t * 512:(t + 1) * 512], [sb_.buf for sb_ in xT_sub[t]]) for t in range(NT)]
                NXF = NT
                xfm = xT_t; xfs = xT_sub
            keep_x = R_x.p
            stLd = lambda t: load_x_fm(R_s, lambda j, t=t: xfs[t % NXF][j - 4 * t], 4 * t, 4, stg)
            stSt = lambda t: norm_stats(xfm[t % NXF], 8, 512, D, ntm, t)
            stAp = lambda t: norm_mod(xfm[t % NXF], 8, 512, D, lambda c: MOD(0, cj, 'A_mix', c), lambda c: MOD(0, cj, 'B_mix', c),
                                      lambda c, t=t: hT_t[t][:, c, :], ntm, t, stats=False)
            stLd(0)
            if NT > 1: stLd(1)
            if S:
                sth = R_s.alloc("xstgh", [D], F32)
                xfh = R_s.alloc("xfh", [8, 16], F32)
                C.dma('sp', sth[0:8, :], I["x_s"][0:8, :], sth.buf, writes=[sth.buf])
                C.dma('sp', sth[8:16, :], I["x_s"][TS + 8:TS + 16, :], sth.buf, writes=[sth.buf])
                for half in range(2):
                    bi, bk = getbank()
                    for q in range(4):
                        c = half * 4 + q
                        C.transpose(bk[:, q * 16:(q + 1) * 16], sth[0:16, c * 128:(c + 1) * 128], ident[0:16, 0:16])
                    C.copy(xfh[:, half * 4:half * 4 + 4, :], bk[:, 0:64].v(lambda a: a.rearrange("p (q t) -> p q t", q=4)))
                norm_mod(xfh, 8, 16, D, lambda c: MOD(0, cj, 'A_mix', c), lambda c: MOD(0, cj, 'B_mix', c),
                         lambda c: hT_h[:, c, :], ntm)
                C.tt(hT_h, hT_h, hmask.v(lambda a: a.unsqueeze(1).broadcast_to([128, 8, 16])), ALU.mult)
            stSt(0)
            for t in range(NT):
                if t + 2 < NT: stLd(t + 2)
                if t + 1 < NT: stSt(t + 1)
                stAp(t)
            C.barrier()

            R_s.reset()
            if S:
                R_x.reset()
            else:
                R_x.p = keep_x
            yT = R_s.alloc("yT", [8, Tn], BF16)
            wsl = [R_s.alloc("wsl%d" % i, [8, 128], BF16) for i in range(4)]
            win = R_s.alloc("win", [nseq, W], F32)
            xc = R_s.alloc("xc", [nseq, L], F32)
            xcb = R_s.alloc("xcb", [nseq, L], BF16)
            gg = R_s.alloc("gg", [Tn], BF16)
            b1s = [R_x.alloc("b1_%d" % d, [nseq, L], F32) for d in range(2)]
            ab = [R_x.alloc("a%d" % d, [nseq, L], F32) for d in range(2)]
            ub = [R_x.alloc("u%d" % d, [nseq, L], F32) for d in range(2)]
            hb_ = [R_x.alloc("h%d" % d, [nseq, L], F32) for d in range(2)]
            w_in0 = I["w_in0"].rearrange("(kc p) f -> p kc f", p=128)
            wslc = [0]

            def load_wcol(col0):
                s = wsl[wslc[0] % 4]; wslc[0] += 1
                C.dma('pool', s, w_in0[:, :, col0:col0 + 128], s.buf, writes=[s.buf])
                return s

            def inproj(ws, evac):
                for t in range(NT):
                    bi, bk = getbank()
                    for kc in range(KC):
                        C.mm(bk, ws[:, kc, :], hT_t[t][:, kc, :], kc == 0, kc == KC - 1)
                    evac(t, bk)

            def win_evac(t, bk):
                if S:
                    C.copy(win[:, 0, 8 + t * 512: 8 + (t + 1) * 512], bk, eng='act')
                else:
                    C.copy(win[:, 2 * t:2 * t + 2, 8:8 + L], bk.v(lambda a: a.rearrange("p (s l) -> p s l", s=2)), eng='act')

            def win_halo(ws):
                if S:
                    bi, bk = getbank()
                    for kc in range(KC):
                        C.mm(bk[:, 0:16], ws[:, kc, :], hT_h[:, kc, :], kc == 0, kc == KC - 1)
                    C.copy(win[:, 0, 0:8], bk[:, 0:8], eng='act')
                    C.copy(win[:, 0, W - 8:W], bk[:, 8:16], eng='act')

            if not S:
                C.memset(win, 0.0)
            invd = I["invc_s"] if S else I["invc_p"]
            p1 = R_ffn.alloc("p1", [nseq, W], F32); p2 = R_ffn.alloc("p2", [nseq, W], F32)
            dpl = R_ffn.alloc("dpl", [Tn], BF16)
            invt = R_ffn.alloc("invt", [nseq, L], F32)

            def pool_branch(g, ws_p):
                inproj(ws_p, win_evac); win_halo(ws_p)
                C.dma('sp', invt, invd[g].rearrange("(s l) -> s l", s=nseq).partition_broadcast(128), invt.buf, writes=[invt.buf])
                C.tt(p1[:, :, 1:W], win[:, :, 0:W - 1], win[:, :, 1:W], ALU.add)
                fin, oth = p1, p2
                if g >= 1:
                    C.tt(p2[:, :, 2:W - 1], p1[:, :, 1:W - 2], p1[:, :, 3:W], ALU.add); fin, oth = p2, p1
                if g >= 2:
                    C.tt(p1[:, :, 4:W - 3], p2[:, :, 2:W - 5], p2[:, :, 6:W - 1], ALU.add); fin, oth = p1, p2
                if g >= 3:
                    C.tt(p2[:, :, 8:W - 7], p1[:, :, 4:W - 11], p1[:, :, 12:W - 3], ALU.add); fin, oth = p2, p1
                C.tt(oth[:, :, 8:8 + L], fin[:, :, 8:8 + L], invt, ALU.mult)
                C.tt(dpl.v(lambda a: a.rearrange("p (s l) -> p s l", s=nseq)), oth[:, :, 8:8 + L], win[:, :, 8:8 + L], ALU.subtract)
                for t in range(NT):
                    bi, bk = getbank()
                    C.mm(bk, poolw[:, g, :], dpl[:, tsl(t)], True, True)
                    C.act(yT[:, 4 + g, tsl(t)], bk, AF.Copy, scale=vcol(V_SPOOL, g))

            ggs = [gg, R_s.alloc("gg2", [Tn], BF16)]

            def front_loads(c):
                return (load_wcol(c * 128), load_wcol(512 + c * 128), load_wcol(1024 + c * 128))

            def front(c, slots):
                ws_x, ws_g, ws_p_ = slots
                inproj(ws_x, win_evac); win_halo(ws_x)
                g_ = ggs[c % 2]
                inproj(ws_g, lambda t, bk: C.act(g_[:, tsl(t)], bk, AF.Gelu_apprx_tanh))
                C.ts(xc, win[:, :, 6:6 + L], vcol(V_CONVW, c * 4 + 0), vcol(V_CONVB, c), ALU.mult, ALU.add)
                for k in range(1, 4):
                    C.stt(xc, win[:, :, 6 + k:6 + k + L], vcol(V_CONVW, c * 4 + k), xc, ALU.mult, ALU.add)
                C.copy(xcb, xc, eng='act')
                return ws_p_

            ibuf = [p1, p2]

            def t3(b3, t):
                if S:
                    return b3[:, 0, t * 512:(t + 1) * 512]
                return b3[:, 2 * t:2 * t + 2, 0:L]

            def bkv(bk):
                return bk if S else bk.v(lambda a: a.rearrange("p (s l) -> p s l", s=2))

            def gates(c):
                xcbf = xcb.v(lambda a: a.rearrange("p s l -> p (s l)"))
                for t in range(NT):
                    for d in range(2):
                        bi, bk = getbank()
                        C.mm(bk, wbd[:, (d * 2 + 0) * 4 + c, :], xcbf[:, tsl(t)], True, True)
                        C.act(t3(b1s[d], t), bkv(bk), AF.Sigmoid, bias=vcol(V_BA, d * 4 + c))
                        bi, bk = getbank()
                        C.mm(bk, wbd[:, (d * 2 + 1) * 4 + c, :], xcbf[:, tsl(t)], True, True)
                        C.act(t3(ibuf[d], t), bkv(bk), AF.Sigmoid, bias=vcol(V_BI, d * 4 + c))

            wsp_next = front(0, front_loads(0))
            gates(0)
            for c in range(4):
                ws_p = wsp_next
                gg = ggs[c % 2]
                for d in range(2):
                    C.act(ab[d], b1s[d], AF.Exp, scale=s8[:, d * 4 + c:d * 4 + c + 1])
                    C.act(b1s[d], b1s[d], AF.Exp, scale=s16[:, d * 4 + c:d * 4 + c + 1])
                if S:
                    for d in range(2):
                        C.tt(ub[d], ibuf[d][:, :, 0:L], xc, ALU.mult)
                        C.act(b1s[d], b1s[d], AF.Relu, bias=1.0, scale=-1.0)
                        C.act(b1s[d], b1s[d], AF.Sqrt)
                    for d in range(2):
                        C.tt(ub[d], ub[d], b1s[d], ALU.mult)
                else:
                    for d in range(2):
                        C.tt(ub[d], ibuf[d][:, :, 0:L], xc, ALU.mult)
                    nl = front_loads(c + 1) if c + 1 < 4 else None
                    pool_branch(c, ws_p)
                    if c + 1 < 4:
                        wsp_next = front(c + 1, nl)
                    for d in range(2):
                        C.act(b1s[d], b1s[d], AF.Relu, bias=1.0, scale=-1.0)
                        C.act(b1s[d], b1s[d], AF.Sqrt)
                    for d in range(2):
                        C.tt(ub[d], ub[d], b1s[d], ALU.mult)
                    if c + 1 < 4:
                        gates(c + 1)

                def do_scans(inits):
                    for d in range(2):
                        for s in range(nseq):
                            sl = (slice(None), s, slice(None)) if d == 0 else (slice(None), s, slice(None, None, -1))
                            C.scan(hb_[d][sl], ab[d][sl], ub[d][sl], inits[d])
                if not S:
                    do_scans([0.0, 0.0])
                    for d in range(2):
                        pos = L - 1 if d == 0 else 0
                        C.copy(stcols.v(lambda a: a.rearrange("p (s d c) -> p s d c", s=4, d=2))[:, :, d, c], hb_[d][:, :, pos])
                else:
                    do_scans([st0[:, 0, c:c + 1], st0[:, 1, c:c + 1]])
                    C.copy(sfin[:, c, 0:1], hb_[0][:, 0, L - 1:L]); C.copy(sfin[:, c, 1:2], hb_[1][:, 0, 0:1])
                    nl = front_loads(c + 1) if c + 1 < 4 else None
                    ccb = Buf("ccst%d" % c)
                    for d in range(2):
                        C.dma('pool', cc_st_in[c][d, :].rearrange("(p o) -> p o", o=1), sfin[:, c, d:d + 1], ccb,
                              reads=[sfin.buf], writes=[ccb])
                    ccb2 = Buf("ccst2_%d" % c)
                    C.collective(cc_st_in[c], cc_st_out[c], ccb, ccb2)
                    pool_branch(c, ws_p)
                    if c + 1 < 4:
                        wsp_next = front(c + 1, nl)
                        gates(c + 1)
                    ld = Buf("ccstl%d" % c)
                    for r_ in (0, 3):
                        C.dma('sp', gsel[:, c, r_:r_ + 1], cc_st_out[c][r_, :].rearrange("(p o) -> p o", o=1), ld,
                              reads=[ccb2], writes=[gsel.buf])
                    C.ts(sinit[:, c, 0:1], st0[:, 0, c:c + 1], sel[:, 0:1], None, ALU.mult)
                    C.stt(sinit[:, c, 0:1], gsel[:, c, 0:1], sel[:, 1:2], sinit[:, c, 0:1], ALU.mult, ALU.add)
                    C.ts(sinit[:, c, 1:2], st0[:, 1, c:c + 1], sel[:, 1:2], None, ALU.mult)
                    C.stt(sinit[:, c, 1:2], gsel[:, c, 3:4], sel[:, 0:1], sinit[:, c, 1:2], ALU.mult, ALU.add)
                    do_scans([sinit[:, c, 0:1], sinit[:, c, 1:2]])
                C.tt(hb_[0], hb_[0], hb_[1], ALU.add)
                C.tt(yT[:, c, :], hb_[0].v(lambda a: a.rearrange("p s l -> p (s l)")), gg, ALU.mult)
            if not S:
                bi, bk = getbank()
                C.transpose(bk[0:32, 0:128], stcols, ident)
                so = R_ffn.alloc("so", [128], F32)
                C.copy(so[0:32, :], bk[0:32, 0:128])
                C.dma('sp', O["o_lru"], so[0:32, :], so.buf, reads=[so.buf])
            C.barrier()

            def ffn_slots():
                R_ffn.reset()
                w1s_ = [R_ffn.alloc("w1s%d" % i, [8, 512], BF16) for i in range(2)]
                w2s_ = [R_ffn.alloc("w2s%d" % i, [4, D], BF16) for i in range(2)]
                return w1s_, w2s_

            def ffn_load(l, g, w1s_, w2s_):
                w1src = I["ffn_w1_%d" % l].rearrange("(kc p) f -> p kc f", p=128)
                w2src = I["ffn_w2_%d" % l].rearrange("(kc p) f -> p kc f", p=128)
                w1, w2 = w1s_[g % 2], w2s_[g % 2]
                C.dma('pool', w1, w1src[:, :, g * 512:(g + 1) * 512], w1.buf, writes=[w1.buf])
                C.dma('pool', w2, w2src[:, g * 4:(g + 1) * 4, :], w2.buf, writes=[w2.buf])

            if S:
                R_x.reset()
                xT = R_x.alloc("xT", [8, Tn], F32)
                xT_sub = [[T(xT.ap[:, :, t * 512 + jj * 128:t * 512 + (jj + 1) * 128], Buf("xT%d_%d" % (t, jj))) for jj in range(4)]
                          for t in range(NT)]
                xT_t = [T(xT.ap[:, :, t * 512:(t + 1) * 512], [sb_.buf for sb_ in xT_sub[t]]) for t in range(NT)]
            R_s.reset()
            yT2 = R_s.alloc("yT", [8, Tn], BF16); yT2.buf = yT.buf
            wo = R_s.alloc("wo", [8, D], BF16)
            stg = [R_s.alloc("xstg%d" % i, [D], F32) for i in range(2)]
            wosrc = I["w_out0"].rearrange("(kc p) f -> p kc f", p=128)
            for hlf in range(2):
                C.dma('pool', wo[:, hlf * 4:hlf * 4 + 4, :], wosrc[:, hlf * 4:hlf * 4 + 4, :], wo.buf, writes=[wo.buf])

            def outproj_acc(wt, rhs_fn, nk, t, gname, l):
                for oc in range(8):
                    bi, bk = getbank()
                    for kc in range(nk):
                        C.mm(bk, wt(kc, oc), rhs_fn(kc), kc == 0, kc == nk - 1)
                    C.stt(xT_t[t][:, oc, :], bk, MOD(l, cj, gname, oc), xT_t[t][:, oc, :], ALU.mult, ALU.add)

            R_ffn.reset()
            ntm_pre = norm_tmps(R_ffn, 1)

            def stD(t):
                outproj_acc(lambda kc, oc: wo[:, kc, oc * 128:(oc + 1) * 128], lambda kc, t=t: yT2[:, kc, tsl(t)], 8, t, 'G_mix', 0)
                if t == 0:
                    norm_mod(xT_t[0], 8, 512, D, lambda c: MOD(0, cj, 'A_ffn', c), lambda c: MOD(0, cj, 'B_ffn', c),
                             lambda c: hT_t[0][:, c, :], ntm_pre, 0)

            pipeline(NT,
                     lambda t: (load_x_fm(R_s, lambda j, t=t: xT_sub[t][j - 4 * t], 4 * t, 4, stg) if S else None),
                     stD)
            C.barrier()

            def ffn(l, pre=None, skip0=False):
                R_s.reset()
                if pre is None:
                    w1s, w2s = ffn_slots()
                else:
                    w1s, w2s = pre
                ntm2 = norm_tmps(R_s, 2)

                def nrm(t):
                    norm_mod(xT_t[t], 8, 512, D, lambda c: MOD(l, cj, 'A_ffn', c), lambda c: MOD(l, cj, 'B_ffn', c),
                             lambda c, t=t: hT_t[t][:, c, :], ntm2, t)
                if not skip0:
                    nrm(0)
                actb = [R_s.alloc("actb%d" % i, [4, 512], BF16) for i in range(2)]
                rl = [R_s.alloc("rl%d" % i, [512], F32) for i in range(2)]
                k = 0
                prev = None

                def acc(pv):
                    w2_, ac_, t_ = pv
                    outproj_acc(lambda kc, oc: w2_[:, kc, oc * 128:(oc + 1) * 128], lambda kc: ac_[:, kc, :], 4, t_, 'G_ffn', l)

                for g in range(8):
                    w1, w2 = w1s[g % 2], w2s[g % 2]
                    if not (g == 0 and pre is not None):
                        ffn_load(l, g, w1s, w2s)
                    for t in range(NT):
                        if g == 0 and t + 1 < NT:
                            nrm(t + 1)
                        ac = actb[k % 2]; k += 1
                        for j in range(4):
                            bi, bk = getbank()
                            for kc in range(KC):
                                C.mm(bk, w1[:, kc, j * 128:(j + 1) * 128], hT_t[t][:, kc, :], kc == 0, kc == KC - 1)
                            r = rl[j % 2]
                            C.act(r, bk, AF.Relu)
                            C.tt(ac[:, j, :], r, r, ALU.mult)
                        if prev is not None:
                            acc(prev)
                        prev = (w2, ac, t)
                acc(prev)
                C.barrier()

            ffn(0, skip0=True)

            R_s.reset(); R_ffn.reset()
            NK = (2 * TS + 512) if S else TP
            ckvn = R_s.alloc("ckvn", [2, NK], BF16)
            Kb = R_s.alloc("Kb", [NK], BF16)
            cqn = R_s.alloc("cqn", [3, Tn], BF16)
            keep_s = R_s.p
            w1x = R_s.alloc("w1x", [8, 704], BF16)
            ntm3 = norm_tmps(R_s)
            C.dma('pool', w1x, I["w_in1x"].rearrange("(kc p) f -> p kc f", p=128), w1x.buf, writes=[w1x.buf])
            def nrmF(t):
                norm_mod(xT_t[t], 8, 512, D, lambda c: MOD(1, cj, 'A_mix', c), lambda c: MOD(1, cj, 'B_mix', c),
                         lambda c, t=t: hT_t[t][:, c, :], ntm3)
            nrmF(0)
            sq3 = R_s.view(ntm3['sq'][0], [3, 512], BF16)
            sd3 = ntm3['sd'][0]
            if S:
                ropeT = R_ffn.alloc("ropeT", [2, TS], F32)
                rp = Buf("ropeld")
                C.dma('sp', ropeT[0:32], I["rope"], rp, writes=[ropeT.buf])
                C.dma('sp', ropeT[64:96], I["rope"], rp, writes=[ropeT.buf])
                C.dma('sp', ropeT[96:128], I["rope"], rp, writes=[ropeT.buf])
                kvx = R_ffn.alloc("kvx", [2, TS], BF16)
                kpx = R_ffn.alloc("kpx", [TS], BF16)
                rt = [R_ffn.alloc("rt%d" % i, [512], F32) for i in range(2)]
            else:
                gkvt = R_ffn.alloc("gkvt", [256], F32)
                C.dma('sp', gkvt, I["gkv_row"].partition_broadcast(128), gkvt.buf, writes=[gkvt.buf])
                ost = [R_ffn.alloc("ost%d" % i, [288], F32) for i in range(2)]
                osq = R_ffn.alloc("osq", [256], F32)
                oss = R_ffn.alloc("oss", [2], F32)

            def lat_norm(col0, nch, dim, goff, out_fn, t):
                bs = []
                for c in range(nch):
                    bi, bk = getbank(hold=True); bs.append((bi, bk))
                    for kc in range(KC):
                        C.mm(bk, w1x[:, kc, col0 + c * 128:col0 + (c + 1) * 128], hT_t[t][:, kc, :], kc == 0, kc == KC - 1)
                    C.act(sq3[:, c, :], bk, AF.Square)
                bi2, bk2 = getbank()
                for c in range(nch):
                    C.mm(bk2, ones_b, sq3[:, c, :], c == 0, c == nch - 1)
                C.act(sd3, bk2, AF.Ln, bias=EPS, scale=1.0 / dim)
                C.act(sd3, sd3, AF.Exp, scale=-0.5)
                for c in range(nch):
                    C.stt(out_fn(c), bs[c][1], vcol(goff, c), sd3, ALU.mult, ALU.mult)
                    release(bs[c][0])

            for t in range(NT):
                if t + 1 < NT:
                    nrmF(t + 1)
                if S:
                    lat_norm(384, 2, 256, V_GKV, lambda c, t=t: kvx[:, c, tsl(t)], t)
                else:
                    lat_norm(384, 2, 256, V_GKV, lambda c, t=t: ckvn[:, c, tsl(t)], t)
                bi, bk = getbank()
                for kc in range(KC):
                    C.mm(bk[0:32, :], w1x[:, kc, 640:672], hT_t[t][:, kc, :], kc == 0, kc == KC - 1)
                if S:
                    bi2, bk2 = getbank()
                    for kc in range(KC):
                        C.mm(bk2[0:32, :], w1x[:, kc, 672:704], hT_t[t][:, kc, :], kc == 0, kc == KC - 1)
                    C.tt(rt[0][0:32, :], bk[0:32, :], ropeT[0:32, 0, tsl(t)], ALU.mult)
                    C.tt(rt[1][0:32, :], bk2[0:32, :], ropeT[0:32, 1, tsl(t)], ALU.mult)
                    C.tt(kpx[0:32, tsl(t)], rt[0][0:32, :], rt[1][0:32, :], ALU.add)
                else:
                    C.copy(Kb[64:96, tsl(t)], bk[0:32, :], eng='act')
            if not S:
                for j in range(Tn // 128):
                    t = j // 4
                    bi, bk = getbank()
                    for kc in range(KC):
                        C.mm(bk[:, 0:288], hT_t[t][:, kc, (j % 4) * 128:(j % 4 + 1) * 128], w1x[:, kc, 384:672], kc == 0, kc == KC - 1)
                    os_ = ost[j % 2]
                    C.act(osq, bk[:, 0:256], AF.Square)
                    C.op('dve', lambda e: e.reduce_sum(out=oss.ap[:, 0:1], in_=osq.ap, axis=AX.X), reads=[osq.buf], writes=[oss.buf])
                    C.act(oss[:, 1:2], oss[:, 0:1], AF.Sqrt, bias=EPS, scale=1.0 / 256)
                    C.recip(oss[:, 1:2], oss[:, 1:2])
                    C.stt(os_[:, 0:256], bk[:, 0:256], oss[:, 1:2], gkvt, ALU.mult, ALU.mult)
                    C.copy(os_[:, 256:288], bk[:, 256:288], eng='act')
                    C.dma('sp', O["o_ckv"][j * 128:(j + 1) * 128, :], os_[:, 0:256], os_.buf, reads=[os_.buf])
                    C.dma('sp', O["o_kpe"][j * 128:(j + 1) * 128, :], os_[:, 256:288], os_.buf, reads=[os_.buf])
            else:
                cst = R_s.view(ntm3['sq'][0], [4, 288], F32)
                C.dma('sp', cst[:, :, 0:256], I["ctx_ckv"].rearrange("(j p) f -> p j f", p=128), cst.buf, writes=[cst.buf])
                C.dma('sp', cst[:, :, 256:288], I["ctx_kpe"].rearrange("(j p) f -> p j f", p=128), cst.buf, writes=[cst.buf])
                for c in range(2):
                    bi, bk = getbank()
                    for j in range(4):
                        C.transpose(bk[:, j * 128:(j + 1) * 128], cst[:, j, c * 128:(c + 1) * 128], ident)
                    C.copy(ckvn[:, c, 2 * TS:2 * TS + 512], bk)
                bi, bk = getbank()
                for j in range(4):
                    C.transpose(bk[0:32, j * 128:(j + 1) * 128], cst[:, j, 256:288], ident)
                C.copy(Kb[64:96, 2 * TS:2 * TS + 512], bk[0:32, :], eng='act')
                cin = Buf("cckvin"); cout = Buf("cckvout")
                for c in range(2):
                    C.dma('pool', cc_kv_in[c * 128:(c + 1) * 128, :], kvx[:, c, :], cin, reads=[kvx.buf], writes=[cin])
                C.dma('pool', cc_kv_in[256:288, :], kpx[0:32, :], cin, reads=[kpx.buf], writes=[cin])
                C.collective(cc_kv_in, cc_kv_out, cin, cout)
                kl = Buf("kvload")
                for r in range(2):
                    for c in range(2):
                        C.dma('sp', ckvn[:, c, r * TS:(r + 1) * TS], cc_kv_out[r * 288 + c * 128:r * 288 + (c + 1) * 128, :], kl,
                              reads=[cout], writes=[ckvn.buf])
                    C.dma('sp', Kb[64:96, r * TS:(r + 1) * TS], cc_kv_out[r * 288 + 256:r * 288 + 288, :], kl, reads=[cout], writes=[Kb.buf])
            for t in range(NT):
                lat_norm(0, 3, 384, V_GQ, lambda c, t=t: cqn[:, c, tsl(t)], t)
            C.barrier()

            R_s.p = keep_s
            R_h.reset()
            if S:
                R_ffn.p = 16 * 1024
            else:
                R_ffn.reset()
            wq = R_s.alloc("wq", [3, 2048], BF16)
            wkv = R_s.alloc("wkv", [2, 2048], BF16)
            wo1s = [R_s.alloc("wo1s%d" % i, [D], BF16) for i in range(2)]
            C.dma('pool', wq, I["w_qbx"].rearrange("(kc p) f -> p kc f", p=128), wq.buf, writes=[wq.buf])
            C.dma('pool', wkv, I["w_kvb"].rearrange("(kc p) f -> p kc f", p=128), wkv.buf, writes=[wkv.buf])
            wo1src = I["w_out1"].rearrange("(kc p) f -> p kc f", p=128)
            NKT = NK // 128
            Vbs = [R_h.alloc("Vb%d" % i, [NKT, 65], BF16) for i in range(2)]
            Kbs = [Kb, R_ffn.alloc("Kb2", [NK], BF16)]
            qbs = [R_h.alloc("qb", [Tn], BF16), R_ffn.alloc("qb2", [Tn], BF16)]
            NPT = 5
            PT = [R_h.alloc("PT%d" % i, [512], BF16) for i in range(NPT)]
            opair = [R_h.alloc("opair%d" % i, [Tn], BF16) for i in range(2)]
            rdens = [R_s.alloc("rden", [512], BF16), R_ffn.alloc("rden2", [512], BF16)]
            rdc = [0]; delayed = []; FDELAY = 10 if S else 2
            lnt = None if S else R_h.alloc("lnt", [512], F32)
            opall = None if S else R_h.alloc("opall", [8, Tn], BF16)
            wo1f = None if S else R_s.alloc("wo1f", [8, D], BF16)
            bcs = R_h.alloc("bcs", [512], F32)
            otmp = R_s.alloc("otmp", [512], BF16)
            qr = [R_h.alloc("qr%d" % i, [512], F32) for i in range(2)] if S else None
            for vb in Vbs:
                C.memset(vb[:, :, 64:65], 1.0)
            C.copy(Kbs[1][64:96, :], Kb[64:96, :], eng='act')
            if S:
                jobs = [(0, TS, 0, NK)]
            else:
                jobs = [(s * 256, 256, s * 256, 256) for s in range(4)]
            ptc = [0]
            LOOK = 3

            def v_unit(h, kt):
                bi, bk = getbank()
                for kc in range(2):
                    C.mm(bk[:, 0:64], ckvn[:, kc, kt * 128:(kt + 1) * 128], wkv[:, kc, h * 128 + 64:h * 128 + 128], kc == 0, kc == 1)
                C.copy(Vbs[h % 2][:, kt, 0:64], bk[:, 0:64], eng='dve')

            def k_unit(h, kb0):
                bi, bk = getbank()
                for kc in range(2):
                    C.mm(bk[0:64, :], wkv[:, kc, h * 128:h * 128 + 64], ckvn[:, kc, kb0:kb0 + 512], kc == 0, kc == 1)
                C.copy(Kbs[h % 2][0:64, kb0:kb0 + 512], bk[0:64, :])

            def q_unit(h, t):
                qb = qbs[h % 2]
                bi, bk = getbank()
                if S:
                    for kc in range(3):
                        C.mm(bk, wq[:, kc, h * 128:(h + 1) * 128], cqn[:, kc, tsl(t)], kc == 0, kc == 2)
                    C.copy(qb[0:64, tsl(t)], bk[0:64, :], eng='dve')
                    C.tt(qr[0][64:96, :], bk[64:96, :], ropeT[64:96, 0, tsl(t)], ALU.mult)
                    C.tt(qr[1][64:96, :], bk[96:128, :], ropeT[96:128, 1, tsl(t)], ALU.mult)
                    C.tt(qb[64:96, tsl(t)], qr[0][64:96, :], qr[1][64:96, :], ALU.add)
                else:
                    for kc in range(3):
                        C.mm(bk[0:96, :], wq[:, kc, h * 128:h * 128 + 96], cqn[:, kc, tsl(t)], kc == 0, kc == 2)
                    C.copy(qb[0:96, tsl(t)], bk[0:96, :], eng='dve')

            def o_unit(pr, t, oc):
                wo1 = wo1s[pr % 2]
                if t == 0 and oc == 0:
                    C.dma('pool', wo1, wo1src[:, pr, :], wo1.buf, writes=[wo1.buf])
                bi, bk = getbank()
                C.mm(bk, wo1[:, oc * 128:(oc + 1) * 128], opair[pr % 2][:, tsl(t)], True, True)
                C.stt(xT_t[t][:, oc, :], bk, MOD(1, cj, 'G_mix', oc), xT_t[t][:, oc, :], ALU.mult, ALU.add)

            def prep_units(h):
                us = [(lambda kb0=kb0: k_unit(h, kb0)) for kb0 in range(0, NK, 512)]
                us += [(lambda t=t: q_unit(h, t)) for t in range(NT)]
                vs = [(lambda kt=kt: v_unit(h, kt)) for kt in range(NKT)]
                out = []
                step = max(1, len(vs) // max(1, len(us)))
                vi = 0
                for u in us:
                    out.append(u)
                    out.extend(vs[vi:vi + step]); vi += step
                out.extend(vs[vi:])
                return out

            side = []

            def attend(h):
                Kh, qb, Vh = Kbs[h % 2], qbs[h % 2], Vbs[h % 2]
                op_ = opair[(h // 2) % 2] if S else opall[:, h // 2, :]
                its = []
                for (q0, nqt, k0, nk) in jobs:
                    for qq in range(q0, q0 + nqt, 512):
                        nq = min(512, q0 + nqt - qq)
                        for ki in range(nk // 128):
                            its.append((qq, nq, k0 + ki * 128, ki, nk // 128))
                cur = {}
                pend = []

                def fin2(qq, nq, oi, ob, rd, last):
                    bi, bk = getbank()
                    C.mm(bk[0:64, 0:nq], ones_b[64:65, 0:64], rd[64:65, 0:nq], True, True)
                    C.copy(bcs[0:64, 0:nq], bk[0:64, 0:nq], eng='dve')
                    C.tt(op_[(h % 2) * 64:(h % 2) * 64 + 64, qq:qq + nq], ob[0:64, 0:nq], bcs[0:64, 0:nq], ALU.mult)
                    release(oi)
                    if last and h % 2 == 1 and S:
                        pr = h // 2
                        side[0:0] = [(lambda t=t, oc=oc, pr=pr: o_unit(pr, t, oc)) for t in range(NT) for oc in range(8)]

                def flush_one():
                    pt, (qq, nq, ks, ki, nkt) = pend.pop(0)
                    oi, ob = cur[qq]
                    C.mm(ob[0:65, 0:nq], Vh[:, ks // 128, 0:65], pt[:, 0:nq], ki == 0, ki == nkt - 1, signal=True)
                    if ki == nkt - 1:
                        rd = rdens[rdc[0] % 2]; rdc[0] += 1
                        if S:
                            def rq(k, rd=rd, ob=ob):
                                with nc.allow_low_precision("softmax 1/den is a bf16 matmul operand"):
                                    C.recip(rd[64:65, k * 128:(k + 1) * 128], ob[64:65, k * 128:(k + 1) * 128])
                            rq(0)
                            for k in range(1, 4):
                                delayed.append([k, (lambda k=k, rq=rq: rq(k))])
                        else:
                            C.act(rd[64:65, 0:nq], ob[64:65, 0:nq], AF.Ln)
                            C.act(rd[64:65, 0:nq], rd[64:65, 0:nq], AF.Exp, scale=-1.0)
                        last = (pend == [] and ii_box[0] == n_it - 1)
                        delayed.append([FDELAY, (lambda qq=qq, nq=nq, oi=oi, ob=ob, rd=rd, last=last: fin2(qq, nq, oi, ob, rd, last))])

                def tick():
                    for dl in delayed:
                        dl[0] -= 1
                    ready = [dl for dl in delayed if dl[0] <= 0]
                    for dl in ready:
                        delayed.remove(dl)
                        dl[1]()

                n_it = len(its)
                ii_box = [0]
                sacc = [0.0]

                def pop_side(frac_left):
                    if side:
                        sacc[0] += len(side) / float(max(1, frac_left))
                        while sacc[0] >= 1.0 and side:
                            sacc[0] -= 1.0
                            side.pop(0)()

                if not S:
                    steps = 6
                    stp = 0
                    pts_all = []
                    for sp in range(2):
                        pts = []
                        for ki in range(2):
                            bi, bk = getbank()
                            for jb in range(2):
                                sq_ = 2 * sp + jb
                                ks = sq_ * 256 + ki * 128
                                C.mm(bk[:, jb * 256:(jb + 1) * 256], Kh[0:96, ks:ks + 128], qb[0:96, sq_ * 256:(sq_ + 1) * 256], True, True)
                            pt = PT[ptc[0] % NPT]; ptc[0] += 1
                            C.act(pt, bk, AF.Exp, scale=SCALE)
                            pts.append(pt)
                            tick(); pop_side(steps - stp); stp += 1
                        pts_all.append(pts)
                    for sp in range(2):
                        oi, ob = getbank(hold=True)
                        for jb in range(2):
                            sq_ = 2 * sp + jb
                            for ki in range(2):
                                ks = sq_ * 256 + ki * 128
                                C.mm(ob[0:65, jb * 256:(jb + 1) * 256], Vh[:, ks // 128, 0:65], pts_all[sp][ki][:, jb * 256:(jb + 1) * 256],
                                     ki == 0, ki == 1, signal=True)
                        rd = rdens[rdc[0] % 2]; rdc[0] += 1
                        C.act(lnt[64:65, :], ob[64:65, :], AF.Ln)
                        with nc.allow_low_precision("softmax 1/den is a bf16 matmul operand"):
                            C.act(rd[64:65, :], lnt[64:65, :], AF.Exp, scale=-1.0)
                        delayed.append([FDELAY, (lambda qq=sp * 512, oi=oi, ob=ob, rd=rd, last=(sp == 1): fin2(qq, 512, oi, ob, rd, last))])
                        tick(); pop_side(steps - stp); stp += 1
                    while side:
                        side.pop(0)()
                    return
                for ii, it in enumerate(its):
                    ii_box[0] = ii
                    qq, nq, ks, ki, nkt = it
                    if ki == 0:
                        cur[qq] = getbank(hold=True)
                    bi, bk = getbank()
                    C.mm(bk[:, 0:nq], Kh[0:96, ks:ks + 128], qb[0:96, qq:qq + nq], True, True)
                    pt = PT[ptc[0] % NPT]; ptc[0] += 1
                    C.act(pt[:, 0:nq], bk[:, 0:nq], AF.Exp, scale=SCALE)
                    pend.append((pt, it))
                    if h == 0:
                        for u_ in jit0.pop(ii, []):
                            u_()
                    if len(pend) > LOOK:
                        flush_one()
                    tick()
                    if side:
                        sacc[0] += len(side) / float(n_it - ii)
                        while sacc[0] >= 1.0 and side:
                            sacc[0] -= 1.0
                            side.pop(0)()
                while pend:
                    flush_one()
                while side:
                    side.pop(0)()

            jit0 = {}
            if S:
                for kb0 in range(0, NK, 512):
                    k_unit(0, kb0)
                q_unit(0, 0)
                for kt in range(NKT):
                    jit0.setdefault(kt, []).append(lambda kt=kt: v_unit(0, kt))
                for t in range(1, NT):
                    jit0.setdefault(t, []).append(lambda t=t: q_unit(0, t))
            else:
                for u in prep_units(0):
                    u()
            for h in range(16):
                if h + 1 < 16:
                    side.extend(prep_units(h + 1))
                attend(h)
            while delayed:
                delayed.pop(0)[1]()
            while side:
                side.pop(0)()
            if not S:
                for hlf in range(2):
                    C.dma('pool', wo1f[:, hlf * 4:hlf * 4 + 4, :], wo1src[:, hlf * 4:hlf * 4 + 4, :], wo1f.buf, writes=[wo1f.buf])
                for t in range(NT):
                    outproj_acc(lambda kc, oc: wo1f[:, kc, oc * 128:(oc + 1) * 128], lambda kc, t=t: opall[:, kc, tsl(t)], 8, t, 'G_mix', 1)
            C.barrier()

            R_h.reset()
            hT2 = R_h.alloc("hT", [8, TH], BF16)
            for t in range(NT):
                hT_t[t] = T(hT2.ap[:, :, t * 512:(t + 1) * 512], Buf("hTb%d" % t))
            ffn(1)

            R_s.reset()
            ntm4 = norm_tmps(R_s, 2)
            yf = [R_s.alloc("yf%d" % i, [8, 512], F32) for i in range(2)]
            ytm = [R_s.alloc("ytm%d" % i, [D], F32) for i in range(2)]

            def fin_store(t):
                y_ = yf[t % 2]
                norm_mod(xT_t[t], 8, 512, D, lambda c: vcol(V_GFIN, c), None, lambda c, y_=y_: y_[:, c, :], ntm4, t, stats=False)
                for jj in range(4):
                    j = t * 4 + jj
                    yt = ytm[j % 2]
                    for half in range(2):
                        bi, bk = getbank()
                        for q in range(4):
                            c = half * 4 + q
                            C.transpose(bk[:, q * 128:(q + 1) * 128], y_[:, c, jj * 128:(jj + 1) * 128], ident)
                        C.copy(yt[:, half * 512:(half + 1) * 512], bk, eng='act' if j % 2 == 0 else 'dve')
                    C.dma('sp', yd[j * 128:(j + 1) * 128, :], yt, yt.buf, reads=[yt.buf])

            pipeline(NT, lambda t: norm_stats(xT_t[t], 8, 512, D, ntm4, t), fin_store)
            C.barrier()

        run_wave('P')
        C.recycle()
        run_wave('S')
        C.finish()
    return nc


def _host_consts():
    inv = (10000.0 ** (-np.arange(0, 16, 2, dtype=np.float32) / np.float32(16))).astype(np.float32)
    return inv


def kernel(x_prompt, x_sample, state_l0_lru, cache_l1_ckv, cache_l1_kpe, c, c_ctx,
           l0_w_mod, l0_b_mod, l0_g_mix, l0_g_ffn, l0_w_in, l0_conv_w, l0_conv_b,
           l0_lru_w_a, l0_lru_b_a, l0_lru_w_i, l0_lru_b_i, l0_lru_lam, l0_pool_w, l0_pool_scale,
           l0_w_out, l0_ffn_w1, l0_ffn_w2,
           l1_w_mod, l1_b_mod, l1_g_mix, l1_g_ffn, l1_w_in, l1_g_q, l1_w_qb, l1_g_kv, l1_w_kvb,
           l1_w_out, l1_ffn_w1, l1_ffn_w2, g_final, _debug=False):
    f = lambda a: np.ascontiguousarray(np.asarray(a, dtype=np.float32))
    x_prompt, x_sample = f(x_prompt), f(x_sample)

    def cols(v, n):
        return f(v).reshape(n, 128).T

    vecs = np.zeros((128, NV), np.float32)
    vecs[:, V_GMIX0:V_GMIX0 + 8] = cols(l0_g_mix, 8); vecs[:, V_GFFN0:V_GFFN0 + 8] = cols(l0_g_ffn, 8)
    vecs[:, V_GMIX1:V_GMIX1 + 8] = cols(l1_g_mix, 8); vecs[:, V_GFFN1:V_GFFN1 + 8] = cols(l1_g_ffn, 8)
    vecs[:, V_GFIN:V_GFIN + 8] = cols(g_final, 8)
    cw = f(l0_conv_w)
    for ch in range(4):
        for k in range(4):
            vecs[:, V_CONVW + ch * 4 + k] = cw[k, ch * 128:(ch + 1) * 128]
    vecs[:, V_CONVB:V_CONVB + 4] = cols(l0_conv_b, 4)
    for d in range(2):
        vecs[:, V_BA + d * 4:V_BA + d * 4 + 4] = cols(f(l0_lru_b_a)[d], 4)
        vecs[:, V_BI + d * 4:V_BI + d * 4 + 4] = cols(f(l0_lru_b_i)[d], 4)
        vecs[:, V_LAM + d * 4:V_LAM + d * 4 + 4] = cols(f(l0_lru_lam)[d], 4)
    vecs[:, V_SPOOL:V_SPOOL + 4] = cols(l0_pool_scale, 4)
    vecs[:, V_GQ:V_GQ + 3] = cols(l1_g_q, 3); vecs[:, V_GKV:V_GKV + 2] = cols(l1_g_kv, 2)
    bmods = (np.ascontiguousarray(cols(l0_b_mod, 48)), np.ascontiguousarray(cols(l1_b_mod, 48)))
    wmods = (f(l0_w_mod), f(l1_w_mod))
    wbd = np.zeros((128, 16, 128), np.float32)
    wa, wi = f(l0_lru_w_a), f(l0_lru_w_i)
    for d in range(2):
        for gi, wsrc in enumerate((wa, wi)):
            for ch in range(4):
                for j in range(2):
                    wbd[j * 64:(j + 1) * 64, (d * 2 + gi) * 4 + ch, j * 64:(j + 1) * 64] = wsrc[d, 2 * ch + j]
    pool_w = np.ascontiguousarray(f(l0_pool_w).transpose(1, 0, 2))
    perm = np.concatenate([np.arange(8, 16), np.arange(0, 8), np.arange(24, 32), np.arange(16, 24)])
    w_in1 = f(l1_w_in)
    w_in1x = np.ascontiguousarray(np.concatenate([w_in1, w_in1[:, 640 + perm]], axis=1))
    w_qb = f(l1_w_qb)
    qcols = np.concatenate([np.concatenate([np.arange(h * 96, (h + 1) * 96), h * 96 + 64 + perm]) for h in range(16)])
    w_qbx = np.ascontiguousarray(w_qb[:, qcols])
    shared = {"vecs": vecs,
              "w_in0": f(l0_w_in), "wbd": wbd, "pool_w": pool_w, "w_out0": f(l0_w_out),
              "ffn_w1_0": f(l0_ffn_w1), "ffn_w2_0": f(l0_ffn_w2), "w_in1x": w_in1x, "w_qbx": w_qbx,
              "w_kvb": f(l1_w_kvb), "w_out1": f(l1_w_out), "ffn_w1_1": f(l1_ffn_w1), "ffn_w2_1": f(l1_ffn_w2),
              "gkv_row": f(l1_g_kv), "ident": np.eye(128, dtype=np.float32)}
    inv = _host_consts()

    def invc(S_, start, n):
        tpos = np.arange(start, start + n)
        out = np.zeros((4, n), np.float32)
        for gi, w in enumerate((2, 4, 8, 16)):
            left = w // 2; right = w - 1 - left
            lo = np.maximum(tpos - left, 0); hi = np.minimum(tpos + right, S_ - 1) + 1
            out[gi] = (1.0 / (hi - lo).astype(np.float32)).astype(np.float32)
        return out

    invc_p = np.concatenate([invc(256, 0, 256)] * 4, axis=1)
    cs, cc_ = f(c), f(c_ctx)
    st = f(state_l0_lru); ckv_c = f(cache_l1_ckv); kpe_c = f(cache_l1_kpe)
    in_maps = []
    for core in range(8):
        sb, half = core // 2, core % 2
        start = half * TS
        xw = np.zeros((TS + 16, D), np.float32)
        lo, hi = start - 8, start + TS + 8
        slo, shi = max(lo, 0), min(hi, 4096)
        xw[slo - lo:shi - lo] = x_sample[sb, slo:shi]
        hm = np.zeros(16, np.float32)
        hm[0:8] = 1.0 if half == 1 else 0.0
        hm[8:16] = 1.0 if half == 0 else 0.0
        pos = np.arange(start, start + TS)
        row = (pos // 64).astype(np.float32); col = (pos % 64).astype(np.float32)
        rope = np.zeros((32, 2, TS), np.float32)
        for r in range(32):
            grp, j = r // 8, r % 8
            ang = ((row if grp < 2 else col) * inv[j]).astype(np.float32)
            rope[r, 0] = np.cos(ang); rope[r, 1] = np.sin(ang) * (-1.0 if grp % 2 == 0 else 1.0)
        cT = np.stack([cols(cc_, 8), cols(cs[sb], 8)], axis=2)
        st0 = np.stack([cols(st[sb, 0], 4), cols(st[sb, 1], 4)], axis=1)
        m = dict(shared)
        m.update({"x_p": np.ascontiguousarray(x_prompt[core * 4:(core + 1) * 4].reshape(TP, D)), "x_s": xw,
                  "cT": np.ascontiguousarray(cT), "st0": np.ascontiguousarray(st0),
                  "wmod_h": wmods[core % 2], "bmod_h": bmods[core % 2],
                  "ctx_ckv": np.ascontiguousarray(ckv_c[sb]), "ctx_kpe": np.ascontiguousarray(kpe_c[sb]),
                  "rope": rope, "invc_p": np.ascontiguousarray(invc_p), "invc_s": invc(4096, start, TS), "hmask": hm,
                  "sel": np.array([1.0, 0.0] if half == 0 else [0.0, 1.0], np.float32)})
        in_maps.append(m)
    nc = build_program()
    res = run_bass_kernel_spmd(nc, in_maps, core_ids=list(range(8)))
    R = res.results
    y_prompt = np.stack([R[k]["y_p"] for k in range(8)]).reshape(32, 256, D)
    y_sample = np.stack([R[k]["y_s"] for k in range(8)]).reshape(4, 4096, D)
    new_lru = np.stack([R[k]["o_lru"].reshape(4, 2, 512) for k in range(8)]).reshape(32, 2, 512)
    new_ckv = np.stack([R[k]["o_ckv"] for k in range(8)]).reshape(32, 256, 256)
    new_kpe = np.stack([R[k]["o_kpe"] for k in range(8)]).reshape(32, 256, 32)
    if _debug:
        return (y_prompt, y_sample, new_lru, new_ckv, new_kpe), R
    return (y_prompt.astype(np.float32), y_sample.astype(np.float32), new_lru.astype(np.float32),
            new_ckv.astype(np.float32), new_kpe.astype(np.float32))
```

```python
import contextlib
import numpy as np
import concourse.bass as bass
import concourse.mybir as mybir
from concourse.bass_utils import run_bass_kernel_spmd

F32 = mybir.dt.float32
BF16 = mybir.dt.bfloat16
AF = mybir.ActivationFunctionType
ALU = mybir.AluOpType
AX = mybir.AxisListType

D = 1024
KC = 8
EPS = 1e-6
TP, TS, HALO = 1024, 2048, 8
NV = 93
V_GMIX0, V_GFFN0, V_GMIX1, V_GFFN1, V_GFIN, V_CONVW, V_CONVB, V_BA, V_BI, V_LAM, V_SPOOL, V_GQ, V_GKV = \
    0, 8, 16, 24, 32, 40, 56, 60, 68, 76, 84, 88, 91
SCALE = 96.0 ** -0.5
PAIRS = [[0, 1], [2, 3], [4, 5], [6, 7]]


class Sem:
    def __init__(self, h):
        self.h = h; self.cnt = 0


class Buf:
    def __init__(self, name, persistent=False):
        self.name = name; self.w = None; self.r = {}; self.sem = None; self.persistent = persistent


class T:
    def __init__(self, ap, buf):
        self.ap = ap; self.bufs = list(buf) if isinstance(buf, (list, tuple)) else [buf]

    @property
    def buf(self):
        return self.bufs[0]

    @buf.setter
    def buf(self, b):
        self.bufs = [b]

    def __getitem__(self, k):
        return T(self.ap[k], self.bufs)

    def v(self, f):
        return T(f(self.ap), self.bufs)


def _bufs(*ts):
    out = []
    for t in ts:
        if isinstance(t, T): out.extend(t.bufs)
    return out


def _ap(x):
    return x.ap if isinstance(x, T) else x


class Ctx:
    def __init__(self, nc, es):
        self.nc = nc; self.es = es
        self.eng = {'pe': nc.tensor, 'act': nc.scalar, 'dve': nc.vector, 'pool': nc.gpsimd, 'sp': nc.sync}
        self.esem = {e: es.enter_context(nc.semaphore("s_" + e)) for e in ('pe', 'act', 'dve', 'pool')}
        self.cnt = {e: 0 for e in self.esem}
        self.seen = {e: {} for e in self.eng}
        self.sems = []
        self.free = []
        self.inuse = []
        self.dumps = []
        self.debug = False

    def getsem(self):
        if self.free:
            sm = self.free.pop()
        else:
            sm = Sem(self.es.enter_context(self.nc.semaphore("d%d" % len(self.sems)))); self.sems.append(sm)
        self.inuse.append(sm)
        return sm

    def recycle(self):
        self.free.extend(self.inuse); self.inuse = []

    def _sid(self, sem):
        return id(sem)

    def _wait(self, e, stamp):
        sem, val = stamp
        d = self.seen[e]; k = self._sid(sem)
        if d.get(k, 0) < val:
            self.eng[e].wait_ge(sem, val); d[k] = val

    def deps(self, e, reads, writes):
        st = []
        own = self.esem.get(e)
        for b in reads:
            if b.w: st.append(b.w)
        for b in writes:
            if b.w and b.w[0] is not own: st.append(b.w)
            st.extend(v for v in b.r.values() if v[0] is not own)
        pes = self.esem['pe']
        for s in st:
            if e == 'pe' and s[0] is pes:
                continue
            self._wait(e, s)

    def _stamp(self, stamp, reads, writes):
        k = self._sid(stamp[0])
        for b in writes:
            b.w = stamp; b.r = {}
        for b in reads:
            b.r[k] = stamp

    def op(self, e, fn, reads=(), writes=(), signal=True):
        self.deps(e, reads, writes)
        ins = fn(self.eng[e])
        sem = self.esem[e]
        if signal:
            self.cnt[e] += 1; ins.then_inc(sem, 1); stamp = (sem, self.cnt[e])
        else:
            stamp = (sem, self.cnt[e] + 1)
        self._stamp(stamp, reads, writes)
        return ins

    def dma(self, q, out, in_, sb, reads=(), writes=()):
        self.deps(q, reads, writes)
        if sb.sem is None:
            sb.sem = self.getsem()
        ins = self.eng[q].dma_start(out=_ap(out), in_=_ap(in_))
        sb.sem.cnt += 16; ins.then_inc(sb.sem.h, 16)
        self._stamp((sb.sem.h, sb.sem.cnt), reads, writes)

    def collective(self, ins_ap, outs_ap, inbuf, outbuf):
        self.deps('pool', [inbuf], [outbuf])
        sm = Sem(self.es.enter_context(self.nc.semaphore("cc%d" % len(self.sems)))); self.sems.append(sm)
        self.eng['pool'].collective_compute("AllGather", ALU.bypass, replica_groups=PAIRS, ins=[ins_ap],
                                            outs=[outs_ap]).then_inc(sm.h, 1)
        sm.cnt = 1
        self._stamp((sm.h, 1), [inbuf], [outbuf])

    def dump(self, name, t):
        if not self.debug: return
        shp = list(t.ap.shape)
        d = self.nc.dram_tensor("dbg_" + name, shp, t.ap.dtype, kind="ExternalOutput").ap()
        b = Buf("dbg_" + name)
        self.dma('sp', d, t, b, reads=[t.buf])
        self.dumps.append("dbg_" + name)

    def barrier(self):
        for e in ('pe', 'act', 'dve', 'pool', 'sp'):
            for f in self.esem:
                if self.cnt[f]: self._wait(e, (self.esem[f], self.cnt[f]))
            for sm in self.sems:
                if sm.cnt: self._wait(e, (sm.h, sm.cnt))

    def finish(self):
        for sm in self.sems:
            if sm.cnt: self._wait('sp', (sm.h, sm.cnt))
        for f in self.esem:
            if self.cnt[f]: self._wait('sp', (self.esem[f], self.cnt[f]))

    def act(self, out, in_, func, bias=0.0, scale=1.0):
        self.op('act', lambda e: e.activation(out=out.ap, in_=in_.ap, func=func, bias=_ap(bias), scale=_ap(scale)),
                reads=_bufs(in_, bias, scale), writes=out.bufs)

    def tt(self, out, a, b, op):
        self.op('dve', lambda e: e.tensor_tensor(out=out.ap, in0=a.ap, in1=b.ap, op=op), reads=_bufs(a, b), writes=out.bufs)

    def stt(self, out, a, s, b, op0, op1):
        self.op('dve', lambda e: e.scalar_tensor_tensor(out=out.ap, in0=a.ap, scalar=_ap(s), in1=b.ap, op0=op0, op1=op1),
                reads=_bufs(a, s, b), writes=out.bufs)

    def ts(self, out, a, s1, s2, op0, op1=None):
        if op1 is None:
            self.op('dve', lambda e: e.tensor_scalar(out=out.ap, in0=a.ap, scalar1=_ap(s1), scalar2=None, op0=op0),
                    reads=_bufs(a, s1), writes=out.bufs)
        else:
            self.op('dve', lambda e: e.tensor_scalar(out=out.ap, in0=a.ap, scalar1=_ap(s1), scalar2=_ap(s2), op0=op0, op1=op1),
                    reads=_bufs(a, s1, s2), writes=out.bufs)

    def copy(self, out, in_, eng='dve'):
        if eng == 'act':
            self.act(out, in_, AF.Copy)
        else:
            self.op(eng, lambda e: e.tensor_copy(out=out.ap, in_=in_.ap), reads=in_.bufs, writes=out.bufs)

    def recip(self, out, in_):
        self.op('dve', lambda e: e.reciprocal(out=out.ap, in_=in_.ap), reads=in_.bufs, writes=out.bufs)

    def memset(self, out, val, eng='dve'):
        self.op(eng, lambda e: e.memset(out.ap, val), writes=out.bufs)

    def mm(self, out, lhsT, rhs, start, stop, signal=None, tile_position=None):
        if signal is None: signal = stop
        kw = {}
        if tile_position is not None: kw['tile_position'] = tile_position
        self.op('pe', lambda e: e.matmul(out.ap, lhsT=lhsT.ap, rhs=rhs.ap, start=start, stop=stop, **kw),
                reads=_bufs(lhsT, rhs), writes=out.bufs, signal=signal)

    def transpose(self, out, in_, ident):
        self.op('pe', lambda e: e.transpose(out=out.ap, in_=in_.ap, identity=ident.ap), reads=_bufs(in_, ident), writes=out.bufs)

    def scan(self, out, a, u, init):
        self.op('dve', lambda e: e.tensor_tensor_scan(out=out.ap, data0=a.ap, data1=u.ap, initial=_ap(init), op0=ALU.mult, op1=ALU.add),
                reads=_bufs(a, u, init), writes=out.bufs)


class Region:
    def __init__(self, arena, base, size):
        self.arena = arena; self.base = base; self.size = size; self.p = 0

    def reset(self):
        self.p = 0

    def alloc(self, name, shape, dtype, persistent=False):
        n = int(np.prod(shape)); b = n * (2 if dtype == BF16 else 4)
        b = (b + 31) // 32 * 32
        assert self.p + b <= self.size, ("region overflow", name, self.p, b, self.size)
        w0 = (self.base + self.p) // 4; self.p += b
        ap = self.arena[:, w0:w0 + b // 4]
        if dtype == BF16:
            ap = ap.bitcast(BF16)
        ap = ap[:, 0:n]
        if len(shape) == 2:
            ap = ap.rearrange("p (a b) -> p a b", a=shape[0])
        elif len(shape) == 3:
            ap = ap.rearrange("p (a b c) -> p a b c", a=shape[0], b=shape[1])
        t = T(ap, Buf(name, persistent)); t.w0 = w0; t.nb = b
        return t

    def view(self, t, shape, dtype):
        n = int(np.prod(shape)); b = n * (2 if dtype == BF16 else 4)
        assert b <= t.nb
        ap = self.arena[:, t.w0:t.w0 + t.nb // 4]
        if dtype == BF16:
            ap = ap.bitcast(BF16)
        ap = ap[:, 0:n]
        if len(shape) == 2:
            ap = ap.rearrange("p (a b) -> p a b", a=shape[0])
        elif len(shape) == 3:
            ap = ap.rearrange("p (a b c) -> p a b c", a=shape[0], b=shape[1])
        return T(ap, t.buf)


def build_program(debug=False):
    nc = bass.Bass("TRN2", target_bir_lowering=False)

    def din(name, shape, dt=F32):
        return nc.dram_tensor(name, list(shape), dt, kind="ExternalInput").ap()

    def dout(name, shape, dt=F32):
        return nc.dram_tensor(name, list(shape), dt, kind="ExternalOutput").ap()

    I = {}
    for nm, shp in [("wmod_h", (D, 6 * D)), ("bmod_h", (128, 48)), ("vecs", (128, NV)),
                    ("w_in0", (D, 1536)), ("wbd", (128, 16, 128)), ("pool_w", (128, 4, 128)), ("w_out0", (D, D)),
                    ("ffn_w1_0", (D, 4 * D)), ("ffn_w2_0", (4 * D, D)), ("w_in1x", (D, 704)), ("w_qbx", (384, 2048)),
                    ("w_kvb", (256, 2048)), ("w_out1", (D, D)), ("ffn_w1_1", (D, 4 * D)), ("ffn_w2_1", (4 * D, D)),
                    ("gkv_row", (256,)), ("ident", (128, 128)),
                    ("x_p", (TP, D)), ("x_s", (TS + 2 * HALO, D)), ("cT", (128, 8, 2)), ("st0", (128, 2, 4)),
                    ("ctx_ckv", (512, 256)), ("ctx_kpe", (512, 32)), ("rope", (32, 2, TS)),
                    ("invc_p", (4, TP)), ("invc_s", (4, TS)), ("hmask", (16,)), ("sel", (2,))]:
        I[nm] = din(nm, shp)
    O = {"y_p": dout("y_p", (TP, D)), "y_s": dout("y_s", (TS, D)), "o_lru": dout("o_lru", (32, 128)),
         "o_ckv": dout("o_ckv", (TP, 256)), "o_kpe": dout("o_kpe", (TP, 32))}
    cc_st_in = [nc.dram_tensor("cc_st_in%d" % c, [2, 128], F32).ap() for c in range(4)]
    cc_st_out = [nc.dram_tensor("cc_st_out%d" % c, [4, 128], F32).ap() for c in range(4)]
    cc_mod_in = nc.dram_tensor("cc_mod_in", [128, 96], F32).ap()
    cc_mod_out = nc.dram_tensor("cc_mod_out", [256, 96], F32).ap()
    cc_kv_in = nc.dram_tensor("cc_kv_in", [288, TS], BF16).ap()
    cc_kv_out = nc.dram_tensor("cc_kv_out", [576, TS], BF16).ap()

    es = contextlib.ExitStack()
    with es:
        C = Ctx(nc, es)
        ARENA = 207 * 1024
        arena = es.enter_context(nc.sbuf_tensor("arena", [128, ARENA // 4], F32))
        psum_t = es.enter_context(nc.psum_tensor("psum", [128, 8, 512], F32))
        banks = [T(psum_t[:, i, :], Buf("bank%d" % i)) for i in range(8)]
        held = set()
        rr = [0]

        def getbank(hold=False):
            for _ in range(8):
                i = rr[0]; rr[0] = (rr[0] + 1) % 8
                if i not in held:
                    if hold: held.add(i)
                    return i, banks[i]
            raise RuntimeError("no bank")

        def release(i):
            held.discard(i)

        R_const = Region(arena, 0, 9 * 1024)
        R_ffn = Region(arena, 9 * 1024, 32 * 1024)
        R_x = Region(arena, 41 * 1024, 64 * 1024)
        R_h = Region(arena, 105 * 1024, 33 * 1024 + 256)
        R_s = Region(arena, 105 * 1024 + 33 * 1024 + 256, ARENA - (105 * 1024 + 33 * 1024 + 256))

        cbuf = Buf("const")

        def calloc(name, shape, dt=F32, own=False):
            t = R_const.alloc(name, shape, dt)
            if not own: t.buf = cbuf
            return t

        vecs = calloc("vecs", [NV]); bmod = calloc("bmod", [48]); cT = calloc("cT", [8, 2]); st0 = calloc("st0", [2, 4])
        ident = calloc("ident", [128]); sel = calloc("sel", [2]); hmask = calloc("hmask", [16])
        ones_b = calloc("ones_b", [128], BF16); ones_f = calloc("ones_f", [64])
        wbd = calloc("wbd", [16, 128], BF16); poolw = calloc("poolw", [4, 128], BF16)
        ones_b.buf = Buf("ones_b"); ones_f.buf = Buf("ones_f")
        scT = calloc("scT", [8, 2], own=True); s8 = calloc("s8", [8], own=True); s16 = calloc("s16", [8], own=True)
        lt = calloc("lt", [8], own=True)
        modsb = [calloc("modsb%d" % l, [48, 2], own=True) for l in range(2)]
        modA = [calloc("modA%d" % l, [2, 2, 8], own=True) for l in range(2)]
        stcols = calloc("stcols", [32], own=True); gsel = calloc("gsel", [4, 4], own=True)
        sinit = calloc("sinit", [4, 2], own=True); sfin = calloc("sfin", [4, 2], own=True)
        lsem = Buf("cload")
        for t, src in [(vecs, I["vecs"]), (bmod, I["bmod_h"]), (cT, I["cT"]), (st0, I["st0"]), (ident, I["ident"]),
                       (sel, I["sel"].partition_broadcast(128)), (hmask, I["hmask"].partition_broadcast(128))]:
            C.dma('sp', t, src, lsem, writes=[])
        C.dma('pool', wbd, I["wbd"], lsem, writes=[])
        C.dma('pool', poolw, I["pool_w"], lsem, writes=[])
        cbuf.w = (lsem.sem.h, lsem.sem.cnt)
        C.memset(ones_b, 1.0); C.memset(ones_f, 1.0)
        C.act(scT, cT, AF.Silu)
        C.act(lt, vecs[:, V_LAM:V_LAM + 8], AF.Exp, scale=-1.0)
        C.act(lt, lt, AF.Ln, bias=1.0)
        C.ts(s8, lt, -8.0, None, ALU.mult)
        C.ts(s16, lt, -16.0, None, ALU.mult)

        def vcol(off, i):
            return vecs[:, off + i:off + i + 1]

        R_s.reset()
        wm_slots = [R_s.alloc("wm%d" % i, [8, 1536], BF16) for i in range(2)]
        scTb = R_s.alloc("scTb", [8, 2], BF16)
        C.copy(scTb, scT)
        pcn = 0
        modl = R_s.alloc("modl", [48, 2], F32)
        wsrc = I["wmod_h"].rearrange("(kc p) f -> p kc f", p=128)
        bi, bk = getbank()
        for pc in range(4):
            sl = wm_slots[pcn % 2]; pcn += 1
            C.dma('pool', sl, wsrc[:, :, pc * 1536:(pc + 1) * 1536], sl.buf, writes=[sl.buf])
            for f in range(12):
                fc = pc * 12 + f
                for kc in range(KC):
                    C.mm(bk[:, fc * 2:fc * 2 + 2], sl[:, kc, f * 128:(f + 1) * 128], scTb[:, kc, :], kc == 0, kc == KC - 1)
        C.tt(modl, bk[:, 0:96].v(lambda a: a.rearrange("p (f j) -> p f j", j=2)),
             bmod.v(lambda a: a.unsqueeze(2).broadcast_to([128, 48, 2])), ALU.add)
        mi = Buf("ccmodin"); mo = Buf("ccmodout")
        C.dma('pool', cc_mod_in, modl.v(lambda a: a.rearrange("p f j -> p (f j)")), mi, reads=[modl.buf], writes=[mi])
        C.collective(cc_mod_in, cc_mod_out, mi, mo)
        for l in range(2):
            C.dma('sp', modsb[l].v(lambda a: a.rearrange("p f j -> p (f j)")), cc_mod_out[l * 128:(l + 1) * 128, :], modsb[l].buf,
                  reads=[mo], writes=[modsb[l].buf])
        for l in range(2):
            for j in range(2):
                for m, (sp_scale, goff) in enumerate([(1, V_GMIX0 if l == 0 else V_GMIX1), (4, V_GFFN0 if l == 0 else V_GFFN1)]):
                    C.ts(modA[l][:, j, m, :], modsb[l][:, sp_scale * 8:sp_scale * 8 + 8, j], 1.0, None, ALU.add)
                    C.tt(modA[l][:, j, m, :], modA[l][:, j, m, :], vecs[:, goff:goff + 8], ALU.mult)

        def MOD(l, j, which, c):
            if which == 'A_mix': return modA[l][:, j, 0, c:c + 1]
            if which == 'A_ffn': return modA[l][:, j, 1, c:c + 1]
            s = {'B_mix': 0, 'G_mix': 2, 'B_ffn': 3, 'G_ffn': 5}[which]
            return modsb[l][:, s * 8 + c:s * 8 + c + 1, j]

        C.barrier()

        def run_wave(kind):
            S = (kind == 'S')
            Tn = TS if S else TP
            nseq, L = (1, TS) if S else (4, 256)
            W = L + 16
            NT = Tn // 512
            cj = 1 if S else 0
            xd = I["x_s"] if S else I["x_p"]
            xoff = HALO if S else 0
            yd = O["y_s"] if S else O["y_p"]
            TH = Tn + (16 if S else 0)

            R_x.reset(); R_h.reset(); R_s.reset(); R_ffn.reset()
            hT = R_h.alloc("hT", [8, TH], BF16)
            hT_t = [T(hT.ap[:, :, t * 512:(t + 1) * 512], Buf("hT%d" % t)) for t in range(NT)]
            hT_h = T(hT.ap[:, :, Tn:TH], Buf("hTh")) if S else None

            def tsl(t):
                return slice(t * 512, (t + 1) * 512)

            def load_x_fm(region, dst_fn, j0, nj, stg):
                for j in range(j0, j0 + nj):
                    st = stg[j % len(stg)]
                    C.dma('sp', st, xd[xoff + j * 128: xoff + (j + 1) * 128, :], st.buf, writes=[st.buf])
                    dst = dst_fn(j)
                    for half in range(2):
                        bi, bk = getbank()
                        for q in range(4):
                            c = half * 4 + q
                            C.transpose(bk[:, q * 128:(q + 1) * 128], st[:, c * 128:(c + 1) * 128], ident)
                        C.copy(dst[:, half * 4:half * 4 + 4, :], bk.v(lambda a: a.rearrange("p (q t) -> p q t", q=4)),
                               eng='act' if j % 2 == 0 else 'dve')

            def norm_stats(src, nch, n, dim, tmp, i=0):
                sq = tmp['sq'][i % len(tmp['sq'])]
                C.act(sq[:, 0:nch, 0:n], src, AF.Square)
                bi, bk = getbank()
                for c in range(nch):
                    C.mm(bk[:, 0:n], ones_b, sq[:, c, 0:n], c == 0, c == nch - 1)
                sd = tmp['sd'][i % len(tmp['sd'])]
                C.act(sd[:, 0:n], bk[:, 0:n], AF.Ln, bias=EPS, scale=1.0 / dim)
                C.act(sd[:, 0:n], sd[:, 0:n], AF.Exp, scale=-0.5)

            def norm_mod(src, nch, n, dim, A, B, out, tmp, i=0, stats=True):
                if stats:
                    norm_stats(src, nch, n, dim, tmp, i)
                sd = tmp['sd'][i % len(tmp['sd'])]
                for c in range(nch):
                    if B is None:
                        C.stt(out(c), src[:, c, :], A(c), sd[:, 0:n], ALU.mult, ALU.mult)
                    else:
                        tt_ = tmp['t'][c % 2]
                        C.stt(tt_[:, 0:n], src[:, c, :], A(c), sd[:, 0:n], ALU.mult, ALU.mult)
                        C.act(out(c), tt_[:, 0:n], AF.Identity, bias=B(c))

            def norm_tmps(region, nb=1):
                return {'sq': [region.alloc("sq%d" % i, [8, 512], BF16) for i in range(nb)],
                        'sd': [region.alloc("sd%d" % i, [512], F32) for i in range(nb)],
                        't': [region.alloc("nt%d" % i, [512], F32) for i in range(2)]}

            def pipeline(n, stA, stB):
                if n == 0: return
                stA(0)
                for i in range(n):
                    if i + 1 < n: stA(i + 1)
                    stB(i)

            stg = [R_s.alloc("xstg%d" % i, [D], F32) for i in range(2)]
            ntm = norm_tmps(R_s, 2)
            if S:
                NXF = 3
                xfm = [R_x.alloc("xfm%d" % i, [8, 512], F32) for i in range(NXF)]
                xfs = []
                for i in range(NXF):
                    subs = [T(xfm[i].ap[:, :, jj * 128:(jj + 1) * 128], Buf("xfm%d_%d" % (i, jj))) for jj in range(4)]
                    xfm[i].bufs = [sb_.buf for sb_ in subs]
                    xfs.append(subs)
            else:
                xT = R_x.alloc("xT", [8, Tn], F32)
                xT_sub = [[T(xT.ap[:, :, t * 512 + jj * 128:t * 512 + (jj + 1) * 128], Buf("xT%d_%d" % (t, jj))) for jj in range(4)]
                          for t in range(NT)]
                xT_t = [T(xT.ap[:, :, t * 512:(t + 1) * 512], [sb_.buf for sb_ in xT_sub[t]]) for t in range(NT)]
                NXF = NT
                xfm = xT_t; xfs = xT_sub
            keep_x = R_x.p
            stLd = lambda t: load_x_fm(R_s, lambda j, t=t: xfs[t % NXF][j - 4 * t], 4 * t, 4, stg)
            stSt = lambda t: norm_stats(xfm[t % NXF], 8, 512, D, ntm, t)
            stAp = lambda t: norm_mod(xfm[t % NXF], 8, 512, D, lambda c: MOD(0, cj, 'A_mix', c), lambda c: MOD(0, cj, 'B_mix', c),
                                      lambda c, t=t: hT_t[t][:, c, :], ntm, t, stats=False)
            stLd(0)
            if NT > 1: stLd(1)
            if S:
                sth = R_s.alloc("xstgh", [D], F32)
                xfh = R_s.alloc("xfh", [8, 16], F32)
                C.dma('sp', sth[0:8, :], I["x_s"][0:8, :], sth.buf, writes=[sth.buf])
                C.dma('sp', sth[8:16, :], I["x_s"][TS + 8:TS + 16, :], sth.buf, writes=[sth.buf])
                for half in range(2):
                    bi, bk = getbank()
                    for q in range(4):
                        c = half * 4 + q
                        C.transpose(bk[:, q * 16:(q + 1) * 16], sth[0:16, c * 128:(c + 1) * 128], ident[0:16, 0:16])
                    C.copy(xfh[:, half * 4:half * 4 + 4, :], bk[:, 0:64].v(lambda a: a.rearrange("p (q t) -> p q t", q=4)))
                norm_mod(xfh, 8, 16, D, lambda c: MOD(0, cj, 'A_mix', c), lambda c: MOD(0, cj, 'B_mix', c),
                         lambda c: hT_h[:, c, :], ntm)
                C.tt(hT_h, hT_h, hmask.v(lambda a: a.unsqueeze(1).broadcast_to([128, 8, 16])), ALU.mult)
            stSt(0)
            for t in range(NT):
                if t + 2 < NT: stLd(t + 2)
                if t + 1 < NT: stSt(t + 1)
                stAp(t)
            C.barrier()

            R_s.reset()
            if S:
                R_x.reset()
            else:
                R_x.p = keep_x
            yT = R_s.alloc("yT", [8, Tn], BF16)
            wsl = [R_s.alloc("wsl%d" % i, [8, 128], BF16) for i in range(4)]
            win = R_s.alloc("win", [nseq, W], F32)
            xc = R_s.alloc("xc", [nseq, L], F32)
            xcb = R_s.alloc("xcb", [nseq, L], BF16)
            gg = R_s.alloc("gg", [Tn], BF16)
            b1s = [R_x.alloc("b1_%d" % d, [nseq, L], F32) for d in range(2)]
            ab = [R_x.alloc("a%d" % d, [nseq, L], F32) for d in range(2)]
            ub = [R_x.alloc("u%d" % d, [nseq, L], F32) for d in range(2)]
            hb_ = [R_x.alloc("h%d" % d, [nseq, L], F32) for d in range(2)]
            w_in0 = I["w_in0"].rearrange("(kc p) f -> p kc f", p=128)
            wslc = [0]

            def load_wcol(col0):
                s = wsl[wslc[0] % 4]; wslc[0] += 1
                C.dma('pool', s, w_in0[:, :, col0:col0 + 128], s.buf, writes=[s.buf])
                return s

            def inproj(ws, evac):
                for t in range(NT):
                    bi, bk = getbank()
                    for kc in range(KC):
                        C.mm(bk, ws[:, kc, :], hT_t[t][:, kc, :], kc == 0, kc == KC - 1)
                    evac(t, bk)

            def win_evac(t, bk):
                if S:
                    C.copy(win[:, 0, 8 + t * 512: 8 + (t + 1) * 512], bk, eng='act')
                else:
                    C.copy(win[:, 2 * t:2 * t + 2, 8:8 + L], bk.v(lambda a: a.rearrange("p (s l) -> p s l", s=2)), eng='act')

            def win_halo(ws):
                if S:
                    bi, bk = getbank()
                    for kc in range(KC):
                        C.mm(bk[:, 0:16], ws[:, kc, :], hT_h[:, kc, :], kc == 0, kc == KC - 1)
                    C.copy(win[:, 0, 0:8], bk[:, 0:8], eng='act')
                    C.copy(win[:, 0, W - 8:W], bk[:, 8:16], eng='act')

            if not S:
                C.memset(win, 0.0)
            invd = I["invc_s"] if S else I["invc_p"]
            p1 = R_ffn.alloc("p1", [nseq, W], F32); p2 = R_ffn.alloc("p2", [nseq, W], F32)
            dpl = R_ffn.alloc("dpl", [Tn], BF16)
            invt = R_ffn.alloc("invt", [nseq, L], F32)

            def pool_branch(g, ws_p):
                inproj(ws_p, win_evac); win_halo(ws_p)
                C.dma('sp', invt, invd[g].rearrange("(s l) -> s l", s=nseq).partition_broadcast(128), invt.buf, writes=[invt.buf])
                C.tt(p1[:, :, 1:W], win[:, :, 0:W - 1], win[:, :, 1:W], ALU.add)
                fin, oth = p1, p2
                if g >= 1:
                    C.tt(p2[:, :, 2:W - 1], p1[:, :, 1:W - 2], p1[:, :, 3:W], ALU.add); fin, oth = p2, p1
                if g >= 2:
                    C.tt(p1[:, :, 4:W - 3], p2[:, :, 2:W - 5], p2[:, :, 6:W - 1], ALU.add); fin, oth = p1, p2
                if g >= 3:
                    C.tt(p2[:, :, 8:W - 7], p1[:, :, 4:W - 11], p1[:, :, 12:W - 3], ALU.add); fin, oth = p2, p1
                C.tt(oth[:, :, 8:8 + L], fin[:, :, 8:8 + L], invt, ALU.mult)
                C.tt(dpl.v(lambda a: a.rearrange("p (s l) -> p s l", s=nseq)), oth[:, :, 8:8 + L], win[:, :, 8:8 + L], ALU.subtract)
                for t in range(NT):
                    bi, bk = getbank()
                    C.mm(bk, poolw[:, g, :], dpl[:, tsl(t)], True, True)
                    C.act(yT[:, 4 + g, tsl(t)], bk, AF.Copy, scale=vcol(V_SPOOL, g))

            ggs = [gg, R_s.alloc("gg2", [Tn], BF16)]

            def front_loads(c):
                return (load_wcol(c * 128), load_wcol(512 + c * 128), load_wcol(1024 + c * 128))

            def front(c, slots):
                ws_x, ws_g, ws_p_ = slots
                inproj(ws_x, win_evac); win_halo(ws_x)
                g_ = ggs[c % 2]
                inproj(ws_g, lambda t, bk: C.act(g_[:, tsl(t)], bk, AF.Gelu_apprx_tanh))
                C.ts(xc, win[:, :, 6:6 + L], vcol(V_CONVW, c * 4 + 0), vcol(V_CONVB, c), ALU.mult, ALU.add)
                for k in range(1, 4):
                    C.stt(xc, win[:, :, 6 + k:6 + k + L], vcol(V_CONVW, c * 4 + k), xc, ALU.mult, ALU.add)
                C.copy(xcb, xc, eng='act')
                return ws_p_

            ibuf = [p1, p2]

            def t3(b3, t):
                if S:
                    return b3[:, 0, t * 512:(t + 1) * 512]
                return b3[:, 2 * t:2 * t + 2, 0:L]

            def bkv(bk):
                return bk if S else bk.v(lambda a: a.rearrange("p (s l) -> p s l", s=2))

            def gates(c):
                xcbf = xcb.v(lambda a: a.rearrange("p s l -> p (s l)"))
                for t in range(NT):
                    for d in range(2):
                        bi, bk = getbank()
                        C.mm(bk, wbd[:, (d * 2 + 0) * 4 + c, :], xcbf[:, tsl(t)], True, True)
                        C.act(t3(b1s[d], t), bkv(bk), AF.Sigmoid, bias=vcol(V_BA, d * 4 + c))
                        bi, bk = getbank()
                        C.mm(bk, wbd[:, (d * 2 + 1) * 4 + c, :], xcbf[:, tsl(t)], True, True)
                        C.act(t3(ibuf[d], t), bkv(bk), AF.Sigmoid, bias=vcol(V_BI, d * 4 + c))

            wsp_next = front(0, front_loads(0))
            gates(0)
            for c in range(4):
                ws_p = wsp_next
                gg = ggs[c % 2]
                for d in range(2):
                    C.act(ab[d], b1s[d], AF.Exp, scale=s8[:, d * 4 + c:d * 4 + c + 1])
                    C.act(b1s[d], b1s[d], AF.Exp, scale=s16[:, d * 4 + c:d * 4 + c + 1])
                if S:
                    for d in range(2):
                        C.tt(ub[d], ibuf[d][:, :, 0:L], xc, ALU.mult)
                        C.act(b1s[d], b1s[d], AF.Relu, bias=1.0, scale=-1.0)
                        C.act(b1s[d], b1s[d], AF.Sqrt)
                    for d in range(2):
                        C.tt(ub[d], ub[d], b1s[d], ALU.mult)
                else:
                    for d in range(2):
                        C.tt(ub[d], ibuf[d][:, :, 0:L], xc, ALU.mult)
                    nl = front_loads(c + 1) if c + 1 < 4 else None
                    pool_branch(c, ws_p)
                    if c + 1 < 4:
                        wsp_next = front(c + 1, nl)
                    for d in range(2):
                        C.act(b1s[d], b1s[d], AF.Relu, bias=1.0, scale=-1.0)
                        C.act(b1s[d], b1s[d], AF.Sqrt)
                    for d in range(2):
                        C.tt(ub[d], ub[d], b1s[d], ALU.mult)
                    if c + 1 < 4:
                        gates(c + 1)

                def do_scans(inits):
                    for d in range(2):
                        for s in range(nseq):
                            sl = (slice(None), s, slice(None)) if d == 0 else (slice(None), s, slice(None, None, -1))
                            C.scan(hb_[d][sl], ab[d][sl], ub[d][sl], inits[d])
                if not S:
                    do_scans([0.0, 0.0])
                    for d in range(2):
                        pos = L - 1 if d == 0 else 0
                        C.copy(stcols.v(lambda a: a.rearrange("p (s d c) -> p s d c", s=4, d=2))[:, :, d, c], hb_[d][:, :, pos])
                else:
                    do_scans([st0[:, 0, c:c + 1], st0[:, 1, c:c + 1]])
                    C.copy(sfin[:, c, 0:1], hb_[0][:, 0, L - 1:L]); C.copy(sfin[:, c, 1:2], hb_[1][:, 0, 0:1])
                    nl = front_loads(c + 1) if c + 1 < 4 else None
                    ccb = Buf("ccst%d" % c)
                    for d in range(2):
                        C.dma('pool', cc_st_in[c][d, :].rearrange("(p o) -> p o", o=1), sfin[:, c, d:d + 1], ccb,
                              reads=[sfin.buf], writes=[ccb])
                    ccb2 = Buf("ccst2_%d" % c)
                    C.collective(cc_st_in[c], cc_st_out[c], ccb, ccb2)
                    pool_branch(c, ws_p)
                    if c + 1 < 4:
                        wsp_next = front(c + 1, nl)
                        gates(c + 1)
                    ld = Buf("ccstl%d" % c)
                    for r_ in (0, 3):
                        C.dma('sp', gsel[:, c, r_:r_ + 1], cc_st_out[c][r_, :].rearrange("(p o) -> p o", o=1), ld,
                              reads=[ccb2], writes=[gsel.buf])
                    C.ts(sinit[:, c, 0:1], st0[:, 0, c:c + 1], sel[:, 0:1], None, ALU.mult)
                    C.stt(sinit[:, c, 0:1], gsel[:, c, 0:1], sel[:, 1:2], sinit[:, c, 0:1], ALU.mult, ALU.add)
                    C.ts(sinit[:, c, 1:2], st0[:, 1, c:c + 1], sel[:, 1:2], None, ALU.mult)
                    C.stt(sinit[:, c, 1:2], gsel[:, c, 3:4], sel[:, 0:1], sinit[:, c, 1:2], ALU.mult, ALU.add)
                    do_scans([sinit[:, c, 0:1], sinit[:, c, 1:2]])
                C.tt(hb_[0], hb_[0], hb_[1], ALU.add)
                C.tt(yT[:, c, :], hb_[0].v(lambda a: a.rearrange("p s l -> p (s l)")), gg, ALU.mult)
            if not S:
                bi, bk = getbank()
                C.transpose(bk[0:32, 0:128], stcols, ident)
                so = R_ffn.alloc("so", [128], F32)
                C.copy(so[0:32, :], bk[0:32, 0:128])
                C.dma('sp', O["o_lru"], so[0:32, :], so.buf, reads=[so.buf])
            C.barrier()

            def ffn_slots():
                R_ffn.reset()
                w1s_ = [R_ffn.alloc("w1s%d" % i, [8, 512], BF16) for i in range(2)]
                w2s_ = [R_ffn.alloc("w2s%d" % i, [4, D], BF16) for i in range(2)]
                return w1s_, w2s_

            def ffn_load(l, g, w1s_, w2s_):
                w1src = I["ffn_w1_%d" % l].rearrange("(kc p) f -> p kc f", p=128)
                w2src = I["ffn_w2_%d" % l].rearrange("(kc p) f -> p kc f", p=128)
                w1, w2 = w1s_[g % 2], w2s_[g % 2]
                C.dma('pool', w1, w1src[:, :, g * 512:(g + 1) * 512], w1.buf, writes=[w1.buf])
                C.dma('pool', w2, w2src[:, g * 4:(g + 1) * 4, :], w2.buf, writes=[w2.buf])

            if S:
                R_x.reset()
                xT = R_x.alloc("xT", [8, Tn], F32)
                xT_sub = [[T(xT.ap[:, :, t * 512 + jj * 128:t * 512 + (jj + 1) * 128], Buf("xT%d_%d" % (t, jj))) for jj in range(4)]
                          for t in range(NT)]
                xT_t = [T(xT.ap[:, :, t * 512:(t + 1) * 512], [sb_.buf for sb_ in xT_sub[t]]) for t in range(NT)]
            R_s.reset()
            yT2 = R_s.alloc("yT", [8, Tn], BF16); yT2.buf = yT.buf
            wo = R_s.alloc("wo", [8, D], BF16)
            stg = [R_s.alloc("xstg%d" % i, [D], F32) for i in range(2)]
            wosrc = I["w_out0"].rearrange("(kc p) f -> p kc f", p=128)
            for hlf in range(2):
                C.dma('pool', wo[:, hlf * 4:hlf * 4 + 4, :], wosrc[:, hlf * 4:hlf * 4 + 4, :], wo.buf, writes=[wo.buf])

            def outproj_acc(wt, rhs_fn, nk, t, gname, l):
                for oc in range(8):
                    bi, bk = getbank()
                    for kc in range(nk):
                        C.mm(bk, wt(kc, oc), rhs_fn(kc), kc == 0, kc == nk - 1)
                    C.stt(xT_t[t][:, oc, :], bk, MOD(l, cj, gname, oc), xT_t[t][:, oc, :], ALU.mult, ALU.add)

            R_ffn.reset()
            ntm_pre = norm_tmps(R_ffn, 1)

            def stD(t):
                outproj_acc(lambda kc, oc: wo[:, kc, oc * 128:(oc + 1) * 128], lambda kc, t=t: yT2[:, kc, tsl(t)], 8, t, 'G_mix', 0)
                if t == 0:
                    norm_mod(xT_t[0], 8, 512, D, lambda c: MOD(0, cj, 'A_ffn', c), lambda c: MOD(0, cj, 'B_ffn', c),
                             lambda c: hT_t[0][:, c, :], ntm_pre, 0)

            pipeline(NT,
                     lambda t: (load_x_fm(R_s, lambda j, t=t: xT_sub[t][j - 4 * t], 4 * t, 4, stg) if S else None),
                     stD)
            C.barrier()

            def ffn(l, pre=None, skip0=False):
                R_s.reset()
                if pre is None:
                    w1s, w2s = ffn_slots()
                else:
                    w1s, w2s = pre
                ntm2 = norm_tmps(R_s, 2)

                def nrm(t):
                    norm_mod(xT_t[t], 8, 512, D, lambda c: MOD(l, cj, 'A_ffn', c), lambda c: MOD(l, cj, 'B_ffn', c),
                             lambda c, t=t: hT_t[t][:, c, :], ntm2, t)
                if not skip0:
                    nrm(0)
                actb = [R_s.alloc("actb%d" % i, [4, 512], BF16) for i in range(2)]
                rl = [R_s.alloc("rl%d" % i, [512], F32) for i in range(2)]
                k = 0
                prev = None

                def acc(pv):
                    w2_, ac_, t_ = pv
                    outproj_acc(lambda kc, oc: w2_[:, kc, oc * 128:(oc + 1) * 128], lambda kc: ac_[:, kc, :], 4, t_, 'G_ffn', l)

                for g in range(8):
                    w1, w2 = w1s[g % 2], w2s[g % 2]
                    if not (g == 0 and pre is not None):
                        ffn_load(l, g, w1s, w2s)
                    for t in range(NT):
                        if g == 0 and t + 1 < NT:
                            nrm(t + 1)
                        ac = actb[k % 2]; k += 1
                        for j in range(4):
                            bi, bk = getbank()
                            for kc in range(KC):
                                C.mm(bk, w1[:, kc, j * 128:(j + 1) * 128], hT_t[t][:, kc, :], kc == 0, kc == KC - 1)
                            r = rl[j % 2]
                            C.act(r, bk, AF.Relu)
                            C.tt(ac[:, j, :], r, r, ALU.mult)
                        if prev is not None:
                            acc(prev)
                        prev = (w2, ac, t)
                acc(prev)
                C.barrier()

            ffn(0, skip0=True)

            R_s.reset(); R_ffn.reset()
            NK = (2 * TS + 512) if S else TP
            ckvn = R_s.alloc("ckvn", [2, NK], BF16)
            Kb = R_s.alloc("Kb", [NK], BF16)
            cqn = R_s.alloc("cqn", [3, Tn], BF16)
            keep_s = R_s.p
            w1x = R_s.alloc("w1x", [8, 704], BF16)
            ntm3 = norm_tmps(R_s)
            C.dma('pool', w1x, I["w_in1x"].rearrange("(kc p) f -> p kc f", p=128), w1x.buf, writes=[w1x.buf])
            def nrmF(t):
                norm_mod(xT_t[t], 8, 512, D, lambda c: MOD(1, cj, 'A_mix', c), lambda c: MOD(1, cj, 'B_mix', c),
                         lambda c, t=t: hT_t[t][:, c, :], ntm3)
            nrmF(0)
            sq3 = R_s.view(ntm3['sq'][0], [3, 512], BF16)
            sd3 = ntm3['sd'][0]
            if S:
                ropeT = R_ffn.alloc("ropeT", [2, TS], F32)
                rp = Buf("ropeld")
                C.dma('sp', ropeT[0:32], I["rope"], rp, writes=[ropeT.buf])
                C.dma('sp', ropeT[64:96], I["rope"], rp, writes=[ropeT.buf])
                C.dma('sp', ropeT[96:128], I["rope"], rp, writes=[ropeT.buf])
                kvx = R_ffn.alloc("kvx", [2, TS], BF16)
                kpx = R_ffn.alloc("kpx", [TS], BF16)
                rt = [R_ffn.alloc("rt%d" % i, [512], F32) for i in range(2)]
            else:
                gkvt = R_ffn.alloc("gkvt", [256], F32)
                C.dma('sp', gkvt, I["gkv_row"].partition_broadcast(128), gkvt.buf, writes=[gkvt.buf])
                ost = [R_ffn.alloc("ost%d" % i, [288], F32) for i in range(2)]
                osqs = [R_ffn.alloc("osq%d" % i, [256], F32) for i in range(2)]
                osss = [R_ffn.alloc("oss%d" % i, [2], F32) for i in range(2)]

                def tm_out(j):
                    t = j // 4
                    osq, oss = osqs[j % 2], osss[j % 2]
                    bi, bk = getbank()
                    for kc in range(KC):
                        C.mm(bk[:, 0:288], hT_t[t][:, kc, (j % 4) * 128:(j % 4 + 1) * 128], w1x[:, kc, 384:672], kc == 0, kc == KC - 1)
                    os_ = ost[j % 2]
                    C.act(osq, bk[:, 0:256], AF.Square)
                    C.op('dve', lambda e: e.reduce_sum(out=oss.ap[:, 0:1], in_=osq.ap, axis=AX.X), reads=[osq.buf], writes=[oss.buf])
                    C.act(oss[:, 1:2], oss[:, 0:1], AF.Sqrt, bias=EPS, scale=1.0 / 256)
                    C.recip(oss[:, 1:2], oss[:, 1:2])
                    C.stt(os_[:, 0:256], bk[:, 0:256], oss[:, 1:2], gkvt, ALU.mult, ALU.mult)
                    C.copy(os_[:, 256:288], bk[:, 256:288], eng='act')
                    C.dma('sp', O["o_ckv"][j * 128:(j + 1) * 128, :], os_[:, 0:256], os_.buf, reads=[os_.buf])
                    C.dma('sp', O["o_kpe"][j * 128:(j + 1) * 128, :], os_[:, 256:288], os_.buf, reads=[os_.buf])

            def lat_norm(col0, nch, dim, goff, out_fn, t):
                bs = []
                for c in range(nch):
                    bi, bk = getbank(hold=True); bs.append((bi, bk))
                    for kc in range(KC):
                        C.mm(bk, w1x[:, kc, col0 + c * 128:col0 + (c + 1) * 128], hT_t[t][:, kc, :], kc == 0, kc == KC - 1)
                    C.act(sq3[:, c, :], bk, AF.Square)
                bi2, bk2 = getbank()
                for c in range(nch):
                    C.mm(bk2, ones_b, sq3[:, c, :], c == 0, c == nch - 1)
                C.act(sd3, bk2, AF.Ln, bias=EPS, scale=1.0 / dim)
                C.act(sd3, sd3, AF.Exp, scale=-0.5)
                for c in range(nch):
                    C.stt(out_fn(c), bs[c][1], vcol(goff, c), sd3, ALU.mult, ALU.mult)
                    release(bs[c][0])

            for t in range(NT):
                if t + 1 < NT:
                    nrmF(t + 1)
                if S:
                    lat_norm(384, 2, 256, V_GKV, lambda c, t=t: kvx[:, c, tsl(t)], t)
                else:
                    lat_norm(384, 2, 256, V_GKV, lambda c, t=t: ckvn[:, c, tsl(t)], t)
                bi, bk = getbank()
                for kc in range(KC):
                    C.mm(bk[0:32, :], w1x[:, kc, 640:672], hT_t[t][:, kc, :], kc == 0, kc == KC - 1)
                if S:
                    bi2, bk2 = getbank()
                    for kc in range(KC):
                        C.mm(bk2[0:32, :], w1x[:, kc, 672:704], hT_t[t][:, kc, :], kc == 0, kc == KC - 1)
                    C.tt(rt[0][0:32, :], bk[0:32, :], ropeT[0:32, 0, tsl(t)], ALU.mult)
                    C.tt(rt[1][0:32, :], bk2[0:32, :], ropeT[0:32, 1, tsl(t)], ALU.mult)
                    C.tt(kpx[0:32, tsl(t)], rt[0][0:32, :], rt[1][0:32, :], ALU.add)
                else:
                    C.copy(Kb[64:96, tsl(t)], bk[0:32, :], eng='act')
                    for j in range(4 * t, 4 * t + 4):
                        tm_out(j)
            if not S:
                pass
            else:
                cst = R_s.view(ntm3['sq'][0], [4, 288], F32)
                C.dma('sp', cst[:, :, 0:256], I["ctx_ckv"].rearrange("(j p) f -> p j f", p=128), cst.buf, writes=[cst.buf])
                C.dma('sp', cst[:, :, 256:288], I["ctx_kpe"].rearrange("(j p) f -> p j f", p=128), cst.buf, writes=[cst.buf])
                for c in range(2):
                    bi, bk = getbank()
                    for j in range(4):
                        C.transpose(bk[:, j * 128:(j + 1) * 128], cst[:, j, c * 128:(c + 1) * 128], ident)
                    C.copy(ckvn[:, c, 2 * TS:2 * TS + 512], bk)
                bi, bk = getbank()
                for j in range(4):
                    C.transpose(bk[0:32, j * 128:(j + 1) * 128], cst[:, j, 256:288], ident)
                C.copy(Kb[64:96, 2 * TS:2 * TS + 512], bk[0:32, :], eng='act')
                cin = Buf("cckvin"); cout = Buf("cckvout")
                for c in range(2):
                    C.dma('pool', cc_kv_in[c * 128:(c + 1) * 128, :], kvx[:, c, :], cin, reads=[kvx.buf], writes=[cin])
                C.dma('pool', cc_kv_in[256:288, :], kpx[0:32, :], cin, reads=[kpx.buf], writes=[cin])
                C.collective(cc_kv_in, cc_kv_out, cin, cout)
                kl = Buf("kvload")
                for r in range(2):
                    for c in range(2):
                        C.dma('sp', ckvn[:, c, r * TS:(r + 1) * TS], cc_kv_out[r * 288 + c * 128:r * 288 + (c + 1) * 128, :], kl,
                              reads=[cout], writes=[ckvn.buf])
                    C.dma('sp', Kb[64:96, r * TS:(r + 1) * TS], cc_kv_out[r * 288 + 256:r * 288 + 288, :], kl, reads=[cout], writes=[Kb.buf])
            for t in range(NT):
                lat_norm(0, 3, 384, V_GQ, lambda c, t=t: cqn[:, c, tsl(t)], t)
            C.barrier()

            R_s.p = keep_s
            R_h.reset()
            if S:
                R_ffn.p = 16 * 1024
            else:
                R_ffn.reset()
            wq = R_s.alloc("wq", [3, 2048], BF16)
            wkv = R_s.alloc("wkv", [2, 2048], BF16)
            wo1s = [R_s.alloc("wo1s%d" % i, [D], BF16) for i in range(2)]
            C.dma('pool', wq, I["w_qbx"].rearrange("(kc p) f -> p kc f", p=128), wq.buf, writes=[wq.buf])
            C.dma('pool', wkv, I["w_kvb"].rearrange("(kc p) f -> p kc f", p=128), wkv.buf, writes=[wkv.buf])
            wo1src = I["w_out1"].rearrange("(kc p) f -> p kc f", p=128)
            NKT = NK // 128
            Vbs = [R_h.alloc("Vb%d" % i, [NKT, 65], BF16) for i in range(2)]
            Kbs = [Kb, R_ffn.alloc("Kb2", [NK], BF16)]
            qbs = [R_h.alloc("qb", [Tn], BF16), R_ffn.alloc("qb2", [Tn], BF16)]
            NPT = 5
            PT = [R_h.alloc("PT%d" % i, [512], BF16) for i in range(NPT)]
            opair = [R_h.alloc("opair%d" % i, [Tn], BF16) for i in range(2)]
            rdens = [R_s.alloc("rden", [512], BF16), R_ffn.alloc("rden2", [512], BF16)]
            rdc = [0]; delayed = []; FDELAY = 10 if S else 2
            lnt = None if S else R_h.alloc("lnt", [512], F32)
            opall = None if S else R_h.alloc("opall", [8, Tn], BF16)
            wo1f = None if S else R_s.alloc("wo1f", [8, D], BF16)
            bcs = R_h.alloc("bcs", [512], F32)
            otmp = R_s.alloc("otmp", [512], BF16)
            qr = [R_h.alloc("qr%d" % i, [512], F32) for i in range(2)] if S else None
            for vb in Vbs:
                C.memset(vb[:, :, 64:65], 1.0)
            C.copy(Kbs[1][64:96, :], Kb[64:96, :], eng='act')
            if S:
                jobs = [(0, TS, 0, NK)]
            else:
                jobs = [(s * 256, 256, s * 256, 256) for s in range(4)]
            ptc = [0]
            LOOK = 3

            def v_unit(h, kt):
                bi, bk = getbank()
                for kc in range(2):
                    C.mm(bk[:, 0:64], ckvn[:, kc, kt * 128:(kt + 1) * 128], wkv[:, kc, h * 128 + 64:h * 128 + 128], kc == 0, kc == 1)
                C.copy(Vbs[h % 2][:, kt, 0:64], bk[:, 0:64], eng='dve')

            def k_unit(h, kb0):
                bi, bk = getbank()
                for kc in range(2):
                    C.mm(bk[0:64, :], wkv[:, kc, h * 128:h * 128 + 64], ckvn[:, kc, kb0:kb0 + 512], kc == 0, kc == 1)
                C.copy(Kbs[h % 2][0:64, kb0:kb0 + 512], bk[0:64, :])

            def q_unit(h, t):
                qb = qbs[h % 2]
                bi, bk = getbank()
                if S:
                    for kc in range(3):
                        C.mm(bk, wq[:, kc, h * 128:(h + 1) * 128], cqn[:, kc, tsl(t)], kc == 0, kc == 2)
                    C.copy(qb[0:64, tsl(t)], bk[0:64, :], eng='dve')
                    C.tt(qr[0][64:96, :], bk[64:96, :], ropeT[64:96, 0, tsl(t)], ALU.mult)
                    C.tt(qr[1][64:96, :], bk[96:128, :], ropeT[96:128, 1, tsl(t)], ALU.mult)
                    C.tt(qb[64:96, tsl(t)], qr[0][64:96, :], qr[1][64:96, :], ALU.add)
                else:
                    for kc in range(3):
                        C.mm(bk[0:96, :], wq[:, kc, h * 128:h * 128 + 96], cqn[:, kc, tsl(t)], kc == 0, kc == 2)
                    C.copy(qb[0:96, tsl(t)], bk[0:96, :], eng='dve')

            def o_unit(pr, t, oc):
                wo1 = wo1s[pr % 2]
                if t == 0 and oc == 0:
                    C.dma('pool', wo1, wo1src[:, pr, :], wo1.buf, writes=[wo1.buf])
                bi, bk = getbank()
                C.mm(bk, wo1[:, oc * 128:(oc + 1) * 128], opair[pr % 2][:, tsl(t)], True, True)
                C.stt(xT_t[t][:, oc, :], bk, MOD(1, cj, 'G_mix', oc), xT_t[t][:, oc, :], ALU.mult, ALU.add)

            def prep_units(h):
                us = [(lambda kb0=kb0: k_unit(h, kb0)) for kb0 in range(0, NK, 512)]
                us += [(lambda t=t: q_unit(h, t)) for t in range(NT)]
                vs = [(lambda kt=kt: v_unit(h, kt)) for kt in range(NKT)]
                out = []
                step = max(1, len(vs) // max(1, len(us)))
                vi = 0
                for u in us:
                    out.append(u)
                    out.extend(vs[vi:vi + step]); vi += step
                out.extend(vs[vi:])
                return out

            side = []

            def attend(h):
                Kh, qb, Vh = Kbs[h % 2], qbs[h % 2], Vbs[h % 2]
                op_ = opair[(h // 2) % 2] if S else opall[:, h // 2, :]
                its = []
                for (q0, nqt, k0, nk) in jobs:
                    for qq in range(q0, q0 + nqt, 512):
                        nq = min(512, q0 + nqt - qq)
                        for ki in range(nk // 128):
                            its.append((qq, nq, k0 + ki * 128, ki, nk // 128))
                cur = {}
                pend = []

                def fin2(qq, nq, oi, ob, rd, last):
                    bi, bk = getbank()
                    C.mm(bk[0:64, 0:nq], ones_b[64:65, 0:64], rd[64:65, 0:nq], True, True)
                    C.copy(bcs[0:64, 0:nq], bk[0:64, 0:nq], eng='dve')
                    C.tt(op_[(h % 2) * 64:(h % 2) * 64 + 64, qq:qq + nq], ob[0:64, 0:nq], bcs[0:64, 0:nq], ALU.mult)
                    release(oi)
                    if last and h % 2 == 1 and S:
                        pr = h // 2
                        side[0:0] = [(lambda t=t, oc=oc, pr=pr: o_unit(pr, t, oc)) for t in range(NT) for oc in range(8)]

                def flush_one():
                    pt, (qq, nq, ks, ki, nkt) = pend.pop(0)
                    oi, ob = cur[qq]
                    C.mm(ob[0:65, 0:nq], Vh[:, ks // 128, 0:65], pt[:, 0:nq], ki == 0, ki == nkt - 1, signal=True)
                    if ki == nkt - 1:
                        rd = rdens[rdc[0] % 2]; rdc[0] += 1
                        if S:
                            def rq(k, rd=rd, ob=ob):
                                with nc.allow_low_precision("softmax 1/den is a bf16 matmul operand"):
                                    C.recip(rd[64:65, k * 128:(k + 1) * 128], ob[64:65, k * 128:(k + 1) * 128])
                            rq(0)
                            for k in range(1, 4):
                                delayed.append([k, (lambda k=k, rq=rq: rq(k))])
                        else:
                            C.act(rd[64:65, 0:nq], ob[64:65, 0:nq], AF.Ln)
                            C.act(rd[64:65, 0:nq], rd[64:65, 0:nq], AF.Exp, scale=-1.0)
                        last = (pend == [] and ii_box[0] == n_it - 1)
                        delayed.append([FDELAY, (lambda qq=qq, nq=nq, oi=oi, ob=ob, rd=rd, last=last: fin2(qq, nq, oi, ob, rd, last))])

                def tick():
                    for dl in delayed:
                        dl[0] -= 1
                    ready = [dl for dl in delayed if dl[0] <= 0]
                    for dl in ready:
                        delayed.remove(dl)
                        dl[1]()

                n_it = len(its)
                ii_box = [0]
                sacc = [0.0]

                def pop_side(frac_left):
                    if side:
                        sacc[0] += len(side) / float(max(1, frac_left))
                        while sacc[0] >= 1.0 and side:
                            sacc[0] -= 1.0
                            side.pop(0)()

                if not S:
                    steps = 6
                    stp = 0
                    pts_all = []
                    for sp in range(2):
                        pts = []
                        for ki in range(2):
                            bi, bk = getbank()
                            for jb in range(2):
                                sq_ = 2 * sp + jb
                                ks = sq_ * 256 + ki * 128
                                C.mm(bk[:, jb * 256:(jb + 1) * 256], Kh[0:96, ks:ks + 128], qb[0:96, sq_ * 256:(sq_ + 1) * 256], True, True)
                            pt = PT[ptc[0] % NPT]; ptc[0] += 1
                            C.act(pt, bk, AF.Exp, scale=SCALE)
                            pts.append(pt)
                            tick(); pop_side(steps - stp); stp += 1
                        pts_all.append(pts)
                    for sp in range(2):
                        oi, ob = getbank(hold=True)
                        for jb in range(2):
                            sq_ = 2 * sp + jb
                            for ki in range(2):
                                ks = sq_ * 256 + ki * 128
                                C.mm(ob[0:65, jb * 256:(jb + 1) * 256], Vh[:, ks // 128, 0:65], pts_all[sp][ki][:, jb * 256:(jb + 1) * 256],
                                     ki == 0, ki == 1, signal=True)
                        rd = rdens[rdc[0] % 2]; rdc[0] += 1
                        C.act(lnt[64:65, :], ob[64:65, :], AF.Ln)
                        with nc.allow_low_precision("softmax 1/den is a bf16 matmul operand"):
                            C.act(rd[64:65, :], lnt[64:65, :], AF.Exp, scale=-1.0)
                        delayed.append([FDELAY, (lambda qq=sp * 512, oi=oi, ob=ob, rd=rd, last=(sp == 1): fin2(qq, 512, oi, ob, rd, last))])
                        tick(); pop_side(steps - stp); stp += 1
                    while side:
                        side.pop(0)()
                    return
                for ii, it in enumerate(its):
                    ii_box[0] = ii
                    qq, nq, ks, ki, nkt = it
                    if ki == 0:
                        cur[qq] = getbank(hold=True)
                    bi, bk = getbank()
                    C.mm(bk[:, 0:nq], Kh[0:96, ks:ks + 128], qb[0:96, qq:qq + nq], True, True)
                    pt = PT[ptc[0] % NPT]; ptc[0] += 1
                    C.act(pt[:, 0:nq], bk[:, 0:nq], AF.Exp, scale=SCALE)
                    pend.append((pt, it))
                    if h == 0:
                        for u_ in jit0.pop(ii, []):
                            u_()
                    if len(pend) > LOOK:
                        flush_one()
                    tick()
                    if side:
                        sacc[0] += len(side) / float(n_it - ii)
                        while sacc[0] >= 1.0 and side:
                            sacc[0] -= 1.0
                            side.pop(0)()
                while pend:
                    flush_one()
                while side:
                    side.pop(0)()

            jit0 = {}
            if S:
                for kb0 in range(0, NK, 512):
                    k_unit(0, kb0)
                q_unit(0, 0)
                for kt in range(NKT):
                    jit0.setdefault(kt, []).append(lambda kt=kt: v_unit(0, kt))
                for t in range(1, NT):
                    jit0.setdefault(t, []).append(lambda t=t: q_unit(0, t))
            else:
                for u in prep_units(0):
                    u()
            for h in range(16):
                if h + 1 < 16:
                    side.extend(prep_units(h + 1))
                attend(h)
            while delayed:
                delayed.pop(0)[1]()
            while side:
                side.pop(0)()
            if not S:
                for hlf in range(2):
                    C.dma('pool', wo1f[:, hlf * 4:hlf * 4 + 4, :], wo1src[:, hlf * 4:hlf * 4 + 4, :], wo1f.buf, writes=[wo1f.buf])
                for t in range(NT):
                    outproj_acc(lambda kc, oc: wo1f[:, kc, oc * 128:(oc + 1) * 128], lambda kc, t=t: opall[:, kc, tsl(t)], 8, t, 'G_mix', 1)
            C.barrier()

            R_h.reset()
            hT2 = R_h.alloc("hT", [8, TH], BF16)
            for t in range(NT):
                hT_t[t] = T(hT2.ap[:, :, t * 512:(t + 1) * 512], Buf("hTb%d" % t))
            ffn(1)

            R_s.reset()
            ntm4 = norm_tmps(R_s, 2)
            yf = [R_s.alloc("yf%d" % i, [8, 512], F32) for i in range(2)]
            ytm = [R_s.alloc("ytm%d" % i, [D], F32) for i in range(2)]

            def fin_store(t):
                y_ = yf[t % 2]
                norm_mod(xT_t[t], 8, 512, D, lambda c: vcol(V_GFIN, c), None, lambda c, y_=y_: y_[:, c, :], ntm4, t, stats=False)
                for jj in range(4):
                    j = t * 4 + jj
                    yt = ytm[j % 2]
                    for half in range(2):
                        bi, bk = getbank()
                        for q in range(4):
                            c = half * 4 + q
                            C.transpose(bk[:, q * 128:(q + 1) * 128], y_[:, c, jj * 128:(jj + 1) * 128], ident)
                        C.copy(yt[:, half * 512:(half + 1) * 512], bk, eng='act' if j % 2 == 0 else 'dve')
                    C.dma('sp', yd[j * 128:(j + 1) * 128, :], yt, yt.buf, reads=[yt.buf])

            pipeline(NT, lambda t: norm_stats(xT_t[t], 8, 512, D, ntm4, t), fin_store)
            C.barrier()

        run_wave('P')
        C.recycle()
        run_wave('S')
        C.finish()
    return nc


def _host_consts():
    inv = (10000.0 ** (-np.arange(0, 16, 2, dtype=np.float32) / np.float32(16))).astype(np.float32)
    return inv


def kernel(x_prompt, x_sample, state_l0_lru, cache_l1_ckv, cache_l1_kpe, c, c_ctx,
           l0_w_mod, l0_b_mod, l0_g_mix, l0_g_ffn, l0_w_in, l0_conv_w, l0_conv_b,
           l0_lru_w_a, l0_lru_b_a, l0_lru_w_i, l0_lru_b_i, l0_lru_lam, l0_pool_w, l0_pool_scale,
           l0_w_out, l0_ffn_w1, l0_ffn_w2,
           l1_w_mod, l1_b_mod, l1_g_mix, l1_g_ffn, l1_w_in, l1_g_q, l1_w_qb, l1_g_kv, l1_w_kvb,
           l1_w_out, l1_ffn_w1, l1_ffn_w2, g_final, _debug=False):
    f = lambda a: np.ascontiguousarray(np.asarray(a, dtype=np.float32))
    x_prompt, x_sample = f(x_prompt), f(x_sample)

    def cols(v, n):
        return f(v).reshape(n, 128).T

    vecs = np.zeros((128, NV), np.float32)
    vecs[:, V_GMIX0:V_GMIX0 + 8] = cols(l0_g_mix, 8); vecs[:, V_GFFN0:V_GFFN0 + 8] = cols(l0_g_ffn, 8)
    vecs[:, V_GMIX1:V_GMIX1 + 8] = cols(l1_g_mix, 8); vecs[:, V_GFFN1:V_GFFN1 + 8] = cols(l1_g_ffn, 8)
    vecs[:, V_GFIN:V_GFIN + 8] = cols(g_final, 8)
    cw = f(l0_conv_w)
    for ch in range(4):
        for k in range(4):
            vecs[:, V_CONVW + ch * 4 + k] = cw[k, ch * 128:(ch + 1) * 128]
    vecs[:, V_CONVB:V_CONVB + 4] = cols(l0_conv_b, 4)
    for d in range(2):
        vecs[:, V_BA + d * 4:V_BA + d * 4 + 4] = cols(f(l0_lru_b_a)[d], 4)
        vecs[:, V_BI + d * 4:V_BI + d * 4 + 4] = cols(f(l0_lru_b_i)[d], 4)
        vecs[:, V_LAM + d * 4:V_LAM + d * 4 + 4] = cols(f(l0_lru_lam)[d], 4)
    vecs[:, V_SPOOL:V_SPOOL + 4] = cols(l0_pool_scale, 4)
    vecs[:, V_GQ:V_GQ + 3] = cols(l1_g_q, 3); vecs[:, V_GKV:V_GKV + 2] = cols(l1_g_kv, 2)
    bmods = (np.ascontiguousarray(cols(l0_b_mod, 48)), np.ascontiguousarray(cols(l1_b_mod, 48)))
    wmods = (f(l0_w_mod), f(l1_w_mod))
    wbd = np.zeros((128, 16, 128), np.float32)
    wa, wi = f(l0_lru_w_a), f(l0_lru_w_i)
    for d in range(2):
        for gi, wsrc in enumerate((wa, wi)):
            for ch in range(4):
                for j in range(2):
                    wbd[j * 64:(j + 1) * 64, (d * 2 + gi) * 4 + ch, j * 64:(j + 1) * 64] = wsrc[d, 2 * ch + j]
    pool_w = np.ascontiguousarray(f(l0_pool_w).transpose(1, 0, 2))
    perm = np.concatenate([np.arange(8, 16), np.arange(0, 8), np.arange(24, 32), np.arange(16, 24)])
    w_in1 = f(l1_w_in)
    w_in1x = np.ascontiguousarray(np.concatenate([w_in1, w_in1[:, 640 + perm]], axis=1))
    w_qb = f(l1_w_qb)
    qcols = np.concatenate([np.concatenate([np.arange(h * 96, (h + 1) * 96), h * 96 + 64 + perm]) for h in range(16)])
    w_qbx = np.ascontiguousarray(w_qb[:, qcols])
    shared = {"vecs": vecs,
              "w_in0": f(l0_w_in), "wbd": wbd, "pool_w": pool_w, "w_out0": f(l0_w_out),
              "ffn_w1_0": f(l0_ffn_w1), "ffn_w2_0": f(l0_ffn_w2), "w_in1x": w_in1x, "w_qbx": w_qbx,
              "w_kvb": f(l1_w_kvb), "w_out1": f(l1_w_out), "ffn_w1_1": f(l1_ffn_w1), "ffn_w2_1": f(l1_ffn_w2),
              "gkv_row": f(l1_g_kv), "ident": np.eye(128, dtype=np.float32)}
    inv = _host_consts()

    def invc(S_, start, n):
        tpos = np.arange(start, start + n)
        out = np.zeros((4, n), np.float32)
        for gi, w in enumerate((2, 4, 8, 16)):
            left = w // 2; right = w - 1 - left
            lo = np.maximum(tpos - left, 0); hi = np.minimum(tpos + right, S_ - 1) + 1
            out[gi] = (1.0 / (hi - lo).astype(np.float32)).astype(np.float32)
        return out

    invc_p = np.concatenate([invc(256, 0, 256)] * 4, axis=1)
    cs, cc_ = f(c), f(c_ctx)
    st = f(state_l0_lru); ckv_c = f(cache_l1_ckv); kpe_c = f(cache_l1_kpe)
    in_maps = []
    for core in range(8):
        sb, half = core // 2, core % 2
        start = half * TS
        xw = np.zeros((TS + 16, D), np.float32)
        lo, hi = start - 8, start + TS + 8
        slo, shi = max(lo, 0), min(hi, 4096)
        xw[slo - lo:shi - lo] = x_sample[sb, slo:shi]
        hm = np.zeros(16, np.float32)
        hm[0:8] = 1.0 if half == 1 else 0.0
        hm[8:16] = 1.0 if half == 0 else 0.0
        pos = np.arange(start, start + TS)
        row = (pos // 64).astype(np.float32); col = (pos % 64).astype(np.float32)
        rope = np.zeros((32, 2, TS), np.float32)
        for r in range(32):
            grp, j = r // 8, r % 8
            ang = ((row if grp < 2 else col) * inv[j]).astype(np.float32)
            rope[r, 0] = np.cos(ang); rope[r, 1] = np.sin(ang) * (-1.0 if grp % 2 == 0 else 1.0)
        cT = np.stack([cols(cc_, 8), cols(cs[sb], 8)], axis=2)
        st0 = np.stack([cols(st[sb, 0], 4), cols(st[sb, 1], 4)], axis=1)
        m = dict(shared)
        m.update({"x_p": np.ascontiguousarray(x_prompt[core * 4:(core + 1) * 4].reshape(TP, D)), "x_s": xw,
                  "cT": np.ascontiguousarray(cT), "st0": np.ascontiguousarray(st0),
                  "wmod_h": wmods[core % 2], "bmod_h": bmods[core % 2],
                  "ctx_ckv": np.ascontiguousarray(ckv_c[sb]), "ctx_kpe": np.ascontiguousarray(kpe_c[sb]),
                  "rope": rope, "invc_p": np.ascontiguousarray(invc_p), "invc_s": invc(4096, start, TS), "hmask": hm,
                  "sel": np.array([1.0, 0.0] if half == 0 else [0.0, 1.0], np.float32)})
        in_maps.append(m)
    nc = build_program()
    res = run_bass_kernel_spmd(nc, in_maps, core_ids=list(range(8)))
    R = res.results
    y_prompt = np.stack([R[k]["y_p"] for k in range(8)]).reshape(32, 256, D)
    y_sample = np.stack([R[k]["y_s"] for k in range(8)]).reshape(4, 4096, D)
    new_lru = np.stack([R[k]["o_lru"].reshape(4, 2, 512) for k in range(8)]).reshape(32, 2, 512)
    new_ckv = np.stack([R[k]["o_ckv"] for k in range(8)]).reshape(32, 256, 256)
    new_kpe = np.stack([R[k]["o_kpe"] for k in range(8)]).reshape(32, 256, 32)
    if _debug:
        return (y_prompt, y_sample, new_lru, new_ckv, new_kpe), R
    return (y_prompt.astype(np.float32), y_sample.astype(np.float32), new_lru.astype(np.float32),
            new_ckv.astype(np.float32), new_kpe.astype(np.float32))
```
